# Optimizing a Trainium2 kernel written in Bass

```python
import math
import jax, jax.numpy as jnp
from jax import lax
import numpy as np

D_MODEL = 1024
BATCH = 1
SEQ = 16384
DEPTH = 4
DEC_BATCH = 16
DEC_SEQ = 2048
PAST_LEN = 128

EPS = 1e-6
NEG_INF = -1e30
N_MOD = 6
N_BRANCH = 4
A_GROUPS = 4
A_GROUP_W = 192
A_W = A_GROUPS * A_GROUP_W
B_PAIRS = ((128, 1), (512, 4), (2048, 16))
B_N_GROUPS = 3
B_HEADS_PER_GROUP = 2
B_HEADS = B_N_GROUPS * B_HEADS_PER_GROUP
B_HEAD_DIM = 64
B_QKV_W = B_HEADS * B_HEAD_DIM
B_OUT_W = B_HEADS_PER_GROUP * B_HEAD_DIM
REL_BUCKETS = 32
REL_MAX_DIST = 1024
C_HEADS = 4
C_HEAD_W = 96
C_W = C_HEADS * C_HEAD_W
C_CHUNK = 128
D_HEADS = 4
D_Q_LORA = 384
D_KV_LORA = 320
D_NOPE = 64
D_ROPE = 32
D_V = 64
D_OUT_W = D_HEADS * D_V
D_QBLK = 128
ROPE_THETA = 10000.0
IN_SPLITS = (A_W, B_QKV_W, B_QKV_W, B_QKV_W, C_W, C_W, D_Q_LORA, D_KV_LORA, D_ROPE, N_BRANCH * D_MODEL)
MIX_W = A_W + 3 * B_QKV_W + 2 * C_W + D_Q_LORA + D_KV_LORA + D_ROPE
IN_W = MIX_W + N_BRANCH * D_MODEL
D_FF = 4 * D_MODEL

kernel_name = "hybrid_bidir_encoder_fnet_longnet_gmlp_mla"


def _rmsnorm(x, g):
    xf = x.astype(jnp.float32)
    y = xf * lax.rsqrt(jnp.mean(xf * xf, axis=-1, keepdims=True) + EPS)
    return (y * g.astype(jnp.float32)).astype(x.dtype)


def _layernorm(x, g, b):
    xf = x.astype(jnp.float32)
    mu = jnp.mean(xf, axis=-1, keepdims=True)
    xc = xf - mu
    y = xc * lax.rsqrt(jnp.mean(xc * xc, axis=-1, keepdims=True) + EPS)
    return (y * g.astype(jnp.float32) + b.astype(jnp.float32)).astype(x.dtype)


def _split_points():
    return np.cumsum(np.array(IN_SPLITS))[:-1].tolist()


def _t5_bucket(rel):
    half = REL_BUCKETS // 2
    max_exact = half // 2
    ret = jnp.where(rel > 0, half, 0)
    n = jnp.abs(rel)
    large = max_exact + (jnp.log(jnp.maximum(n, max_exact).astype(jnp.float32) / max_exact)
                         / math.log(REL_MAX_DIST / max_exact) * (half - max_exact)).astype(jnp.int32)
    large = jnp.minimum(large, half - 1)
    return ret + jnp.where(n < max_exact, n, large)


def _rope(x, pos):
    half = D_ROPE // 2
    inv = ROPE_THETA ** (-jnp.arange(half, dtype=jnp.float32) / half)
    ang = pos[:, None].astype(jnp.float32) * inv[None, :]
    cos = jnp.cos(ang)[:, None, :]
    sin = jnp.sin(ang)[:, None, :]
    xf = x.astype(jnp.float32)
    x1, x2 = xf[..., :half], xf[..., half:]
    return jnp.concatenate([x1 * cos - x2 * sin, x1 * sin + x2 * cos], axis=-1).astype(x.dtype)


def _fourier_mix(a):
    B, S, _ = a.shape
    ag = a.reshape(B, S, A_GROUPS, A_GROUP_W).astype(jnp.float32)
    f = jnp.fft.fft2(ag, axes=(1, 3), norm="ortho").real
    return f.reshape(B, S, A_W).astype(a.dtype)


def _dilated_group(q, k, v, bias_table, dil, half):
    B, S, H, hd = q.shape
    L = S // dil
    N = B * dil
    blk = half
    nb = -(-L // blk)
    Lp = nb * blk

    def to_sub(t):
        return t.reshape(B, L, dil, H, hd).transpose(0, 2, 1, 3, 4).reshape(N, L, H, hd)

    def windows(t):
        tp = jnp.pad(to_sub(t), ((0, 0), (blk, Lp - L + blk), (0, 0), (0, 0))).reshape(N, nb + 2, blk, H, hd)
        return jnp.concatenate([tp[:, :-2], tp[:, 1:-1], tp[:, 2:]], axis=2)

    qs = jnp.pad(to_sub(q), ((0, 0), (0, Lp - L), (0, 0), (0, 0))).reshape(N, nb, blk, H, hd)
    kw = windows(k)
    vw = windows(v)
    qi = jnp.arange(blk)
    ki = jnp.arange(3 * blk)
    rel = ki[None, :] - blk - qi[:, None]
    band = jnp.abs(rel) <= half
    kpos = jnp.arange(nb)[:, None] * blk - blk + ki[None, :]
    kvalid = (kpos >= 0) & (kpos < L)
    mask = band[None, :, :] & kvalid[:, None, :]
    bias = jnp.transpose(bias_table[_t5_bucket(rel * dil)], (2, 0, 1)).astype(jnp.float32)
    s = jnp.einsum('nbqhd,nbkhd->nbhqk', qs, kw, preferred_element_type=jnp.float32) * (hd ** -0.5) + bias
    s = jnp.where(mask[None, :, None], s, NEG_INF)
    lse = jax.nn.logsumexp(s, axis=-1)
    p = jnp.exp(s - lse[..., None])
    o = jnp.einsum('nbhqk,nbkhd->nbqhd', p.astype(v.dtype), vw)
    o = o.reshape(N, Lp, H, hd)[:, :L]
    lse = lse.transpose(0, 1, 3, 2).reshape(N, Lp, H)[:, :L]
    o = o.reshape(B, dil, L, H, hd).transpose(0, 2, 1, 3, 4).reshape(B, S, H, hd)
    lse = lse.reshape(B, dil, L, H).transpose(0, 2, 1, 3).reshape(B, S, H)
    return o, lse


def _dilated_attention(q, k, v, rel_bias):
    B, S, _ = q.shape
    shp = (B, S, B_N_GROUPS, B_HEADS_PER_GROUP, B_HEAD_DIM)
    q, k, v = q.reshape(shp), k.reshape(shp), v.reshape(shp)
    outs, lses = [], []
    for g, (window, dil) in enumerate(B_PAIRS):
        hs = slice(g * B_HEADS_PER_GROUP, (g + 1) * B_HEADS_PER_GROUP)
        o, l = _dilated_group(q[:, :, g], k[:, :, g], v[:, :, g], rel_bias[:, hs], dil, window // (2 * dil))
        outs.append(o)
        lses.append(l)
    o = jnp.stack(outs).astype(jnp.float32)
    alpha = jax.nn.softmax(jnp.stack(lses), axis=0)
    out = jnp.sum(alpha[..., None] * o, axis=0)
    return out.reshape(B, S, B_OUT_W).astype(q.dtype)


def _spatial_gating(u, v, ln_g, ln_b, w_s, b_s):
    B, S, _ = u.shape
    vn = _layernorm(v, ln_g, ln_b)
    vc = vn.reshape(B, S // C_CHUNK, C_CHUNK, C_HEADS, C_HEAD_W)
    mixed = jnp.einsum('hpq,bnqhc->bnphc', w_s, vc) + b_s.T[:, :, None]
    return u * mixed.reshape(B, S, C_W).astype(u.dtype)


def _mla(cq, ckv, kr, q_norm_g, kv_norm_g, w_uq, w_ukv):
    B, S, _ = cq.shape
    pos = jnp.arange(S)
    q = (_rmsnorm(cq, q_norm_g) @ w_uq).reshape(B, S, D_HEADS, D_NOPE + D_ROPE)
    kv = (_rmsnorm(ckv, kv_norm_g) @ w_ukv).reshape(B, S, D_HEADS, D_NOPE + D_V)
    q_nope = q[..., :D_NOPE]
    q_pe = _rope(q[..., D_NOPE:], pos)
    k_nope = kv[..., :D_NOPE]
    v = kv[..., D_NOPE:]
    k_pe = _rope(kr[:, :, None, :], pos)[:, :, 0]
    scale = (D_NOPE + D_ROPE) ** -0.5
    nqb = S // D_QBLK

    def blocks(t):
        return t.reshape(B, nqb, D_QBLK, t.shape[2], t.shape[3]).swapaxes(0, 1)

    def attend(qb):
        qn, qp = qb
        s = (jnp.einsum('bqhd,bkhd->bhqk', qn, k_nope, preferred_element_type=jnp.float32)
             + jnp.einsum('bqhr,bkr->bhqk', qp, k_pe, preferred_element_type=jnp.float32)) * scale
        p = jax.nn.softmax(s, axis=-1)
        return jnp.einsum('bhqk,bkhd->bqhd', p.astype(v.dtype), v)

    o = lax.map(attend, (blocks(q_nope), blocks(q_pe)))
    return o.swapaxes(0, 1).reshape(B, S, D_OUT_W)


def _layer(x, c, rel_bias, ada_w, ada_b, norm1_g, w_in, q_norm_g, kv_norm_g, w_uq, w_ukv,
           sgu_ln_g, sgu_ln_b, sgu_w, sgu_b, p_a, p_b, p_c, p_d, w_o, norm2_g, w1, w2):
    B, S, D = x.shape
    mod = (jax.nn.silu(c) @ ada_w + ada_b)[:, None, :]
    sh1, sc1, gt1, sh2, sc2, gt2 = jnp.split(mod, N_MOD, axis=-1)
    h = _rmsnorm(x, norm1_g) * (1 + sc1) + sh1
    z = h @ w_in
    za, bq, bk, bv, cu, cv, dcq, dckv, dkr, zg = jnp.split(z, _split_points(), axis=-1)
    ya = _fourier_mix(za) @ p_a
    yb = _dilated_attention(bq, bk, bv, rel_bias) @ p_b
    yc = _spatial_gating(cu, cv, sgu_ln_g, sgu_ln_b, sgu_w, sgu_b) @ p_c
    yd = _mla(dcq, dckv, dkr, q_norm_g, kv_norm_g, w_uq, w_ukv) @ p_d
    g = jax.nn.sigmoid(zg).reshape(B, S, N_BRANCH, D)
    merged = g[:, :, 0] * ya + g[:, :, 1] * yb + g[:, :, 2] * yc + g[:, :, 3] * yd
    x = x + gt1 * (merged @ w_o)
    h = _rmsnorm(x, norm2_g) * (1 + sc2) + sh2
    x = x + gt2 * (jnp.square(jax.nn.relu(h @ w1)) @ w2)
    return x


def setup_inputs(seed: int = 0) -> dict:
    key = jax.random.key(seed)
    ks = jax.random.split(key, 32)
    f32 = jnp.float32

    def nrm(k, shape, fan_in, scale=1.0):
        return jax.random.normal(k, shape, f32) * (scale * fan_in ** -0.5)

    def gain(k, shape):
        return 1.0 + 0.05 * jax.random.normal(k, shape, f32)

    D = D_MODEL
    return {
        "x_prompt": jax.random.normal(ks[0], (BATCH, SEQ, D), f32),
        "x_sample": jax.random.normal(ks[1], (DEC_BATCH, DEC_SEQ, D), f32),
        "c_prompt": jax.random.normal(ks[2], (BATCH, D), f32),
        "c_sample": jax.random.normal(ks[3], (DEC_BATCH, D), f32),
        "rel_bias": 0.1 * jax.random.normal(ks[4], (REL_BUCKETS, B_HEADS), f32),
        "ada_w": nrm(ks[5], (DEPTH, D, N_MOD * D), D, 0.5),
        "ada_b": 0.02 * jax.random.normal(ks[6], (DEPTH, N_MOD * D), f32),
        "norm1_g": gain(ks[7], (DEPTH, D)),
        "w_in": nrm(ks[8], (DEPTH, D, IN_W), D),
        "mla_q_norm_g": gain(ks[9], (DEPTH, D_Q_LORA)),
        "mla_kv_norm_g": gain(ks[10], (DEPTH, D_KV_LORA)),
        "mla_w_uq": nrm(ks[11], (DEPTH, D_Q_LORA, D_HEADS * (D_NOPE + D_ROPE)), D_Q_LORA),
        "mla_w_ukv": nrm(ks[12], (DEPTH, D_KV_LORA, D_HEADS * (D_NOPE + D_V)), D_KV_LORA),
        "sgu_ln_g": gain(ks[13], (DEPTH, C_W)),
        "sgu_ln_b": 0.02 * jax.random.normal(ks[14], (DEPTH, C_W), f32),
        "sgu_w": nrm(ks[15], (DEPTH, C_HEADS, C_CHUNK, C_CHUNK), C_CHUNK),
        "sgu_b": gain(ks[16], (DEPTH, C_HEADS, C_CHUNK)),
        "p_a": nrm(ks[17], (DEPTH, A_W, D), A_W),
        "p_b": nrm(ks[18], (DEPTH, B_OUT_W, D), B_OUT_W),
        "p_c": nrm(ks[19], (DEPTH, C_W, D), C_W),
        "p_d": nrm(ks[20], (DEPTH, D_OUT_W, D), D_OUT_W),
        "w_o": nrm(ks[21], (DEPTH, D, D), D),
        "norm2_g": gain(ks[22], (DEPTH, D)),
        "mlp_w1": nrm(ks[23], (DEPTH, D, D_FF), D),
        "mlp_w2": nrm(ks[24], (DEPTH, D_FF, D), D_FF),
        "final_g": gain(ks[25], (D,)),
    }


def reference(x_prompt, x_sample, c_prompt, c_sample, rel_bias, ada_w, ada_b, norm1_g, w_in,
              mla_q_norm_g, mla_kv_norm_g, mla_w_uq, mla_w_ukv, sgu_ln_g, sgu_ln_b, sgu_w, sgu_b,
              p_a, p_b, p_c, p_d, w_o, norm2_g, mlp_w1, mlp_w2, final_g):
    def run(x, c):
        for l in range(DEPTH):
            x = _layer(x, c, rel_bias, ada_w[l], ada_b[l], norm1_g[l], w_in[l],
                       mla_q_norm_g[l], mla_kv_norm_g[l], mla_w_uq[l], mla_w_ukv[l],
                       sgu_ln_g[l], sgu_ln_b[l], sgu_w[l], sgu_b[l],
                       p_a[l], p_b[l], p_c[l], p_d[l], w_o[l], norm2_g[l], mlp_w1[l], mlp_w2[l])
        return _rmsnorm(x, final_g)

    y_prompt = run(x_prompt, c_prompt)
    y_sample = run(x_sample, c_sample)
    return (y_prompt, y_sample)
```

```python
import math
from contextlib import ExitStack
import numpy as np
import ml_dtypes
import concourse.bass as bass
import concourse.mybir as mybir
from concourse.bass_utils import run_bass_kernel_spmd

F32 = mybir.dt.float32
BF16 = mybir.dt.bfloat16
AF = mybir.ActivationFunctionType
ALU = mybir.AluOpType
AX = mybir.AxisListType
NPBF = ml_dtypes.bfloat16

NCORES = 8
D = 1024
KD = 8
T = 2048
NTT = 4
DEPTH = 4
EPS = 1e-6
NEG = -30000.0
FLAGS = {"depth": 4, "fnet": True, "dil": True, "sgu": True, "mla": True, "jobs": (0, 1, 2)}

C_ZA, C_BV, C_CV, C_BQ, C_BK, C_CU, C_CQ, C_CKV, C_KR, C_ZG, C_END = 0, 768, 1152, 1536, 1920, 2304, 2688, 3072, 3392, 3456, 7552
R_ZA, R_KD, R_VD, R_KM, R_VM, R_TOT = 0, 768, 768 + 390, 768 + 780, 768 + 780 + 384, 768 + 780 + 384 + 260


class Res:
    __slots__ = ("w", "r")

    def __init__(self):
        self.w = None
        self.r = {}


class Prog:
    NDMA = 56

    def __init__(self, nc, stack):
        self.nc = nc
        self.eng = {"pe": nc.tensor, "act": nc.scalar, "dve": nc.vector, "pool": nc.gpsimd, "sp": nc.sync}
        self.sem = {k: stack.enter_context(nc.semaphore("sem_" + k)) for k in self.eng}
        self.dsem = [stack.enter_context(nc.semaphore("dsem%d" % i)) for i in range(self.NDMA)]
        self.cnt = {k: 0 for k in self.eng}
        self.dval = [0] * self.NDMA
        self.drr = 0
        self.waited = {k: {} for k in self.eng}
        self._res = {}

    def res(self, key):
        r = self._res.get(key)
        if r is None:
            r = Res()
            self._res[key] = r
        return r

    def _semobj(self, key):
        return self.dsem[key[1]] if isinstance(key, tuple) else self.sem[key]

    def _deps(self, rd, wr):
        deps = {}
        for r in rd:
            r = self.res(r)
            if r.w is not None and deps.get(r.w[0], 0) < r.w[1]:
                deps[r.w[0]] = r.w[1]
        for r in wr:
            r = self.res(r)
            if r.w is not None and deps.get(r.w[0], 0) < r.w[1]:
                deps[r.w[0]] = r.w[1]
            for k, v in r.r.items():
                if deps.get(k, 0) < v:
                    deps[k] = v
        return deps

    def _wait(self, e, deps):
        w = self.waited[e]
        for k, v in deps.items():
            if e == "pe" and k == "pe":
                continue
            if w.get(k, 0) >= v:
                continue
            self.eng[e].wait_ge(self._semobj(k), v)
            w[k] = v

    def _mark(self, rd, wr, key, val):
        for r in rd:
            r = self.res(r)
            if r.r.get(key, 0) < val:
                r.r[key] = val
        for r in wr:
            r = self.res(r)
            r.w = (key, val)
            r.r = {}

    def op(self, e, fn, rd=(), wr=()):
        self._wait(e, self._deps(rd, wr))
        ins = fn(self.eng[e])
        self.cnt[e] += 1
        ins.then_inc(self.sem[e], 1)
        self._mark(rd, wr, e, self.cnt[e])
        return ins

    def _slot(self, q, rd, wr):
        i = self.drr
        self.drr = (self.drr + 1) % self.NDMA
        deps = self._deps(rd, wr)
        if self.dval[i] > 0:
            deps[("d", i)] = max(deps.get(("d", i), 0), self.dval[i])
        self._wait(q, deps)
        return i

    def dma(self, q, out, in_, rd=(), wr=(), **kw):
        i = self._slot(q, rd, wr)
        ins = self.eng[q].dma_start(out=out, in_=in_, **kw)
        self.dval[i] += 16
        ins.then_inc(self.dsem[i], 16)
        self._mark(rd, wr, ("d", i), self.dval[i])
        return ins

    def custom(self, q, fn, rd=(), wr=(), inc=1):
        i = self._slot(q, rd, wr)
        ins = fn(self.eng[q])
        self.dval[i] += inc
        ins.then_inc(self.dsem[i], inc)
        self._mark(rd, wr, ("d", i), self.dval[i])
        return ins

    def barrier(self, engines=None):
        deps = {k: v for k, v in self.cnt.items() if v > 0}
        for i in range(self.NDMA):
            if self.dval[i] > 0:
                deps[("d", i)] = self.dval[i]
        for e in (engines or list(self.eng)):
            d = dict(deps)
            d.pop(e, None)
            self._wait(e, d)


def _t5_bucket_np(rel):
    half, max_exact = 16, 8
    ret = np.where(rel > 0, half, 0)
    n = np.abs(rel)
    large = max_exact + (np.log(np.maximum(n, max_exact).astype(np.float32) / max_exact)
                         / math.log(1024 / max_exact) * (half - max_exact)).astype(np.int32)
    large = np.minimum(large, half - 1)
    return ret + np.where(n < max_exact, n, large)


def host_tables(r):
    tb = {}
    tb["ident"] = np.eye(128, dtype=np.float32)
    half = 16
    inv = 10000.0 ** (-np.arange(half, dtype=np.float64) / half)
    rp = np.zeros((32, 2, 2, T), np.float32)
    for kind in range(2):
        pos = (np.arange(T) + (T * r if kind == 0 else 0)).astype(np.float64)
        ang = pos[None, :] * inv[:, None]
        cs, sn = np.cos(ang), np.sin(ang)
        rp[:, 0, kind] = np.concatenate([cs, cs], 0)
        rp[:, 1, kind] = np.concatenate([-sn, sn], 0)
    tb["rope"] = rp.astype(NPBF)
    kk = np.arange(256)[:, None]
    qq = np.arange(128)[None, :]
    rel = kk - 64 - qq
    oh = np.zeros((3, 128, 32, 2, 128), np.float32)
    for g, dil in enumerate((1, 4, 16)):
        bk = _t5_bucket_np(rel * dil)
        for b in range(32):
            m = (bk == b).astype(np.float32)
            oh[g, :, b, 0] = m[0:128]
            oh[g, :, b, 1] = m[128:256]
    tb["oh"] = oh.astype(NPBF)
    band = np.where(np.abs(rel) <= 64, 0.0, NEG).astype(np.float32)
    tb["band"] = np.stack([band[0:128], band[128:256]], 1).copy()
    j = np.arange(192)
    ang = 2 * np.pi * np.outer(j, j) / 192.0
    ch = np.zeros((2, 192, 192), np.float32)
    ch[0] = np.cos(ang) / math.sqrt(192)
    ch[1] = -np.sin(ang) / math.sqrt(192)
    tb["chdft"] = ch
    return tb


_DFT_CACHE = {}


def dft_table(S, k0, nk):
    key = (S, k0, nk)
    if key not in _DFT_CACHE:
        n = np.arange(S, dtype=np.int64)[:, None]
        k = (k0 + np.arange(nk, dtype=np.int64))[None, :]
        ph = ((n * k) % S).astype(np.float64) * (2 * np.pi / S)
        out = np.empty((2, S, nk), NPBF)
        out[0] = (np.cos(ph) / math.sqrt(S)).astype(NPBF)
        out[1] = (np.sin(ph) / math.sqrt(S)).astype(NPBF)
        _DFT_CACHE[key] = out
    return _DFT_CACHE[key]


def build(flags):
    depth = flags["depth"]
    jobs = flags["jobs"]
    nc = bass.Bass("TRN2", target_bir_lowering=False)

    def din(name, shape, dt=F32):
        return nc.dram_tensor(name, list(shape), dt, kind="ExternalInput").ap()

    x_in = din("x", [3, T, D])
    cT_in = din("cT", [128, KD, 3])
    relb_in = din("relb", [128, 192])
    ada_w = din("ada_w", [DEPTH, D, 6 * D])
    ada_bT = din("ada_bT", [128, DEPTH, 48])
    g1T = din("g1T", [128, DEPTH, KD])
    g2T = din("g2T", [128, DEPTH, KD])
    gfT = din("gfT", [128, KD])
    w_in = din("w_in", [DEPTH, D, C_END])
    qng = din("qng", [128, DEPTH, 3])
    kvg = din("kvg", [128, DEPTH, 3])
    w_uq = din("w_uq", [DEPTH, 384, 768])
    w_ukv = din("w_ukv", [DEPTH, 320, 512])
    lng = din("lng", [128, DEPTH, 384])
    lnb = din("lnb", [128, DEPTH, 384])
    wsT = din("wsT", [DEPTH, 4, 128, 128])
    sgb = din("sgb", [128, DEPTH, 4, 128])
    p_all = din("p_all", [DEPTH, 1536, D])
    w_o = din("w_o", [DEPTH, D, D])
    w1 = din("w1", [DEPTH, D, 4 * D])
    w2 = din("w2", [DEPTH, 4 * D, D])
    ident_in = din("ident", [128, 128])
    rope_in = din("rope", [32, 2, 2, T], BF16)
    oh_in = din("oh", [3, 128, 32, 2, 128], BF16)
    band_in = din("band", [128, 2, 128])
    chdft_in = din("chdft", [2, 192, 192])
    dftS = din("dftS", [2, T, T], BF16)
    dftP = din("dftP", [2, 8 * T, T], BF16)
    y_out = nc.dram_tensor("y", [3, T, D], F32, kind="ExternalOutput").ap()

    def dscr(name, shape, dt=BF16):
        return nc.dram_tensor(name, list(shape), dt).ap()

    xT = dscr("xT", [3, D, T], F32)
    W16 = {"w_in": dscr("w16_in", [DEPTH, D, C_END]), "p_all": dscr("w16_p", [DEPTH, 1536, D]), "w_o": dscr("w16_o", [DEPTH, D, D]),
           "w1": dscr("w16_1", [DEPTH, D, 4 * D]), "w2": dscr("w16_2", [DEPTH, 4 * D, D])}
    W32 = {"w_in": w_in, "p_all": p_all, "w_o": w_o, "w1": w1, "w2": w2}
    PK = dscr("PK", [R_TOT, T])
    AGB = dscr("AGB", [10 * R_TOT, T])
    ZA = {1: dscr("ZA1", [768, T]), 2: dscr("ZA2", [768, T])}
    QD = [dscr("QD%d" % j, [390, T]) for j in range(3)]
    KDs = [dscr("KD%d" % j, [390, 2 * T]) for j in range(3)]
    VDs = [dscr("VD%d" % j, [2 * T + 16, 390]) for j in range(3)]
    CU = [dscr("CU%d" % j, [384, T]) for j in range(3)]
    CV = [dscr("CV%d" % j, [T, 384]) for j in range(3)]
    CQ = [dscr("CQ%d" % j, [768, T]) for j in range(3)]
    GT = [dscr("GT%d" % j, [4 * D, T]) for j in range(3)]
    MT = [dscr("MT%d" % j, [1536, T]) for j in range(3)]
    KM = {1: dscr("KM1", [384, T]), 2: dscr("KM2", [384, T])}
    VM = {1: dscr("VM1", [T, 260]), 2: dscr("VM2", [T, 260])}
    QM = [dscr("QM%d" % j, [384, T]) for j in range(3)]
    DACC = [dscr("DACC%d" % j, [3, T + 16, 130], F32) for j in range(3)]

    with ExitStack() as st:
        PID = [nc.gpsimd.partition_id()]
        P = Prog(nc, st)
        uid = [0]

        def sb(shape, dt, stack=st, name=None):
            uid[0] += 1
            return stack.enter_context(nc.sbuf_tensor(name or ("t%d" % uid[0]), list(shape), dt))

        ident = sb([128, 128], F32)
        ones_bf = sb([128, 128], BF16)
        eps_t = sb([128, 1], F32)
        zero_t = sb([128, 1], F32)
        mods = sb([128, DEPTH, 48, 3], F32)
        a1 = sb([128, DEPTH, KD, 3], F32)
        a2 = sb([128, DEPTH, KD, 3], F32)
        g1s = sb([128, DEPTH, KD], F32)
        g2s = sb([128, DEPTH, KD], F32)
        gfs = sb([128, KD], F32)
        qngs = sb([128, DEPTH, 3], F32)
        kvgs = sb([128, DEPTH, 3], F32)
        biasT = sb([128, 6, 2, 128], BF16)
        ident_bf = sb([128, 128], BF16)
        cf = [sb([128, 2048], F32) for _ in range(2)]
        wb = [sb([128, 8, 512], BF16) for _ in range(4)]
        cvb = [sb([128, 2048], BF16) for _ in range(2)]
        ps = [st.enter_context(nc.psum_tensor("ps%d" % i, [128, 512], F32)) for i in range(8)]
        psk = ["ps%d" % i for i in range(8)]
        psrr = [0]

        def nps():
            i = psrr[0]
            psrr[0] = (i + 1) % 8
            return i

        P.dma("sp", ident[:], ident_in, wr=["ident"])
        P.op("pool", lambda e: e.memset(ones_bf[:], 1.0), wr=["ones"])
        P.op("pool", lambda e: e.memset(eps_t[:], EPS), wr=["eps"])
        P.op("pool", lambda e: e.memset(zero_t[:], 0.0), wr=["zero"])
        P.op("pool", lambda e: e.tensor_copy(out=ident_bf[:], in_=ident[:]), rd=["ident"], wr=["identbf"])
        P.dma("sp", g1s[:], g1T, wr=["g1s"])
        P.dma("sp", g2s[:], g2T, wr=["g2s"])
        P.dma("sp", gfs[:], gfT, wr=["gfs"])
        P.dma("sp", qngs[:], qng, wr=["qngs"])
        P.dma("sp", kvgs[:], kvg, wr=["kvgs"])

        wst = {"f": 0, "b": 0}

        def wload(wname, l, kch, c0, ncols):
            src = W16[wname][l]
            bi = wst["b"]; wst["b"] = (bi + 1) % 4
            bk = "wb%d" % bi
            full = all(n == 128 for _, n in kch) and all(kch[i][0] == kch[0][0] + 128 * i for i in range(len(kch)))
            if full:
                r0 = kch[0][0]
                P.dma("sp", wb[bi][:, 0:len(kch), 0:ncols],
                      src[r0:r0 + 128 * len(kch), c0:c0 + ncols].rearrange("(k p) c -> p k c", p=128), rd=[("W16", wname, l)], wr=[bk])
            else:
                for i, (r0, n) in enumerate(kch):
                    P.dma("sp", wb[bi][0:n, i, 0:ncols], src[r0:r0 + n, c0:c0 + ncols], rd=[("W16", wname, l)], wr=[bk])
            return wb[bi], bk

        cvi = [0]

        def convert_weights(l):
            for wname in ("w_in", "p_all", "w_o", "w1", "w2"):
                src, dst = W32[wname][l], W16[wname][l]
                nr, ncol = src.shape[0], src.shape[1]
                npc = (ncol + 2047) // 2048
                cw = (ncol + npc - 1) // npc
                for r0 in range(0, nr, 128):
                    for c0 in range(0, ncol, cw):
                        n = min(cw, ncol - c0)
                        i = cvi[0]; cvi[0] += 1
                        fi = i % 2
                        fk, ck = "cf%d" % fi, "cvb%d" % fi
                        fv = cf[fi]
                        P.dma("sp", fv[:, 0:n], src[r0:r0 + 128, c0:c0 + n], wr=[fk])
                        if i % 2 == 0:
                            P.op("dve", lambda e, fi=fi, n=n, fv=fv: e.tensor_copy(out=cvb[fi][:, 0:n], in_=fv[:, 0:n]), rd=[fk], wr=[ck])
                        else:
                            P.op("act", lambda e, fi=fi, n=n, fv=fv: e.activation(out=cvb[fi][:, 0:n], in_=fv[:, 0:n], func=AF.Copy), rd=[fk], wr=[ck])
                        P.dma("sp", dst[r0:r0 + 128, c0:c0 + n], cvb[fi][:, 0:n], rd=[ck], wr=[("W16", wname, l)])

        with ExitStack() as s1:
            wf = [sb([128, 8, 512], F32, s1) for _ in range(2)]
            cT = sb([128, KD, 3], F32, s1)
            sc = sb([128, KD, 3], F32, s1)
            abT = sb([128, DEPTH, 48], F32, s1)
            P.dma("sp", cT[:], cT_in, wr=["cT"])
            P.dma("sp", abT[:], ada_bT, wr=["abT"])
            P.op("act", lambda e: e.activation(out=sc[:], in_=cT[:], func=AF.Silu), rd=["cT"], wr=["sc"])
            for l in range(depth):
                for pc in range(12):
                    fi = wst["f"]; wst["f"] = (fi + 1) % 2
                    fk = "wf%d" % fi
                    P.dma("sp", wf[fi][:], ada_w[l, :, pc * 512:(pc + 1) * 512].rearrange("(k p) c -> p k c", p=128), wr=[fk])
                    pi = nps()
                    def mm(e, fi=fi, pi=pi):
                        ins = None
                        for cc in range(4):
                            for k in range(KD):
                                ins = e.matmul(ps[pi][:, cc * 3:cc * 3 + 3], lhsT=wf[fi][:, k, cc * 128:(cc + 1) * 128], rhs=sc[:, k, :],
                                               start=(k == 0), stop=(k == KD - 1))
                        return ins
                    P.op("pe", mm, rd=[fk, "sc"], wr=[psk[pi]])
                    P.op("dve", lambda e, pi=pi, pc=pc, l=l: e.tensor_copy(
                        out=mods[:, l, pc * 4:(pc + 1) * 4, :], in_=ps[pi][:, 0:12].rearrange("p (c j) -> p c j", j=3)),
                        rd=[psk[pi]], wr=["mods"])
                for j in range(3):
                    P.op("dve", lambda e, l=l, j=j: e.tensor_tensor(out=mods[:, l, :, j], in0=mods[:, l, :, j], in1=abT[:, l, :], op=ALU.add),
                         rd=["mods", "abT"], wr=["mods"])
                    P.op("dve", lambda e, l=l, j=j: e.scalar_tensor_tensor(out=a1[:, l, :, j], in0=mods[:, l, 8:16, j], scalar=1.0, in1=g1s[:, l, :],
                                                                          op0=ALU.add, op1=ALU.mult), rd=["mods", "g1s"], wr=["a1"])
                    P.op("dve", lambda e, l=l, j=j: e.scalar_tensor_tensor(out=a2[:, l, :, j], in0=mods[:, l, 32:40, j], scalar=1.0, in1=g2s[:, l, :],
                                                                          op0=ALU.add, op1=ALU.mult), rd=["mods", "g2s"], wr=["a2"])
            P.barrier()

        with ExitStack() as s1:
            xin = [sb([128, D], F32, s1) for _ in range(2)]
            xo = [sb([128, KD, 128], F32, s1) for _ in range(2)]
            it = 0
            for j in jobs:
                for tb in range(T // 128):
                    b = it % 2; it += 1
                    P.dma("sp", xin[b][:], x_in[j, tb * 128:(tb + 1) * 128, :], wr=["xin%d" % b])
                    for hf in range(2):
                        pi = nps()
                        def tr(e, b=b, hf=hf, pi=pi):
                            ins = None
                            for q in range(4):
                                k = hf * 4 + q
                                ins = e.transpose(out=ps[pi][:, q * 128:(q + 1) * 128], in_=xin[b][:, k * 128:(k + 1) * 128], identity=ident[:])
                            return ins
                        P.op("pe", tr, rd=["xin%d" % b, "ident"], wr=[psk[pi]])
                        eng = "act" if hf == 0 else "dve"
                        if eng == "act":
                            P.op("act", lambda e, b=b, hf=hf, pi=pi: e.activation(out=xo[b][:, hf * 4:hf * 4 + 4, :], in_=ps[pi][:].rearrange("p (k t) -> p k t", t=128), func=AF.Copy),
                                 rd=[psk[pi]], wr=["xo%d%d" % (b, hf)])
                        else:
                            P.op("dve", lambda e, b=b, hf=hf, pi=pi: e.tensor_copy(out=xo[b][:, hf * 4:hf * 4 + 4, :], in_=ps[pi][:].rearrange("p (k t) -> p k t", t=128)),
                                 rd=[psk[pi]], wr=["xo%d%d" % (b, hf)])
                    P.dma("sp", xT[j, :, tb * 128:(tb + 1) * 128].rearrange("(k p) t -> p k t", p=128), xo[b][:],
                          rd=["xo%d0" % b, "xo%d1" % b], wr=[("xT", j)])
            P.barrier()

        def rms_tile(s1, src_tile, src_key, nk, a_ap_fn, b_ap_fn, dst, dst_key, dst_off, ncol, inv_n, tmp, tmpk):
            sq, rs = tmp
            P.op("act", lambda e: e.activation(out=sq[:, 0:nk, 0:ncol], in_=src_tile[:, 0:nk, 0:ncol], func=AF.Square), rd=[src_key], wr=[tmpk + "sq"])
            pi = nps()
            def mm(e):
                ins = None
                for k in range(nk):
                    ins = e.matmul(ps[pi][:, 0:ncol], lhsT=ones_bf[:], rhs=sq[:, k, 0:ncol], start=(k == 0), stop=(k == nk - 1))
                return ins
            P.op("pe", mm, rd=[tmpk + "sq", "ones"], wr=[psk[pi]])
            P.op("act", lambda e: e.activation(out=rs[:, 0:ncol], in_=ps[pi][:, 0:ncol], func=AF.Sqrt, scale=inv_n, bias=eps_t[:, 0:1]), rd=[psk[pi], "eps"], wr=[tmpk + "rs"])
            P.op("dve", lambda e: e.reciprocal(out=rs[:, 0:ncol], in_=rs[:, 0:ncol]), rd=[tmpk + "rs"], wr=[tmpk + "rs"])
            for k in range(nk):
                P.op("dve", lambda e, k=k: e.tensor_tensor(out=src_tile[:, k, 0:ncol], in0=src_tile[:, k, 0:ncol], in1=rs[:, 0:ncol], op=ALU.mult),
                     rd=[src_key, tmpk + "rs"], wr=[src_key])
                P.op("act", lambda e, k=k: e.activation(out=dst[:, k, dst_off:dst_off + ncol], in_=src_tile[:, k, 0:ncol], func=AF.Identity,
                                                        scale=a_ap_fn(k), bias=b_ap_fn(k)), rd=[src_key], wr=[dst_key])

        negM = sb([128, 8], F32)

        def sumsq_max(s1, src_fn, K, ntiles, dst_ap, rdkeys, tag):
            sqb = sb([128, 512], BF16, s1)
            mx = sb([128, 32], F32, s1)
            for t in range(ntiles):
                P.op("act", lambda e, t=t: e.activation(out=sqb[0:K, :], in_=src_fn(t), func=AF.Square), rd=rdkeys, wr=[tag + "sqb"])
                pi = nps()
                P.op("pe", lambda e, pi=pi: e.matmul(ps[pi][:, :], lhsT=ones_bf[0:K, :], rhs=sqb[0:K, :], start=True, stop=True), rd=[tag + "sqb", "ones"], wr=[psk[pi]])
                P.op("dve", lambda e, pi=pi, t=t: e.tensor_reduce(out=mx[:, t:t + 1], in_=ps[pi][:, :], axis=AX.X, op=ALU.max), rd=[psk[pi]], wr=[tag + "mx"])
            P.op("dve", lambda e: e.tensor_reduce(out=dst_ap, in_=mx[:, 0:ntiles], axis=AX.X, op=ALU.max), rd=[tag + "mx"], wr=[tag + "dst"])

        def do_sgu(l, j):
            with ExitStack() as s1:
                lg = sb([128, 384], F32, s1); lb = sb([128, 384], F32, s1)
                wsf = sb([128, 4, 128], F32, s1); wsb = sb([128, 4, 128], BF16, s1)
                sbt = sb([128, 4, 128], F32, s1)
                P.dma("sp", lg[:], lng[:, l, :], wr=["lg"]); P.dma("sp", lb[:], lnb[:, l, :], wr=["lb"])
                P.dma("sp", wsf[:], wsT[l].rearrange("h q p -> q h p"), wr=["wsf"])
                P.dma("sp", sbt[:], sgb[:, l, :, :], wr=["sbt"])
                P.op("pool", lambda e: e.tensor_copy(out=wsb[:], in_=wsf[:]), rd=["wsf"], wr=["wsb"])
                cvt = [sb([128, 384], BF16, s1) for _ in range(2)]
                cut = [sb([128, 4, 128], BF16, s1) for _ in range(2)]
                vn = [sb([128, 384], F32, s1) for _ in range(2)]
                vnb = [sb([128, 384], BF16, s1) for _ in range(2)]
                stt = [sb([128, 8], F32, s1) for _ in range(2)]
                mxf = [sb([128, 4, 128], F32, s1) for _ in range(2)]
                ob = [sb([128, 4, 128], BF16, s1) for _ in range(2)]
                for n in range(T // 128):
                    b = n % 2
                    tsl = slice(n * 128, (n + 1) * 128)
                    P.dma("sp", cvt[b][:], CV[j][tsl, :], rd=[("CV", j)], wr=["cvt%d" % b])
                    P.dma("sp", cut[b][0:96], CU[j][:, tsl].rearrange("(h c) t -> c h t", h=4), rd=[("CU", j)], wr=["cut%d" % b])
                    P.op("dve", lambda e, b=b: e.bn_stats(out=stt[b][:, 0:6], in_=cvt[b][:]), rd=["cvt%d" % b], wr=["stt%d" % b])
                    P.op("dve", lambda e, b=b: e.bn_aggr(out=stt[b][:, 6:8], in_=stt[b][:, 0:6]), rd=["stt%d" % b], wr=["stt%d" % b])
                    P.op("act", lambda e, b=b: e.activation(out=stt[b][:, 7:8], in_=stt[b][:, 7:8], func=AF.Sqrt, bias=eps_t[:, 0:1]), rd=["stt%d" % b, "eps"], wr=["stt%d" % b])
                    P.op("dve", lambda e, b=b: e.reciprocal(out=stt[b][:, 7:8], in_=stt[b][:, 7:8]), rd=["stt%d" % b], wr=["stt%d" % b])
                    P.op("dve", lambda e, b=b: e.tensor_scalar(out=vn[b][:], in0=cvt[b][:], scalar1=stt[b][:, 6:7], scalar2=stt[b][:, 7:8], op0=ALU.subtract, op1=ALU.mult),
                         rd=["cvt%d" % b, "stt%d" % b], wr=["vn%d" % b])
                    P.op("pool", lambda e, b=b: e.tensor_tensor(out=vn[b][:], in0=vn[b][:], in1=lg[:], op=ALU.mult), rd=["vn%d" % b, "lg"], wr=["vn%d" % b])
                    P.op("pool", lambda e, b=b: e.tensor_tensor(out=vnb[b][:], in0=vn[b][:], in1=lb[:], op=ALU.add), rd=["vn%d" % b, "lb"], wr=["vnb%d" % b])
                    pi = nps()
                    def mm(e, pi=pi, b=b):
                        ins = None
                        for h in range(4):
                            ins = e.matmul(ps[pi][0:96, h * 128:(h + 1) * 128], lhsT=vnb[b][:, h * 96:(h + 1) * 96], rhs=wsb[:, h, :], start=True, stop=True)
                        return ins
                    P.op("pe", mm, rd=["vnb%d" % b, "wsb"], wr=[psk[pi]])
                    P.op("dve", lambda e, pi=pi, b=b: e.tensor_tensor(out=mxf[b][0:96], in0=ps[pi][0:96, :].rearrange("p (h t) -> p h t", h=4), in1=sbt[0:96], op=ALU.add),
                         rd=[psk[pi], "sbt"], wr=["mxf%d" % b])
                    P.op("pool", lambda e, b=b: e.tensor_tensor(out=ob[b][0:96], in0=mxf[b][0:96], in1=cut[b][0:96], op=ALU.mult), rd=["mxf%d" % b, "cut%d" % b], wr=["ob%d" % b])
                    P.dma("sp", MT[j][896:1280, tsl].rearrange("(h c) t -> c h t", h=4), ob[b][0:96], rd=["ob%d" % b], wr=[("MT", j)])
            P.barrier()

        def do_mla(l, j):
            nk_t = 64 if j == 0 else 16
            nkt = 128 if j == 0 else 16
            kind = 0 if j == 0 else 1
            kdst = PK[R_KM:R_KM + 384, :] if j == 0 else KM[j]
            vdst = (PK[R_VM:R_VM + 260, :].rearrange("r c -> (r c)").rearrange("(t c) -> t c", c=260)) if j == 0 else VM[j]
            with ExitStack() as s1:
                with ExitStack() as s2:
                    cq = sb([128, 6, T], BF16, s2)
                    ksw = sb([128, T], BF16, s2)
                    rp = sb([128, 2, T], BF16, s2)
                    wqf = sb([128, 3, 768], F32, s2); wqb = sb([128, 3, 768], BF16, s2)
                    wkf = sb([128, 3, 512], F32, s2); wkb = sb([128, 3, 512], BF16, s2)
                    sqb = sb([128, 6, 512], BF16, s2)
                    rq = sb([128, 512], F32, s2); rk = sb([128, 512], F32, s2)
                    qf = sb([128, 512], F32, s2); t1 = sb([128, 512], F32, s2); t2 = sb([128, 512], F32, s2)
                    qst = sb([128, 4, 512], BF16, s2)
                    kst = sb([128, 4, 512], BF16, s2)
                    kpe = sb([128, 512], BF16, s2)
                    rkt = sb([128, 4], F32, s2)
                    vst = sb([128, 4, 4, 65], BF16, s2)
                    P.dma("sp", cq[:], CQ[j].rearrange("(k p) t -> p k t", p=128), rd=[("CQ", j)], wr=["cq"])
                    P.dma("sp", ksw[64:96, :], CQ[j][736:768, :], rd=[("CQ", j)], wr=["ksw"])
                    P.dma("sp", rp[64:96], rope_in[:, :, kind, :], wr=["rp"])
                    P.dma("sp", wqf[:], w_uq[l].rearrange("(k p) c -> p k c", p=128), wr=["wqf"])
                    P.dma("sp", wkf[:, 0:2, :], w_ukv[l, 0:256, :].rearrange("(k p) c -> p k c", p=128), wr=["wkf"])
                    P.dma("sp", wkf[0:64, 2, :], w_ukv[l, 256:320, :], wr=["wkf"])
                    P.op("pool", lambda e: e.memset(vst[:], 1.0), wr=["vst"])
                    for k in range(3):
                        P.op("dve", lambda e, k=k: e.tensor_scalar(out=wqb[:, k, :], in0=wqf[:, k, :], scalar1=qngs[:, l, k:k + 1], scalar2=96.0 ** -0.5, op0=ALU.mult, op1=ALU.mult),
                             rd=["wqf", "qngs"], wr=["wqb"])
                        np_ = 128 if k < 2 else 64
                        P.op("dve", lambda e, k=k, np_=np_: e.tensor_scalar(out=wkb[0:np_, k, :], in0=wkf[0:np_, k, :], scalar1=kvgs[0:np_, l, k:k + 1], scalar2=1.0, op0=ALU.mult, op1=ALU.mult),
                             rd=["wkf", "kvgs"], wr=["wkb"])
                    for tt in range(NTT):
                        tsl = slice(tt * 512, (tt + 1) * 512)
                        P.op("act", lambda e, tsl=tsl: e.activation(out=sqb[:], in_=cq[:, :, tsl], func=AF.Square), rd=["cq"], wr=["sqb"])
                        pq, pk = nps(), nps()
                        def mmq(e, pq=pq):
                            ins = None
                            for k in range(3):
                                ins = e.matmul(ps[pq][:, :], lhsT=ones_bf[:], rhs=sqb[:, k, :], start=(k == 0), stop=(k == 2))
                            return ins
                        P.op("pe", mmq, rd=["sqb", "ones"], wr=[psk[pq]])
                        def mmk(e, pk=pk):
                            e.matmul(ps[pk][:, :], lhsT=ones_bf[:], rhs=sqb[:, 3, :], start=True, stop=False)
                            e.matmul(ps[pk][:, :], lhsT=ones_bf[:], rhs=sqb[:, 4, :], start=False, stop=False)
                            return e.matmul(ps[pk][:, :], lhsT=ones_bf[0:64, :], rhs=sqb[0:64, 5, :], start=False, stop=True)
                        P.op("pe", mmk, rd=["sqb", "ones"], wr=[psk[pk]])
                        P.op("act", lambda e, pq=pq: e.activation(out=rq[:], in_=ps[pq][:, :], func=AF.Sqrt, scale=1.0 / 384, bias=eps_t[:, 0:1]), rd=[psk[pq], "eps"], wr=["rq"])
                        P.op("dve", lambda e: e.reciprocal(out=rq[:], in_=rq[:]), rd=["rq"], wr=["rq"])
                        P.op("act", lambda e, pk=pk: e.activation(out=rk[:], in_=ps[pk][:, :], func=AF.Sqrt, scale=1.0 / 320, bias=eps_t[:, 0:1]), rd=[psk[pk], "eps"], wr=["rk"])
                        P.op("dve", lambda e: e.reciprocal(out=rk[:], in_=rk[:]), rd=["rk"], wr=["rk"])
                        pv = nps()
                        def mmv(e, pv=pv):
                            ins = None
                            for tb in range(4):
                                e.matmul(ps[pv][:, tb:tb + 1], lhsT=sqb[:, 3, tb * 128:(tb + 1) * 128], rhs=ones_bf[:, 0:1], start=True, stop=False)
                                e.matmul(ps[pv][:, tb:tb + 1], lhsT=sqb[:, 4, tb * 128:(tb + 1) * 128], rhs=ones_bf[:, 0:1], start=False, stop=False)
                                ins = e.matmul(ps[pv][:, tb:tb + 1], lhsT=sqb[0:64, 5, tb * 128:(tb + 1) * 128], rhs=ones_bf[0:64, 0:1], start=False, stop=True)
                            return ins
                        P.op("pe", mmv, rd=["sqb", "ones"], wr=[psk[pv]])
                        P.op("act", lambda e, pv=pv: e.activation(out=rkt[:], in_=ps[pv][:, 0:4], func=AF.Sqrt, scale=1.0 / 320, bias=eps_t[:, 0:1]), rd=[psk[pv], "eps"], wr=["rkt"])
                        P.op("dve", lambda e: e.reciprocal(out=rkt[:], in_=rkt[:]), rd=["rkt"], wr=["rkt"])
                        P.op("dve", lambda e, tsl=tsl: e.tensor_tensor(out=t1[64:96], in0=cq[64:96, 5, tsl], in1=rp[64:96, 0, tsl], op=ALU.mult), rd=["cq", "rp"], wr=["t1"])
                        P.op("dve", lambda e, tsl=tsl: e.tensor_tensor(out=t2[64:96], in0=ksw[64:96, tsl], in1=rp[64:96, 1, tsl], op=ALU.mult), rd=["ksw", "rp"], wr=["t2"])
                        P.op("dve", lambda e: e.tensor_tensor(out=kpe[64:96], in0=t1[64:96], in1=t2[64:96], op=ALU.add), rd=["t1", "t2"], wr=["kpe"])
                        for h in range(4):
                            p1, p2 = nps(), nps()
                            def mq(e, p1=p1, p2=p2, h=h, tsl=tsl):
                                ins = None
                                for k in range(3):
                                    e.matmul(ps[p1][0:96, :], lhsT=wqb[:, k, h * 96:(h + 1) * 96], rhs=cq[:, k, tsl], start=(k == 0), stop=(k == 2))
                                for k in range(3):
                                    ins = e.matmul(ps[p2][0:96, :], lhsT=wqb[:, k, 384 + h * 96:384 + (h + 1) * 96], rhs=cq[:, k, tsl], start=(k == 0), stop=(k == 2))
                                return ins
                            P.op("pe", mq, rd=["wqb", "cq"], wr=[psk[p1], psk[p2]])
                            P.op("dve", lambda e, p1=p1: e.tensor_tensor(out=qf[0:96], in0=ps[p1][0:96, :], in1=rq[0:96], op=ALU.mult), rd=[psk[p1], "rq"], wr=["qf"])
                            P.op("dve", lambda e, p2=p2: e.tensor_tensor(out=t1[64:96], in0=ps[p2][64:96, :], in1=rq[64:96], op=ALU.mult), rd=[psk[p2], "rq"], wr=["t1"])
                            P.op("dve", lambda e, tsl=tsl: e.tensor_tensor(out=t1[64:96], in0=t1[64:96], in1=rp[64:96, 1, tsl], op=ALU.mult), rd=["t1", "rp"], wr=["t1"])
                            P.op("dve", lambda e, tsl=tsl: e.tensor_tensor(out=t2[64:96], in0=qf[64:96], in1=rp[64:96, 0, tsl], op=ALU.mult), rd=["qf", "rp"], wr=["t2"])
                            P.op("dve", lambda e: e.tensor_tensor(out=qf[64:96], in0=t1[64:96], in1=t2[64:96], op=ALU.add), rd=["t1", "t2"], wr=["qf"])
                            P.op("act", lambda e, h=h: e.activation(out=qst[0:96, h, :], in_=qf[0:96], func=AF.Copy), rd=["qf"], wr=["qst"])
                            p3 = nps()
                            def mk(e, p3=p3, h=h, tsl=tsl):
                                e.matmul(ps[p3][0:64, :], lhsT=wkb[:, 0, h * 64:(h + 1) * 64], rhs=cq[:, 3, tsl], start=True, stop=False)
                                e.matmul(ps[p3][0:64, :], lhsT=wkb[:, 1, h * 64:(h + 1) * 64], rhs=cq[:, 4, tsl], start=False, stop=False)
                                return e.matmul(ps[p3][0:64, :], lhsT=wkb[0:64, 2, h * 64:(h + 1) * 64], rhs=cq[0:64, 5, tsl], start=False, stop=True)
                            P.op("pe", mk, rd=["wkb", "cq"], wr=[psk[p3]])
                            P.op("dve", lambda e, p3=p3, h=h: e.tensor_tensor(out=kst[0:64, h, :], in0=ps[p3][0:64, :], in1=rk[0:64], op=ALU.mult), rd=[psk[p3], "rk"], wr=["kst"])
                            P.op("act", lambda e, h=h: e.activation(out=kst[64:96, h, :], in_=kpe[64:96], func=AF.Copy), rd=["kpe"], wr=["kst"])
                        P.dma("sp", QM[j][:, tsl].rearrange("(h d) t -> d h t", h=4), qst[0:96], rd=["qst"], wr=[("QM", j)])
                        P.dma("sp", kdst[:, tsl].rearrange("(h d) t -> d h t", h=4), kst[0:96], rd=["kst"], wr=[("KM", j)])
                        for tb in range(4):
                            pv2 = nps()
                            def mv(e, pv2=pv2, tb=tb, tt=tt):
                                c0 = tt * 512 + tb * 128
                                e.matmul(ps[pv2][:, 0:256], lhsT=cq[:, 3, c0:c0 + 128], rhs=wkb[:, 0, 256:512], start=True, stop=False)
                                e.matmul(ps[pv2][:, 0:256], lhsT=cq[:, 4, c0:c0 + 128], rhs=wkb[:, 1, 256:512], start=False, stop=False)
                                return e.matmul(ps[pv2][:, 0:256], lhsT=cq[0:64, 5, c0:c0 + 128], rhs=wkb[0:64, 2, 256:512], start=False, stop=True)
                            P.op("pe", mv, rd=["wkb", "cq"], wr=[psk[pv2]])
                            P.op("dve", lambda e, pv2=pv2, tb=tb: e.tensor_scalar(out=vst[:, tb, :, 0:64], in0=ps[pv2][:, 0:256].rearrange("p (h d) -> p h d", h=4), scalar1=rkt[:, tb:tb + 1], scalar2=1.0, op0=ALU.mult, op1=ALU.mult),
                                 rd=[psk[pv2], "rkt"], wr=["vst"])
                        P.dma("sp", vdst[tt * 512:(tt + 1) * 512, :].rearrange("(tb p) c -> p tb c", p=128), vst[:].rearrange("p tb h d -> p tb (h d)"), rd=["vst"], wr=[("VM", j)])
                P.barrier()

        def do_ag():
            P.custom("pool", lambda e: e.collective_compute("AllGather", ALU.bypass, replica_groups=[list(range(NCORES))],
                                                            ins=[PK], outs=[AGB[R_TOT:9 * R_TOT, :]]),
                     rd=[("ZA", 0), ("KD", 0), ("VD", 0), ("KM", 0), ("VM", 0)], wr=["AGB"], inc=1)

        def mla_attend(l, j):
            nkt = 128 if j == 0 else 16
            with ExitStack() as s1:
                otok = sb([128, 16, 256], F32, s1)
                mq = sb([128, 4], F32, s1)
                pT = [sb([128, 512], BF16, s1) for _ in range(3)]
                rc = sb([128, 4], F32, s1)
                for h in range(4):
                    with ExitStack() as s2:
                        kT = sb([128, nkt * 128], BF16, s2)
                        vv = sb([128, nkt, 65], BF16, s2)
                        qT = sb([128, T], BF16, s2)
                        if j == 0:
                            for s_ in range(8):
                                base = (1 + s_) * R_TOT
                                P.dma("sp", kT[0:96, s_ * T:(s_ + 1) * T], AGB[base + R_KM + h * 96: base + R_KM + (h + 1) * 96, :], rd=["AGB"], wr=["kT"])
                                vsrc = AGB[base + R_VM: base + R_VM + 260, :].rearrange("r c -> (r c)").rearrange("(tb p c) -> p tb c", p=128, c=260)
                                P.dma("sp", vv[:, s_ * 16:(s_ + 1) * 16, :], vsrc[:, :, h * 65:(h + 1) * 65], rd=["AGB"], wr=["vv"])
                        else:
                            P.dma("sp", kT[0:96, :], KM[j][h * 96:(h + 1) * 96, :], rd=[("KM", j)], wr=["kT"])
                            P.dma("sp", vv[:], VM[j].rearrange("(tb p) c -> p tb c", p=128)[:, :, h * 65:(h + 1) * 65], rd=[("VM", j)], wr=["vv"])
                        P.dma("sp", qT[0:96, :], QM[j][h * 96:(h + 1) * 96, :], rd=[("QM", j)], wr=["qT"])
                        sumsq_max(s2, lambda t: qT[0:96, t * 512:(t + 1) * 512], 96, 4, mq[:, 0:1], ["qT"], "mq")
                        sumsq_max(s2, lambda t: kT[0:96, t * 512:(t + 1) * 512], 96, nkt // 4, mq[:, 1:2], ["kT"], "mk")
                        P.op("dve", lambda e: e.tensor_tensor(out=mq[:, 2:3], in0=mq[:, 0:1], in1=mq[:, 1:2], op=ALU.mult), rd=["mqdst", "mkdst"], wr=["mq2"])
                        P.op("act", lambda e: e.activation(out=mq[:, 2:3], in_=mq[:, 2:3], func=AF.Sqrt), rd=["mq2"], wr=["mq2"])
                        P.op("dve", lambda e: e.tensor_scalar(out=mq[:, 3:4], in0=mq[:, 2:3], scalar1=-1.0, scalar2=1.0, op0=ALU.mult, op1=ALU.mult), rd=["mq2"], wr=["mq3"])
                        for qt in range(4):
                            def emitS(kt, qt=qt):
                                pS = kt % 4
                                P.op("pe", lambda e, pS=pS, kt=kt, qt=qt: e.matmul(ps[pS][:, :], lhsT=kT[0:96, kt * 128:(kt + 1) * 128], rhs=qT[0:96, qt * 512:(qt + 1) * 512], start=True, stop=True),
                                     rd=["kT", "qT"], wr=[psk[pS]])
                            def emitE(kt):
                                pS = kt % 4; b = kt % 3
                                P.op("act", lambda e, pS=pS, b=b: e.activation(out=pT[b][:], in_=ps[pS][:, :], func=AF.Exp, bias=mq[:, 3:4]), rd=[psk[pS], "mq3"], wr=["pT%d" % b])
                            def emitPV(kt):
                                b = kt % 3
                                def pv(e, b=b, kt=kt):
                                    ins = None
                                    for qs in range(4):
                                        ins = e.matmul(ps[4 + qs][:, 0:65], lhsT=pT[b][:, qs * 128:(qs + 1) * 128], rhs=vv[:, kt, :], start=(kt == 0), stop=(kt == nkt - 1))
                                    return ins
                                P.op("pe", pv, rd=["pT%d" % b, "vv"], wr=[psk[4], psk[5], psk[6], psk[7]])
                            emitS(0); emitS(1)
                            for kt in range(nkt):
                                if kt + 2 < nkt:
                                    emitS(kt + 2)
                                emitE(kt)
                                emitPV(kt)
                            for qs in range(4):
                                P.op("dve", lambda e, qs=qs: e.reciprocal(out=rc[:, qs:qs + 1], in_=ps[4 + qs][:, 64:65]), rd=[psk[4 + qs]], wr=["rc%d" % qs])
                                P.op("dve", lambda e, qs=qs, qt=qt, h=h: e.tensor_scalar(out=otok[:, qt * 4 + qs, h * 64:(h + 1) * 64], in0=ps[4 + qs][:, 0:64], scalar1=rc[:, qs:qs + 1], scalar2=1.0, op0=ALU.mult, op1=ALU.mult),
                                     rd=[psk[4 + qs], "rc%d" % qs], wr=["otok"])
                ost = [sb([128, 2, 128], BF16, s1) for _ in range(2)]
                for tq in range(16):
                    b = tq % 2
                    pi = tq % 4
                    def tr(e, pi=pi, tq=tq):
                        e.transpose(out=ps[pi][:, 0:128], in_=otok[:, tq, 0:128], identity=ident[:])
                        return e.transpose(out=ps[pi][:, 128:256], in_=otok[:, tq, 128:256], identity=ident[:])
                    P.op("pe", tr, rd=["otok", "ident"], wr=[psk[pi]])
                    P.op("act", lambda e, pi=pi, b=b: e.activation(out=ost[b][:], in_=ps[pi][:, 0:256].rearrange("p (a t) -> p a t", a=2), func=AF.Copy), rd=[psk[pi]], wr=["ost%d" % b])
                    P.dma("sp", MT[j][1280:1536, tq * 128:(tq + 1) * 128].rearrange("(a p) t -> p a t", p=128), ost[b][:], rd=["ost%d" % b], wr=[("MT", j)])
            P.barrier()

        def setup_dil():
            with ExitStack() as s1:
                c1 = sb([128, T], BF16, s1); cz = sb([128, 2 * T], BF16, s1); cn = sb([128, T], BF16, s1)
                vrow = sb([128, 390], BF16, s1)
                P.op("pool", lambda e: e.memset(c1[:], 1.0), wr=["c1"])
                P.op("pool", lambda e: e.memset(cz[:], 0.0), wr=["cz"])
                P.op("pool", lambda e: e.memset(cn[:], NEG), wr=["cn"])
                P.op("pool", lambda e: e.memset(vrow[:], 0.0), wr=["vrow"])
                P.op("pool", lambda e: e.memset(vrow[:].rearrange("p (g d) -> p g d", d=65)[:, :, 64:65], 1.0), wr=["vrow"])
                for j in jobs:
                    P.dma("sp", QD[j].rearrange("(g d) t -> g d t", d=65)[:, 64, :], c1[0:6, :], rd=["c1"], wr=[("QD", j)])
                def fillk(dst390, ncol, neg_ranges, key):
                    for r0 in range(0, 390, 128):
                        n = min(128, 390 - r0)
                        P.dma("sp", dst390[r0:r0 + n, :], cz[0:n, 0:ncol], rd=["cz"], wr=[key])
                    for (a, b_) in neg_ranges:
                        P.dma("sp", dst390.rearrange("(g d) t -> g d t", d=65)[:, 64, a:b_], cn[0:6, 0:b_ - a], rd=["cn"], wr=[key])
                def fillv(dst_rows_fn, nblk, key):
                    for n in range(nblk):
                        P.dma("sp", dst_rows_fn(n), vrow[:], rd=["vrow"], wr=[key])
                for j in jobs:
                    if j == 0:
                        fillk(PK[R_KD:R_KD + 390, :], T, [], ("KD", 0))
                        pv = PK[R_VD:R_VD + 390, :].rearrange("r c -> (r c)").rearrange("(t c) -> t c", c=390)
                        fillv(lambda n, pv=pv: pv[n * 128:(n + 1) * 128, :], 16, ("VD", 0))
                        for s_ in (0, 9):
                            base = s_ * R_TOT
                            fillk(AGB[base + R_KD: base + R_KD + 390, :], T, [(0, T)], "AGB")
                            av = AGB[base + R_VD: base + R_VD + 390, :].rearrange("r c -> (r c)").rearrange("(t c) -> t c", c=390)
                            fillv(lambda n, av=av: av[n * 128:(n + 1) * 128, :], 16, "AGB")
                    else:
                        fillk(KDs[j], 2 * T, [(0, 1024), (3072, 4096)], ("KD", j))
                        fillv(lambda n, j=j: VDs[j][n * 128:(n + 1) * 128, :], 32, ("VD", j))
                ohs = sb([128, 32, 2, 128], BF16, s1)
                bnd = sb([128, 2, 128], F32, s1)
                rb = sb([128, 192], F32, s1)
                acc = sb([128, 2, 128], F32, s1)
                P.dma("sp", bnd[:], band_in, wr=["bnd"])
                P.dma("sp", rb[:], relb_in, wr=["rb"])
                for g in range(3):
                    P.dma("sp", ohs[:], oh_in[g], wr=["ohs"])
                    for hh in range(2):
                        gh = g * 2 + hh
                        P.op("dve", lambda e: e.tensor_copy(out=acc[:], in_=bnd[:]), rd=["bnd"], wr=["bacc"])
                        for b in range(32):
                            P.op("dve", lambda e, b=b, gh=gh: e.scalar_tensor_tensor(out=acc[:], in0=ohs[:, b, :, :], scalar=rb[:, b * 6 + gh: b * 6 + gh + 1], in1=acc[:], op0=ALU.mult, op1=ALU.add),
                                 rd=["ohs", "rb", "bacc"], wr=["bacc"])
                        P.op("act", lambda e, gh=gh: e.activation(out=biasT[:, gh, :, :], in_=acc[:], func=AF.Copy), rd=["bacc"], wr=["biasT"])
            P.barrier()

        def dyn_copy():
            pid = PID[0]
            for (c_a, c_b, so, s_a, s_b) in ((0, 1024, 0, 1024, 2048), (1024, 3072, 1, 0, 2048), (3072, 4096, 2, 0, 1024)):
                P.dma("pool", KDs[0][:, c_a:c_b], AGB[bass.ds(pid * R_TOT + (so * R_TOT + R_KD), 390), s_a:s_b], rd=["AGB"], wr=[("KD", 0)])
            vflat = VDs[0][0:2 * T, :].rearrange("t c -> (t c)").rearrange("(r w) -> r w", w=T)
            for (r_a, r_b, so, off) in ((0, 195, 0, 195), (195, 585, 1, 0), (585, 780, 2, 0)):
                P.dma("pool", vflat[r_a:r_b, :], AGB[bass.ds(pid * R_TOT + (so * R_TOT + R_VD + off), r_b - r_a), :], rd=["AGB"], wr=[("VD", 0)])

        def do_dil(l, j):
            with ExitStack() as s1:
                oacc = sb([128, 16, 65], F32, s1)
                mq = sb([128, 4], F32, s1)
                pT = [sb([128, 256], BF16, s1) for _ in range(3)]
                it = 0
                for g in range(3):
                    dil = (1, 4, 16)[g]
                    nb = T // dil // 128
                    Pd = 1024 // dil
                    for hh in range(2):
                        gh = g * 2 + hh
                        with ExitStack() as s2:
                            qT = sb([128, T], BF16, s2); kT = sb([128, 2 * T], BF16, s2)
                            vv = sb([128, dil * (nb + 1), 65], BF16, s2)
                            P.dma("sp", qT[0:65, :], QD[j][gh * 65:(gh + 1) * 65, :], rd=[("QD", j)], wr=["dqT"])
                            P.dma("sp", kT[0:65, :], KDs[j][gh * 65:(gh + 1) * 65, :], rd=[("KD", j)], wr=["dkT"])
                            for rho in range(dil):
                                base = rho + dil * (Pd - 64)
                                src = VDs[j][base: base + dil * 128 * (nb + 1), gh * 65:(gh + 1) * 65].rearrange("(m i d) c -> i m d c", i=128, d=dil)[:, :, 0, :]
                                P.dma("sp", vv[:, rho * (nb + 1):(rho + 1) * (nb + 1), :], src, rd=[("VD", j)], wr=["dvv"])
                            sumsq_max(s2, lambda t: qT[0:64, t * 512:(t + 1) * 512], 64, 4, mq[:, 0:1], ["dqT"], "dq")
                            sumsq_max(s2, lambda t: kT[0:64, t * 512:(t + 1) * 512], 64, 8, mq[:, 1:2], ["dkT"], "dk")
                            P.op("dve", lambda e: e.tensor_tensor(out=mq[:, 2:3], in0=mq[:, 0:1], in1=mq[:, 1:2], op=ALU.mult), rd=["dqdst", "dkdst"], wr=["dm2"])
                            P.op("act", lambda e: e.activation(out=mq[:, 2:3], in_=mq[:, 2:3], func=AF.Sqrt), rd=["dm2"], wr=["dm2"])
                            P.op("dve", lambda e, gh=gh: e.tensor_scalar(out=negM[:, gh:gh + 1], in0=mq[:, 2:3], scalar1=-1.0, scalar2=1.0, op0=ALU.mult, op1=ALU.mult), rd=["dm2"], wr=["negM"])
                            blocks = [(rho, b) for rho in range(dil) for b in range(nb)]
                            sbank = {}
                            def emit_ms(i, dil=dil, Pd=Pd, gh=gh):
                                rho, b = blocks[i]
                                pS = nps(); sbank[i] = pS
                                def ms(e, pS=pS, rho=rho, b=b):
                                    ins = None
                                    q0 = rho + dil * 128 * b
                                    for mi in range(2):
                                        k0 = rho + dil * (Pd - 64 + 128 * (b + mi))
                                        e.matmul(ps[pS][:, mi * 128:(mi + 1) * 128], lhsT=kT[0:65, k0:k0 + 127 * dil + 1:dil], rhs=qT[0:65, q0:q0 + 127 * dil + 1:dil], start=True, stop=False)
                                        ins = e.matmul(ps[pS][:, mi * 128:(mi + 1) * 128], lhsT=ident_bf[:], rhs=biasT[:, gh, mi, :], start=False, stop=True)
                                    return ins
                                P.op("pe", ms, rd=["dkT", "dqT", "identbf", "biasT"], wr=[psk[pS]])
                            def emit_rest(i, nb=nb, gh=gh):
                                rho, b = blocks[i]
                                slot = rho * nb + b
                                pS = sbank[i]; pb = i % 3
                                P.op("act", lambda e, pS=pS, pb=pb: e.activation(out=pT[pb][:], in_=ps[pS][:, 0:256], func=AF.Exp, bias=negM[:, gh:gh + 1]), rd=[psk[pS], "negM"], wr=["dpT%d" % pb])
                                pO = nps()
                                def mo(e, pO=pO, pb=pb, rho=rho, b=b):
                                    e.matmul(ps[pO][:, 0:65], lhsT=pT[pb][:, 0:128], rhs=vv[:, rho * (nb + 1) + b, :], start=True, stop=False)
                                    return e.matmul(ps[pO][:, 0:65], lhsT=pT[pb][:, 128:256], rhs=vv[:, rho * (nb + 1) + b + 1, :], start=False, stop=True)
                                P.op("pe", mo, rd=["dpT%d" % pb, "dvv"], wr=[psk[pO]])
                                P.op("dve", lambda e, pO=pO, slot=slot: e.tensor_copy(out=oacc[:, slot, :], in_=ps[pO][:, 0:65]), rd=[psk[pO]], wr=["oacc"])
                            emit_ms(0)
                            if len(blocks) > 1:
                                emit_ms(1)
                            for i in range(len(blocks)):
                                if i + 2 < len(blocks):
                                    emit_ms(i + 2)
                                emit_rest(i)
                            for rho in range(dil):
                                dst = DACC[j][g, rho: rho + T, hh * 65:(hh + 1) * 65].rearrange("(b i d) c -> i b d c", i=128, d=dil)[:, :, 0, :]
                                P.dma("sp", dst, oacc[:, rho * nb:(rho + 1) * nb, :], rd=["oacc"], wr=[("DACC", j)])
                w = sb([128, 8], F32, s1); mn = sb([128, 2], F32, s1)
                for hh in range(2):
                    P.op("dve", lambda e, hh=hh: e.tensor_tensor(out=mn[:, hh:hh + 1], in0=negM[:, hh:hh + 1], in1=negM[:, 2 + hh:3 + hh], op=ALU.min), rd=["negM"], wr=["mn"])
                    P.op("dve", lambda e, hh=hh: e.tensor_tensor(out=mn[:, hh:hh + 1], in0=mn[:, hh:hh + 1], in1=negM[:, 4 + hh:5 + hh], op=ALU.min), rd=["negM", "mn"], wr=["mn"])
                    for g in range(3):
                        P.op("act", lambda e, hh=hh, g=g: e.activation(out=w[:, g * 2 + hh:g * 2 + hh + 1], in_=negM[:, g * 2 + hh:g * 2 + hh + 1], func=AF.Exp, scale=-1.0, bias=mn[:, hh:hh + 1]),
                             rd=["negM", "mn"], wr=["dw"])
                da = [sb([128, 3, 130], F32, s1) for _ in range(2)]
                ss = [sb([128, 130], F32, s1) for _ in range(2)]
                oo = [sb([128, 128], F32, s1) for _ in range(2)]
                ob = [sb([128, 128], BF16, s1) for _ in range(2)]
                rc = [sb([128, 2], F32, s1) for _ in range(2)]
                for tb in range(T // 128):
                    b = tb % 2
                    P.dma("sp", da[b][:], DACC[j][:, tb * 128:(tb + 1) * 128, :].rearrange("g t c -> t g c"), rd=[("DACC", j)], wr=["da%d" % b])
                    for g in range(3):
                        for hh in range(2):
                            P.op("dve", lambda e, b=b, g=g, hh=hh: e.tensor_scalar(out=da[b][:, g, hh * 65:(hh + 1) * 65], in0=da[b][:, g, hh * 65:(hh + 1) * 65],
                                                                                 scalar1=w[:, g * 2 + hh:g * 2 + hh + 1], scalar2=1.0, op0=ALU.mult, op1=ALU.mult), rd=["da%d" % b, "dw"], wr=["da%d" % b])
                    P.op("pool", lambda e, b=b: e.tensor_tensor(out=ss[b][:], in0=da[b][:, 0, :], in1=da[b][:, 1, :], op=ALU.add), rd=["da%d" % b], wr=["ss%d" % b])
                    P.op("pool", lambda e, b=b: e.tensor_tensor(out=ss[b][:], in0=ss[b][:], in1=da[b][:, 2, :], op=ALU.add), rd=["da%d" % b, "ss%d" % b], wr=["ss%d" % b])
                    for hh in range(2):
                        P.op("dve", lambda e, b=b, hh=hh: e.reciprocal(out=rc[b][:, hh:hh + 1], in_=ss[b][:, hh * 65 + 64:hh * 65 + 65]), rd=["ss%d" % b], wr=["drc%d" % b])
                        P.op("dve", lambda e, b=b, hh=hh: e.tensor_scalar(out=oo[b][:, hh * 64:(hh + 1) * 64], in0=ss[b][:, hh * 65:hh * 65 + 64], scalar1=rc[b][:, hh:hh + 1], scalar2=1.0, op0=ALU.mult, op1=ALU.mult),
                             rd=["ss%d" % b, "drc%d" % b], wr=["oo%d" % b])
                    pi = nps()
                    P.op("pe", lambda e, pi=pi, b=b: e.transpose(out=ps[pi][:, 0:128], in_=oo[b][:], identity=ident[:]), rd=["oo%d" % b, "ident"], wr=[psk[pi]])
                    P.op("act", lambda e, pi=pi, b=b: e.activation(out=ob[b][:], in_=ps[pi][:, 0:128], func=AF.Copy), rd=[psk[pi]], wr=["dob%d" % b])
                    P.dma("sp", MT[j][768:896, tb * 128:(tb + 1) * 128], ob[b][:], rd=["dob%d" % b], wr=[("MT", j)])
            P.barrier()

        def do_fnet(l, j):
            S_tiles = 128 if j == 0 else 16
            tab = dftP if j == 0 else dftS

            def xsrc(nt):
                if j == 0:
                    s_, r_ = divmod(nt, 16)
                    base = (1 + s_) * R_TOT + R_ZA
                    v = AGB[base:base + 768, :].rearrange("r c -> (r c)").rearrange("(t c) -> t c", c=768)
                    return v[r_ * 128:(r_ + 1) * 128, :]
                v = ZA[j].rearrange("r c -> (r c)").rearrange("(t c) -> t c", c=768)
                return v[nt * 128:(nt + 1) * 128, :]
            xkey = "AGB" if j == 0 else ("ZA", j)
            variants = [(0, 128, 0), (0, 64, 128), (64, 64, 0), (0, 128, 64)]
            with ExitStack() as s1:
                chf = sb([128, 4, 2, 192], F32, s1); cht = sb([128, 4, 2, 192], BF16, s1)
                P.op("pool", lambda e: e.memset(chf[:], 0.0), wr=["chf"])
                for v, (p0, n, ch0) in enumerate(variants):
                    for cs_ in range(2):
                        P.dma("sp", chf[p0:p0 + n, v, cs_, :], chdft_in[cs_, ch0:ch0 + n, :], wr=["chf"])
                P.op("pool", lambda e: e.tensor_copy(out=cht[:], in_=chf[:]), rd=["chf"], wr=["cht"])
                uv = sb([128, 2, 6, 512], BF16, s1)
                xt = [sb([128, 384], BF16, s1) for _ in range(3)]
                ct = [sb([128, 512], BF16, s1) for _ in range(3)]
                st_ = [sb([128, 512], BF16, s1) for _ in range(3)]
                fo = [sb([128, 512], BF16, s1) for _ in range(2)]
                for kt in range(4):
                    for cg in range(2):
                        for nt in range(S_tiles):
                            b = nt % 3
                            P.dma("sp", xt[b][:], xsrc(nt)[:, cg * 384:(cg + 1) * 384], rd=[xkey], wr=["fx%d" % b])
                            P.dma("sp", ct[b][:], tab[0, nt * 128:(nt + 1) * 128, kt * 512:(kt + 1) * 512], wr=["fc%d" % b])
                            P.dma("sp", st_[b][:], tab[1, nt * 128:(nt + 1) * 128, kt * 512:(kt + 1) * 512], wr=["fs%d" % b])
                            def mm(e, b=b, nt=nt):
                                ins = None
                                for cc in range(3):
                                    e.matmul(ps[cc][:, :], lhsT=xt[b][:, cc * 128:(cc + 1) * 128], rhs=ct[b][:], start=(nt == 0), stop=(nt == S_tiles - 1))
                                    ins = e.matmul(ps[3 + cc][:, :], lhsT=xt[b][:, cc * 128:(cc + 1) * 128], rhs=st_[b][:], start=(nt == 0), stop=(nt == S_tiles - 1))
                                return ins
                            P.op("pe", mm, rd=["fx%d" % b, "fc%d" % b, "fs%d" % b], wr=[psk[i] for i in range(6)])
                        for cc in range(3):
                            P.op("act", lambda e, cc=cc, cg=cg: e.activation(out=uv[:, 0, cg * 3 + cc, :], in_=ps[cc][:, :], func=AF.Copy), rd=[psk[cc]], wr=["uv"])
                            P.op("dve", lambda e, cc=cc, cg=cg: e.tensor_copy(out=uv[:, 1, cg * 3 + cc, :], in_=ps[3 + cc][:, :]), rd=[psk[3 + cc]], wr=["uv"])
                    for oc in range(6):
                        pb = 6 + oc % 2
                        def chm(e, oc=oc, pb=pb):
                            ins = None
                            for gi in range(4):
                                c0 = 192 * gi
                                cA, offA = c0 // 128, c0 % 128
                                if offA == 0:
                                    pieces = [(cA, 0, 128, 0, 0), (cA + 1, 0, 64, 128, 1)]
                                else:
                                    pieces = [(cA, 64, 64, 0, 2), (cA + 1, 0, 128, 64, 3)]
                                for (mchunk, mp0, mn_, mch0, _) in pieces:
                                    if mchunk != oc:
                                        continue
                                    first = True
                                    for pi_, (kchunk, kp0, kn, kch0, kv) in enumerate(pieces):
                                        for cs_ in range(2):
                                            last = (pi_ == 1 and cs_ == 1)
                                            ins = e.matmul(ps[pb][mp0:mp0 + mn_, :], lhsT=cht[kp0:kp0 + kn, kv, cs_, mch0:mch0 + mn_], rhs=uv[kp0:kp0 + kn, cs_, kchunk, :], start=first, stop=last)
                                            first = False
                            return ins
                        P.op("pe", chm, rd=["cht", "uv"], wr=[psk[pb]])
                        fb = oc % 2
                        P.op("act", lambda e, pb=pb, fb=fb: e.activation(out=fo[fb][:], in_=ps[pb][:, :], func=AF.Copy), rd=[psk[pb]], wr=["fo%d" % fb])
                        P.dma("sp", MT[j][oc * 128:(oc + 1) * 128, kt * 512:(kt + 1) * 512], fo[fb][:], rd=["fo%d" % fb], wr=[("MT", j)])
            P.barrier()

        if flags['dil']:
            setup_dil()
        convert_weights(0)
        for l in range(depth):
            for j in jobs:
                with ExitStack() as s1:
                    hT = sb([128, KD, T], BF16, s1)
                    xt = [sb([128, KD, 512], F32, s1) for _ in range(2)]
                    sq = sb([128, KD, 512], BF16, s1)
                    rs = sb([128, 512], F32, s1)
                    stg = [sb([128, 512], BF16, s1) for _ in range(4)]
                    sgi = [0]
                    for tt in range(NTT):
                        b = tt % 2
                        P.dma("sp", xt[b][:], xT[j, :, tt * 512:(tt + 1) * 512].rearrange("(k p) t -> p k t", p=128), rd=[("xT", j)], wr=["xt%d" % b])
                        rms_tile(s1, xt[b], "xt%d" % b, KD, lambda k: a1[:, l, k, j:j + 1], lambda k: mods[:, l, k, j:j + 1],
                                 hT, "hT", tt * 512, 512, 1.0 / D, (sq, rs), "A")

                    def store(pi, nrow, ncol, dst, func=AF.Copy, scale=1.0, wrk=None):
                        si = sgi[0]; sgi[0] = (si + 1) % 4
                        P.op("act", lambda e: e.activation(out=stg[si][0:nrow, 0:ncol], in_=ps[pi][0:nrow, 0:ncol], func=func, scale=scale),
                             rd=[psk[pi]], wr=["stg%d" % si])
                        P.dma("act", dst, stg[si][0:nrow, 0:ncol], rd=["stg%d" % si], wr=[wrk])

                    pieces = [(0, 384), (384, 384), (768, 384), (1152, 384), (1536, 384), (1920, 384), (2304, 384), (2688, 384), (3072, 384)]
                    pieces += [(C_ZG + i * 512, 512) for i in range(8)]
                    kch = [(k * 128, 128) for k in range(KD)]
                    nxt = wload("w_in", l, kch, *pieces[0])
                    for pi_, (c0, ncols) in enumerate(pieces):
                        wt, wk = nxt
                        if pi_ + 1 < len(pieces):
                            nxt = wload("w_in", l, kch, *pieces[pi_ + 1])
                        if c0 < C_BQ:
                            for tb in range(T // 128):
                                pi = nps()
                                def mm(e, pi=pi, tb=tb, wt=wt, ncols=ncols):
                                    ins = None
                                    for k in range(KD):
                                        ins = e.matmul(ps[pi][:, 0:ncols], lhsT=hT[:, k, tb * 128:(tb + 1) * 128], rhs=wt[:, k, 0:ncols], start=(k == 0), stop=(k == KD - 1))
                                    return ins
                                P.op("pe", mm, rd=[wk, "hT"], wr=[psk[pi]])
                                if c0 < C_BV:
                                    zdst = (PK[R_ZA:R_ZA + 768, :] if j == 0 else ZA[j]).rearrange("r c -> (r c)").rearrange("(t c) -> t c", c=768)
                                    store(pi, 128, 384, zdst[tb * 128:(tb + 1) * 128, c0:c0 + 384], wrk=("ZA", j))
                                elif c0 < C_CV:
                                    if j == 0:
                                        vdst = PK[R_VD:R_VD + 390, :].rearrange("r c -> (r c)").rearrange("(t g h d) -> t g h d", g=3, h=2, d=65)
                                        vdst = vdst[tb * 128:(tb + 1) * 128, :, :, 0:64]
                                    else:
                                        vdst = VDs[j].rearrange("t (g h d) -> t g h d", g=3, h=2)[T // 2 + tb * 128:T // 2 + (tb + 1) * 128, :, :, 0:64]
                                    si = sgi[0]; sgi[0] = (si + 1) % 4
                                    P.op("act", lambda e, si=si, pi=pi: e.activation(out=stg[si][:, 0:384], in_=ps[pi][:, 0:384], func=AF.Copy), rd=[psk[pi]], wr=["stg%d" % si])
                                    P.dma("act", vdst, stg[si][:, 0:384].rearrange("p (g h d) -> p g h d", g=3, h=2), rd=["stg%d" % si], wr=[("VD", j)])
                                else:
                                    store(pi, 128, 384, CV[j][tb * 128:(tb + 1) * 128, :], wrk=("CV", j))
                        else:
                            if c0 == C_CU:
                                chunks = [(i * 96, 96) for i in range(4)]
                            elif c0 == C_CKV:
                                chunks = [(0, 128), (128, 128), (256, 64), (320, 64)]
                            else:
                                chunks = [(i * 128, 128) for i in range(ncols // 128)]
                            for ci, (co, M) in enumerate(chunks):
                                for tt in range(NTT):
                                    pi = nps()
                                    def mm(e, pi=pi, tt=tt, wt=wt, co=co, M=M):
                                        ins = None
                                        for k in range(KD):
                                            ins = e.matmul(ps[pi][0:M, :], lhsT=wt[:, k, co:co + M], rhs=hT[:, k, tt * 512:(tt + 1) * 512], start=(k == 0), stop=(k == KD - 1))
                                        return ins
                                    P.op("pe", mm, rd=[wk, "hT"], wr=[psk[pi]])
                                    tsl = slice(tt * 512, (tt + 1) * 512)
                                    if c0 == C_BQ:
                                        dst = QD[j].rearrange("(g h d) t -> g h d t", g=3, h=2)[ci, :, 0:64, tsl]
                                        si = sgi[0]; sgi[0] = (si + 1) % 4
                                        P.op("act", lambda e, si=si, pi=pi: e.activation(out=stg[si][:, :], in_=ps[pi][:, :], func=AF.Copy, scale=0.125), rd=[psk[pi]], wr=["stg%d" % si])
                                        for hh in range(2):
                                            P.dma("act", dst[hh], stg[si][hh * 64:(hh + 1) * 64, :], rd=["stg%d" % si], wr=[("QD", j)])
                                    elif c0 == C_BK:
                                        if j == 0:
                                            dst = PK[R_KD:R_KD + 390, :].rearrange("(g h d) t -> g h d t", g=3, h=2)[ci, :, 0:64, tsl]
                                        else:
                                            dst = KDs[j].rearrange("(g h d) t -> g h d t", g=3, h=2)[ci, :, 0:64, T // 2 + tt * 512:T // 2 + (tt + 1) * 512]
                                        si = sgi[0]; sgi[0] = (si + 1) % 4
                                        P.op("act", lambda e, si=si, pi=pi: e.activation(out=stg[si][:, :], in_=ps[pi][:, :], func=AF.Copy), rd=[psk[pi]], wr=["stg%d" % si])
                                        for hh in range(2):
                                            P.dma("act", dst[hh], stg[si][hh * 64:(hh + 1) * 64, :], rd=["stg%d" % si], wr=[("KD", j)])
                                    elif c0 == C_CU:
                                        store(pi, 96, 512, CU[j][ci * 96:(ci + 1) * 96, tsl], wrk=("CU", j))
                                    elif c0 == C_CQ:
                                        store(pi, 128, 512, CQ[j][ci * 128:(ci + 1) * 128, tsl], wrk=("CQ", j))
                                    elif c0 == C_CKV:
                                        store(pi, M, 512, CQ[j][384 + co:384 + co + M, tsl], wrk=("CQ", j))
                                    else:
                                        zc = (c0 - C_ZG) // 128 + ci
                                        store(pi, 128, 512, GT[j][zc * 128:(zc + 1) * 128, tsl], func=AF.Sigmoid, wrk=("GT", j))
                P.barrier()

            if l + 1 < depth:
                convert_weights(l + 1)
            if flags["mla"]:
                for j in jobs:
                    do_mla(l, j)
            if 0 in jobs:
                do_ag()
            if 0 in jobs and flags["dil"]:
                dyn_copy()
            for j in jobs:
                if flags["sgu"]:
                    do_sgu(l, j)
            for j in jobs:
                if flags["dil"]:
                    do_dil(l, j)
            for j in jobs:
                if flags["fnet"]:
                    do_fnet(l, j)
            for j in jobs:
                if flags["mla"]:
                    mla_attend(l, j)

            for j in jobs:
                for hf in range(2):
                    t0 = hf * 1024
                    with ExitStack() as s1:
                        xr = sb([128, KD, 1024], F32, s1)
                        mg = sb([128, KD, 1024], BF16, s1)
                        h2 = mg
                        sq = sb([128, KD, 512], BF16, s1)
                        rs = sb([128, 512], F32, s1)
                        xc = sb([128, KD, 512], F32, s1)
                        P.dma("sp", xr[:], xT[j, :, t0:t0 + 1024].rearrange("(k p) t -> p k t", p=128), rd=[("xT", j)], wr=["xr"])
                        any_mix = flags["fnet"] or flags["dil"] or flags["sgu"] or flags["mla"]
                        if any_mix:
                            with ExitStack() as s2:
                                br = sb([128, 13, 1024], BF16, s2)
                                gt = [sb([128, 1024], BF16, s2) for _ in range(2)]
                                tmpf = [sb([128, 1024], F32, s2) for _ in range(2)]
                                acc = sb([128, 1024], F32, s2)
                                brk = [(i * 128, 128) for i in range(6)] + [(768, 128)] + [(896 + i * 96, 96) for i in range(4)] + [(1280, 128), (1408, 128)]
                                for i, (r0, n) in enumerate(brk):
                                    P.dma("sp", br[0:n, i, :], MT[j][r0:r0 + n, t0:t0 + 1024], rd=[("MT", j)], wr=["br"])
                                bsets = [(list(range(0, 6)), flags["fnet"]), ([6], flags["dil"]), (list(range(7, 11)), flags["sgu"]), ([11, 12], flags["mla"])]
                                gi = 0
                                for dc in range(KD):
                                    first = True
                                    for bi_, (ks, en) in enumerate(bsets):
                                        if not en:
                                            continue
                                        wt, wk = wload("p_all", l, [brk[k] for k in ks], dc * 128, 128)
                                        g = gi % 2; gi += 1
                                        P.dma("sp", gt[g][:], GT[j][bi_ * D + dc * 128: bi_ * D + (dc + 1) * 128, t0:t0 + 1024], rd=[("GT", j)], wr=["gt%d" % g])
                                        for th in range(2):
                                            pi = nps()
                                            def mm(e, pi=pi, th=th, wt=wt, ks=ks):
                                                ins = None
                                                for ii, k in enumerate(ks):
                                                    n = brk[k][1]
                                                    ins = e.matmul(ps[pi][:, :], lhsT=wt[0:n, ii, 0:128], rhs=br[0:n, k, th * 512:(th + 1) * 512], start=(ii == 0), stop=(ii == len(ks) - 1))
                                                return ins
                                            P.op("pe", mm, rd=[wk, "br"], wr=[psk[pi]])
                                            tsl = slice(th * 512, (th + 1) * 512)
                                            if first:
                                                P.op("dve", lambda e, pi=pi, g=g, tsl=tsl: e.tensor_tensor(out=acc[:, tsl], in0=ps[pi][:, :], in1=gt[g][:, tsl], op=ALU.mult),
                                                     rd=[psk[pi], "gt%d" % g], wr=["acc%d" % th])
                                            else:
                                                P.op("dve", lambda e, pi=pi, g=g, tsl=tsl, th=th: e.tensor_tensor(out=tmpf[th][:, 0:512], in0=ps[pi][:, :], in1=gt[g][:, tsl], op=ALU.mult),
                                                     rd=[psk[pi], "gt%d" % g], wr=["tmpf%d" % th])
                                                P.op("dve", lambda e, tsl=tsl, th=th: e.tensor_tensor(out=acc[:, tsl], in0=acc[:, tsl], in1=tmpf[th][:, 0:512], op=ALU.add),
                                                     rd=["tmpf%d" % th, "acc%d" % th], wr=["acc%d" % th])
                                        first = False
                                    P.op("act", lambda e, dc=dc: e.activation(out=mg[:, dc, :], in_=acc[:], func=AF.Copy), rd=["acc0", "acc1"], wr=["mg"])
                                P.barrier()
                            for dc2 in range(2):
                                wt, wk = wload("w_o", l, [(k * 128, 128) for k in range(KD)], dc2 * 512, 512)
                                for cc in range(4):
                                    dc = dc2 * 4 + cc
                                    for th in range(2):
                                        pi = nps()
                                        def mm(e, pi=pi, th=th, wt=wt, cc=cc):
                                            ins = None
                                            for k in range(KD):
                                                ins = e.matmul(ps[pi][:, :], lhsT=wt[:, k, cc * 128:(cc + 1) * 128], rhs=mg[:, k, th * 512:(th + 1) * 512], start=(k == 0), stop=(k == KD - 1))
                                            return ins
                                        P.op("pe", mm, rd=[wk, "mg"], wr=[psk[pi]])
                                        P.op("dve", lambda e, pi=pi, dc=dc, th=th: e.scalar_tensor_tensor(
                                            out=xr[:, dc, th * 512:(th + 1) * 512], in0=ps[pi][:, :], scalar=mods[:, l, 16 + dc, j:j + 1], in1=xr[:, dc, th * 512:(th + 1) * 512],
                                            op0=ALU.mult, op1=ALU.add), rd=[psk[pi], "xr"], wr=["xr"])
                        for th in range(2):
                            P.op("pool", lambda e, th=th: e.tensor_copy(out=xc[:], in_=xr[:, :, th * 512:(th + 1) * 512]), rd=["xr"], wr=["xc"])
                            rms_tile(s1, xc, "xc", KD, lambda k: a2[:, l, k, j:j + 1], lambda k: mods[:, l, 24 + k, j:j + 1], h2, "mg", th * 512, 512, 1.0 / D, (sq, rs), "C")
                        aT = sb([128, 32, 1024], BF16, s1)
                        kch = [(k * 128, 128) for k in range(KD)]
                        nxt = wload("w1", l, kch, 0, 512)
                        for pc in range(8):
                            wt, wk = nxt
                            if pc + 1 < 8:
                                nxt = wload("w1", l, kch, (pc + 1) * 512, 512)
                            for cc in range(4):
                                hc = pc * 4 + cc
                                for th in range(2):
                                    pi = nps()
                                    def mm(e, pi=pi, th=th, wt=wt, cc=cc):
                                        ins = None
                                        for k in range(KD):
                                            ins = e.matmul(ps[pi][:, :], lhsT=wt[:, k, cc * 128:(cc + 1) * 128], rhs=h2[:, k, th * 512:(th + 1) * 512], start=(k == 0), stop=(k == KD - 1))
                                        return ins
                                    P.op("pe", mm, rd=[wk, "mg"], wr=[psk[pi]])
                                    P.op("act", lambda e, pi=pi, hc=hc, th=th: e.activation(out=aT[:, hc, th * 512:(th + 1) * 512], in_=ps[pi][:, :], func=AF.Relu),
                                         rd=[psk[pi]], wr=[("aT", hc, th)])
                                    P.op("pool", lambda e, hc=hc, th=th: e.tensor_tensor(out=aT[:, hc, th * 512:(th + 1) * 512], in0=aT[:, hc, th * 512:(th + 1) * 512],
                                                                                       in1=aT[:, hc, th * 512:(th + 1) * 512], op=ALU.mult),
                                         rd=[("aT", hc, th)], wr=[("aT", hc, th)])
                        for dc in range(KD):
                            wts = [wload("w2", l, [((kg * 8 + k) * 128, 128) for k in range(8)], dc * 128, 128) for kg in range(3)]
                            pis = [nps(), nps()]
                            for kg in range(4):
                                if kg == 3:
                                    wt, wk = wload("w2", l, [((kg * 8 + k) * 128, 128) for k in range(8)], dc * 128, 128)
                                else:
                                    wt, wk = wts[kg]
                                for th in range(2):
                                    pi = pis[th]
                                    def mm(e, pi=pi, th=th, wt=wt, kg=kg):
                                        ins = None
                                        for k in range(8):
                                            ins = e.matmul(ps[pi][:, :], lhsT=wt[:, k, 0:128], rhs=aT[:, kg * 8 + k, th * 512:(th + 1) * 512],
                                                           start=(kg == 0 and k == 0), stop=(kg == 3 and k == 7))
                                        return ins
                                    P.op("pe", mm, rd=[wk] + [("aT", kg * 8 + k, th) for k in range(8)], wr=[psk[pi]])
                            for th in range(2):
                                pi = pis[th]
                                P.op("dve", lambda e, pi=pi, dc=dc, th=th: e.scalar_tensor_tensor(
                                    out=xr[:, dc, th * 512:(th + 1) * 512], in0=ps[pi][:, :], scalar=mods[:, l, 40 + dc, j:j + 1], in1=xr[:, dc, th * 512:(th + 1) * 512],
                                    op0=ALU.mult, op1=ALU.add), rd=[psk[pi], "xr"], wr=["xr"])
                        P.dma("sp", xT[j, :, t0:t0 + 1024].rearrange("(k p) t -> p k t", p=128), xr[:], rd=["xr"], wr=[("xT", j)])
                        P.barrier()

        with ExitStack() as s1:
            xt = [sb([128, KD, 512], F32, s1) for _ in range(2)]
            sqf = sb([128, KD, 512], BF16, s1)
            rsf = sb([128, 512], F32, s1)
            yo = [sb([128, D], F32, s1) for _ in range(2)]
            hf32 = sb([128, KD, 512], F32, s1)
            it = 0
            for j in jobs:
                for tt in range(NTT):
                    b = tt % 2
                    P.dma("sp", xt[b][:], xT[j, :, tt * 512:(tt + 1) * 512].rearrange("(k p) t -> p k t", p=128), rd=[("xT", j)], wr=["xt%d" % b])
                    rms_tile(s1, xt[b], "xt%d" % b, KD, lambda k: gfs[:, k:k + 1], lambda k: zero_t[:, 0:1], hf32, "hf32", 0, 512, 1.0 / D, (sqf, rsf), "F")
                    for tb in range(4):
                        ob = it % 2; it += 1
                        for hf in range(2):
                            pi = nps()
                            def tr(e, pi=pi, hf=hf, tb=tb):
                                ins = None
                                for q in range(4):
                                    k = hf * 4 + q
                                    ins = e.transpose(out=ps[pi][:, q * 128:(q + 1) * 128], in_=hf32[:, k, tb * 128:(tb + 1) * 128], identity=ident[:])
                                return ins
                            P.op("pe", tr, rd=["hf32", "ident"], wr=[psk[pi]])
                            if hf == 0:
                                P.op("act", lambda e, pi=pi, ob=ob: e.activation(out=yo[ob][:, 0:512], in_=ps[pi][:, :], func=AF.Copy), rd=[psk[pi]], wr=["yo%d0" % ob])
                            else:
                                P.op("dve", lambda e, pi=pi, ob=ob: e.tensor_copy(out=yo[ob][:, 512:1024], in_=ps[pi][:, :]), rd=[psk[pi]], wr=["yo%d1" % ob])
                        P.dma("sp", y_out[j, tt * 512 + tb * 128: tt * 512 + (tb + 1) * 128, :], yo[ob][:], rd=["yo%d0" % ob, "yo%d1" % ob], wr=["y"])
        P.barrier(["sp"])
    return nc


def prep_shared(inp):
    f = lambda a: np.ascontiguousarray(np.asarray(a, dtype=np.float32))
    sh = {}
    sh["ada_w"] = f(inp["ada_w"])
    sh["ada_bT"] = f(np.asarray(inp["ada_b"]).reshape(DEPTH, 48, 128).transpose(2, 0, 1))
    sh["g1T"] = f(np.asarray(inp["norm1_g"]).reshape(DEPTH, KD, 128).transpose(2, 0, 1))
    sh["g2T"] = f(np.asarray(inp["norm2_g"]).reshape(DEPTH, KD, 128).transpose(2, 0, 1))
    sh["gfT"] = f(np.asarray(inp["final_g"]).reshape(KD, 128).transpose(1, 0))
    w = np.asarray(inp["w_in"])
    za, bq, bk, bv, cu, cv, dcq, dckv, dkr, zg = np.split(w, np.cumsum([768, 384, 384, 384, 384, 384, 384, 320, 32])[:].tolist(), axis=-1)
    dkr_sw = np.concatenate([dkr[..., 16:32], dkr[..., 0:16]], -1)
    sh["w_in"] = f(np.concatenate([za, bv, cv, bq, bk, cu, dcq, dckv, dkr, dkr_sw, zg], -1))
    assert sh["w_in"].shape[-1] == C_END
    qg = np.asarray(inp["mla_q_norm_g"]).reshape(DEPTH, 3, 128).transpose(2, 0, 1)
    sh["qng"] = f(qg)
    kg = np.zeros((DEPTH, 384), np.float32)
    kg[:, :320] = np.asarray(inp["mla_kv_norm_g"])
    sh["kvg"] = f(kg.reshape(DEPTH, 3, 128).transpose(2, 0, 1))
    wq = np.asarray(inp["mla_w_uq"]).reshape(DEPTH, 384, 4, 96)
    sw = np.concatenate([np.zeros_like(wq[..., 0:64]), wq[..., 80:96], wq[..., 64:80]], -1)
    sh["w_uq"] = f(np.concatenate([wq.reshape(DEPTH, 384, 384), sw.reshape(DEPTH, 384, 384)], -1))
    wkv = np.asarray(inp["mla_w_ukv"]).reshape(DEPTH, 320, 4, 128)
    sh["w_ukv"] = f(np.concatenate([wkv[..., 0:64].reshape(DEPTH, 320, 256), wkv[..., 64:128].reshape(DEPTH, 320, 256)], -1))
    sh["lng"] = f(np.broadcast_to(np.asarray(inp["sgu_ln_g"])[None], (128, DEPTH, 384)))
    sh["lnb"] = f(np.broadcast_to(np.asarray(inp["sgu_ln_b"])[None], (128, DEPTH, 384)))
    sh["wsT"] = f(np.asarray(inp["sgu_w"]).transpose(0, 1, 3, 2))
    sh["sgb"] = f(np.broadcast_to(np.asarray(inp["sgu_b"])[None], (128, DEPTH, 4, 128)))
    sh["p_all"] = f(np.concatenate([np.asarray(inp["p_a"]), np.asarray(inp["p_b"]), np.asarray(inp["p_c"]), np.asarray(inp["p_d"])], 1))
    sh["w_o"] = f(inp["w_o"])
    sh["w1"] = f(inp["mlp_w1"])
    sh["w2"] = f(inp["mlp_w2"])
    sh["relb"] = f(np.broadcast_to(np.asarray(inp["rel_bias"]).reshape(1, 192), (128, 192)))
    sh["dftS"] = dft_table(T, 0, T)
    return sh


def kernel(**inp):
    flags = FLAGS
    sh = prep_shared(inp)
    xp = np.asarray(inp["x_prompt"], np.float32)
    xs = np.asarray(inp["x_sample"], np.float32)
    cp = np.asarray(inp["c_prompt"], np.float32)
    cs = np.asarray(inp["c_sample"], np.float32)
    in_maps = []
    for r in range(NCORES):
        m = dict(sh)
        m.update(host_tables(r))
        m["x"] = np.ascontiguousarray(np.stack([xp[0, r * T:(r + 1) * T], xs[2 * r], xs[2 * r + 1]], 0))
        c3 = np.stack([cp[0], cs[2 * r], cs[2 * r + 1]], 0)
        m["cT"] = np.ascontiguousarray(c3.reshape(3, KD, 128).transpose(2, 1, 0))
        m["dftP"] = dft_table(8 * T, r * T, T)
        in_maps.append(m)
    nc = build(flags)
    res = run_bass_kernel_spmd(nc, in_maps, core_ids=list(range(NCORES)))
    yp = np.zeros((1, 8 * T, D), np.float32)
    ys = np.zeros((16, T, D), np.float32)
    for r in range(NCORES):
        y = np.asarray(res.results[r]["y"], np.float32)
        yp[0, r * T:(r + 1) * T] = y[0]
        ys[2 * r] = y[1]
        ys[2 * r + 1] = y[2]
    return (yp, ys)
```

```python
import math
from contextlib import ExitStack
import numpy as np
import ml_dtypes
import concourse.bass as bass
import concourse.mybir as mybir
from concourse.bass_utils import run_bass_kernel_spmd

F32 = mybir.dt.float32
BF16 = mybir.dt.bfloat16
AF = mybir.ActivationFunctionType
ALU = mybir.AluOpType
AX = mybir.AxisListType
NPBF = ml_dtypes.bfloat16

NCORES = 8
D = 1024
KD = 8
T = 2048
NTT = 4
DEPTH = 4
EPS = 1e-6
NEG = -30000.0
FLAGS = {"depth": 4, "fnet": True, "dil": True, "sgu": True, "mla": True, "jobs": (0, 1, 2)}

C_ZA, C_BV, C_CV, C_BQ, C_BK, C_CU, C_CQ, C_CKV, C_KR, C_ZG, C_END = 0, 768, 1152, 1536, 1920, 2304, 2688, 3072, 3392, 3456, 7552
R_ZA, R_KD, R_VD, R_KM, R_VM, R_TOT = 0, 768, 768 + 390, 768 + 780, 768 + 780 + 384, 768 + 780 + 384 + 260


class Res:
    __slots__ = ("w", "r")

    def __init__(self):
        self.w = None
        self.r = {}


class Prog:
    NDMA = 56

    def __init__(self, nc, stack):
        self.nc = nc
        self.eng = {"pe": nc.tensor, "act": nc.scalar, "dve": nc.vector, "pool": nc.gpsimd, "sp": nc.sync}
        self.sem = {k: stack.enter_context(nc.semaphore("sem_" + k)) for k in self.eng}
        self.dsem = [stack.enter_context(nc.semaphore("dsem%d" % i)) for i in range(self.NDMA)]
        self.cnt = {k: 0 for k in self.eng}
        self.dval = [0] * self.NDMA
        self.drr = 0
        self.waited = {k: {} for k in self.eng}
        self._res = {}

    def res(self, key):
        r = self._res.get(key)
        if r is None:
            r = Res()
            self._res[key] = r
        return r

    def _semobj(self, key):
        return self.dsem[key[1]] if isinstance(key, tuple) else self.sem[key]

    def _deps(self, rd, wr):
        deps = {}
        for r in rd:
            r = self.res(r)
            if r.w is not None and deps.get(r.w[0], 0) < r.w[1]:
                deps[r.w[0]] = r.w[1]
        for r in wr:
            r = self.res(r)
            if r.w is not None and deps.get(r.w[0], 0) < r.w[1]:
                deps[r.w[0]] = r.w[1]
            for k, v in r.r.items():
                if deps.get(k, 0) < v:
                    deps[k] = v
        return deps

    def _wait(self, e, deps):
        w = self.waited[e]
        for k, v in deps.items():
            if e == "pe" and k == "pe":
                continue
            if w.get(k, 0) >= v:
                continue
            self.eng[e].wait_ge(self._semobj(k), v)
            w[k] = v

    def _mark(self, rd, wr, key, val):
        for r in rd:
            r = self.res(r)
            if r.r.get(key, 0) < val:
                r.r[key] = val
        for r in wr:
            r = self.res(r)
            r.w = (key, val)
            r.r = {}

    def op(self, e, fn, rd=(), wr=()):
        self._wait(e, self._deps(rd, wr))
        ins = fn(self.eng[e])
        self.cnt[e] += 1
        ins.then_inc(self.sem[e], 1)
        self._mark(rd, wr, e, self.cnt[e])
        return ins

    def _slot(self, q, rd, wr):
        i = self.drr
        self.drr = (self.drr + 1) % self.NDMA
        deps = self._deps(rd, wr)
        if self.dval[i] > 0:
            deps[("d", i)] = max(deps.get(("d", i), 0), self.dval[i])
        self._wait(q, deps)
        return i

    def dma(self, q, out, in_, rd=(), wr=(), **kw):
        i = self._slot(q, rd, wr)
        ins = self.eng[q].dma_start(out=out, in_=in_, **kw)
        self.dval[i] += 16
        ins.then_inc(self.dsem[i], 16)
        self._mark(rd, wr, ("d", i), self.dval[i])
        return ins

    def custom(self, q, fn, rd=(), wr=(), inc=1):
        i = self._slot(q, rd, wr)
        ins = fn(self.eng[q])
        self.dval[i] += inc
        ins.then_inc(self.dsem[i], inc)
        self._mark(rd, wr, ("d", i), self.dval[i])
        return ins

    def barrier(self, engines=None):
        deps = {k: v for k, v in self.cnt.items() if v > 0}
        for i in range(self.NDMA):
            if self.dval[i] > 0:
                deps[("d", i)] = self.dval[i]
        for e in (engines or list(self.eng)):
            d = dict(deps)
            d.pop(e, None)
            self._wait(e, d)


def _t5_bucket_np(rel):
    half, max_exact = 16, 8
    ret = np.where(rel > 0, half, 0)
    n = np.abs(rel)
    large = max_exact + (np.log(np.maximum(n, max_exact).astype(np.float32) / max_exact)
                         / math.log(1024 / max_exact) * (half - max_exact)).astype(np.int32)
    large = np.minimum(large, half - 1)
    return ret + np.where(n < max_exact, n, large)


def host_tables(r):
    tb = {}
    tb["ident"] = np.eye(128, dtype=np.float32)
    half = 16
    inv = 10000.0 ** (-np.arange(half, dtype=np.float64) / half)
    rp = np.zeros((32, 2, 2, T), np.float32)
    for kind in range(2):
        pos = (np.arange(T) + (T * r if kind == 0 else 0)).astype(np.float64)
        ang = pos[None, :] * inv[:, None]
        cs, sn = np.cos(ang), np.sin(ang)
        rp[:, 0, kind] = np.concatenate([cs, cs], 0)
        rp[:, 1, kind] = np.concatenate([-sn, sn], 0)
    tb["rope"] = rp.astype(NPBF)
    kk = np.arange(256)[:, None]
    qq = np.arange(128)[None, :]
    rel = kk - 64 - qq
    oh = np.zeros((3, 128, 32, 2, 128), np.float32)
    for g, dil in enumerate((1, 4, 16)):
        bk = _t5_bucket_np(rel * dil)
        for b in range(32):
            m = (bk == b).astype(np.float32)
            oh[g, :, b, 0] = m[0:128]
            oh[g, :, b, 1] = m[128:256]
    tb["oh"] = oh.astype(NPBF)
    band = np.where(np.abs(rel) <= 64, 0.0, NEG).astype(np.float32)
    tb["band"] = np.stack([band[0:128], band[128:256]], 1).copy()
    j = np.arange(192)
    ang = 2 * np.pi * np.outer(j, j) / 192.0
    ch = np.zeros((2, 192, 192), np.float32)
    ch[0] = np.cos(ang) / math.sqrt(192)
    ch[1] = -np.sin(ang) / math.sqrt(192)
    tb["chdft"] = ch
    return tb


_DFT_CACHE = {}


def dft_table(S, k0, nk):
    key = (S, k0, nk)
    if key not in _DFT_CACHE:
        n = np.arange(S, dtype=np.int64)[:, None]
        k = (k0 + np.arange(nk, dtype=np.int64))[None, :]
        ph = ((n * k) % S).astype(np.float64) * (2 * np.pi / S)
        out = np.empty((2, S, nk), NPBF)
        out[0] = (np.cos(ph) / math.sqrt(S)).astype(NPBF)
        out[1] = (np.sin(ph) / math.sqrt(S)).astype(NPBF)
        _DFT_CACHE[key] = out
    return _DFT_CACHE[key]


def build(flags):
    depth = flags["depth"]
    jobs = flags["jobs"]
    nc = bass.Bass("TRN2", target_bir_lowering=False)

    def din(name, shape, dt=F32):
        return nc.dram_tensor(name, list(shape), dt, kind="ExternalInput").ap()

    x_in = din("x", [3, T, D])
    cT_in = din("cT", [128, KD, 3])
    relb_in = din("relb", [128, 192])
    ada_w = din("ada_w", [DEPTH, D, 6 * D])
    ada_bT = din("ada_bT", [128, DEPTH, 48])
    g1T = din("g1T", [128, DEPTH, KD])
    g2T = din("g2T", [128, DEPTH, KD])
    gfT = din("gfT", [128, KD])
    w_in = din("w_in", [DEPTH, D, C_END])
    qng = din("qng", [128, DEPTH, 3])
    kvg = din("kvg", [128, DEPTH, 3])
    w_uq = din("w_uq", [DEPTH, 384, 768])
    w_ukv = din("w_ukv", [DEPTH, 320, 512])
    lng = din("lng", [128, DEPTH, 384])
    lnb = din("lnb", [128, DEPTH, 384])
    wsT = din("wsT", [DEPTH, 4, 128, 128])
    sgb = din("sgb", [128, DEPTH, 4, 128])
    p_all = din("p_all", [DEPTH, 1536, D])
    w_o = din("w_o", [DEPTH, D, D])
    w1 = din("w1", [DEPTH, D, 4 * D])
    w2 = din("w2", [DEPTH, 4 * D, D])
    ident_in = din("ident", [128, 128])
    rope_in = din("rope", [32, 2, 2, T], BF16)
    oh_in = din("oh", [3, 128, 32, 2, 128], BF16)
    band_in = din("band", [128, 2, 128])
    chdft_in = din("chdft", [2, 192, 192])
    dftS = din("dftS", [2, T, T], BF16)
    dftP = din("dftP", [2, 8 * T, T], BF16)
    y_out = nc.dram_tensor("y", [3, T, D], F32, kind="ExternalOutput").ap()

    def dscr(name, shape, dt=BF16):
        return nc.dram_tensor(name, list(shape), dt).ap()

    xT = dscr("xT", [3, D, T], F32)
    W16 = {"w_in": dscr("w16_in", [DEPTH, D, C_END]), "p_all": dscr("w16_p", [DEPTH, 1536, D]), "w_o": dscr("w16_o", [DEPTH, D, D]),
           "w1": dscr("w16_1", [DEPTH, D, 4 * D]), "w2": dscr("w16_2", [DEPTH, 4 * D, D])}
    W32 = {"w_in": w_in, "p_all": p_all, "w_o": w_o, "w1": w1, "w2": w2}
    PK = dscr("PK", [R_TOT, T])
    AGB = dscr("AGB", [10 * R_TOT, T])
    ZA = {1: dscr("ZA1", [768, T]), 2: dscr("ZA2", [768, T])}
    QD = [dscr("QD%d" % j, [390, T]) for j in range(3)]
    KDs = [dscr("KD%d" % j, [390, 2 * T]) for j in range(3)]
    VDs = [dscr("VD%d" % j, [2 * T + 16, 390]) for j in range(3)]
    CU = [dscr("CU%d" % j, [384, T]) for j in range(3)]
    CV = [dscr("CV%d" % j, [T, 384]) for j in range(3)]
    CQ = [dscr("CQ%d" % j, [768, T]) for j in range(3)]
    GT = [dscr("GT%d" % j, [4 * D, T]) for j in range(3)]
    MT = [dscr("MT%d" % j, [1536, T]) for j in range(3)]
    KM = {1: dscr("KM1", [384, T]), 2: dscr("KM2", [384, T])}
    VM = {1: dscr("VM1", [T, 260]), 2: dscr("VM2", [T, 260])}
    QM = [dscr("QM%d" % j, [384, T]) for j in range(3)]
    DACC = [dscr("DACC%d" % j, [3, T + 16, 130], F32) for j in range(3)]

    with ExitStack() as st:
        PID = [nc.gpsimd.partition_id()]
        P = Prog(nc, st)
        uid = [0]

        def sb(shape, dt, stack=st, name=None):
            uid[0] += 1
            return stack.enter_context(nc.sbuf_tensor(name or ("t%d" % uid[0]), list(shape), dt))

        ident = sb([128, 128], F32)
        ones_bf = sb([128, 128], BF16)
        eps_t = sb([128, 1], F32)
        zero_t = sb([128, 1], F32)
        mods = sb([128, DEPTH, 48, 3], F32)
        a1 = sb([128, DEPTH, KD, 3], F32)
        a2 = sb([128, DEPTH, KD, 3], F32)
        g1s = sb([128, DEPTH, KD], F32)
        g2s = sb([128, DEPTH, KD], F32)
        gfs = sb([128, KD], F32)
        qngs = sb([128, DEPTH, 3], F32)
        kvgs = sb([128, DEPTH, 3], F32)
        biasT = sb([128, 6, 2, 128], BF16)
        ident_bf = sb([128, 128], BF16)
        cf = [sb([128, 2048], F32) for _ in range(2)]
        wb = [sb([128, 8, 512], BF16) for _ in range(4)]
        cvb = [sb([128, 2048], BF16) for _ in range(2)]
        ps = [st.enter_context(nc.psum_tensor("ps%d" % i, [128, 512], F32)) for i in range(8)]
        psk = ["ps%d" % i for i in range(8)]
        psrr = [0]

        def nps():
            i = psrr[0]
            psrr[0] = (i + 1) % 8
            return i

        P.dma("sp", ident[:], ident_in, wr=["ident"])
        P.op("pool", lambda e: e.memset(ones_bf[:], 1.0), wr=["ones"])
        P.op("pool", lambda e: e.memset(eps_t[:], EPS), wr=["eps"])
        P.op("pool", lambda e: e.memset(zero_t[:], 0.0), wr=["zero"])
        P.op("pool", lambda e: e.tensor_copy(out=ident_bf[:], in_=ident[:]), rd=["ident"], wr=["identbf"])
        P.dma("sp", g1s[:], g1T, wr=["g1s"])
        P.dma("sp", g2s[:], g2T, wr=["g2s"])
        P.dma("sp", gfs[:], gfT, wr=["gfs"])
        P.dma("sp", qngs[:], qng, wr=["qngs"])
        P.dma("sp", kvgs[:], kvg, wr=["kvgs"])

        wst = {"f": 0, "b": 0}

        def wload(wname, l, kch, c0, ncols):
            src = W16[wname][l]
            bi = wst["b"]; wst["b"] = (bi + 1) % 4
            bk = "wb%d" % bi
            full = all(n == 128 for _, n in kch) and all(kch[i][0] == kch[0][0] + 128 * i for i in range(len(kch)))
            if full:
                r0 = kch[0][0]
                P.dma("sp", wb[bi][:, 0:len(kch), 0:ncols],
                      src[r0:r0 + 128 * len(kch), c0:c0 + ncols].rearrange("(k p) c -> p k c", p=128), rd=[("W16", wname, l)], wr=[bk])
            else:
                for i, (r0, n) in enumerate(kch):
                    P.dma("sp", wb[bi][0:n, i, 0:ncols], src[r0:r0 + n, c0:c0 + ncols], rd=[("W16", wname, l)], wr=[bk])
            return wb[bi], bk

        cvi = [0]

        def convert_weights(l):
            for wname in ("w_in", "p_all", "w_o", "w1", "w2"):
                src, dst = W32[wname][l], W16[wname][l]
                nr, ncol = src.shape[0], src.shape[1]
                npc = (ncol + 2047) // 2048
                cw = (ncol + npc - 1) // npc
                for r0 in range(0, nr, 128):
                    for c0 in range(0, ncol, cw):
                        n = min(cw, ncol - c0)
                        i = cvi[0]; cvi[0] += 1
                        fi = i % 2
                        fk, ck = "cf%d" % fi, "cvb%d" % fi
                        fv = cf[fi]
                        P.dma("sp", fv[:, 0:n], src[r0:r0 + 128, c0:c0 + n], wr=[fk])
                        if i % 2 == 0:
                            P.op("dve", lambda e, fi=fi, n=n, fv=fv: e.tensor_copy(out=cvb[fi][:, 0:n], in_=fv[:, 0:n]), rd=[fk], wr=[ck])
                        else:
                            P.op("act", lambda e, fi=fi, n=n, fv=fv: e.activation(out=cvb[fi][:, 0:n], in_=fv[:, 0:n], func=AF.Copy), rd=[fk], wr=[ck])
                        P.dma("sp", dst[r0:r0 + 128, c0:c0 + n], cvb[fi][:, 0:n], rd=[ck], wr=[("W16", wname, l)])

        with ExitStack() as s1:
            wf = [sb([128, 8, 512], F32, s1) for _ in range(2)]
            cT = sb([128, KD, 3], F32, s1)
            sc = sb([128, KD, 3], F32, s1)
            abT = sb([128, DEPTH, 48], F32, s1)
            P.dma("sp", cT[:], cT_in, wr=["cT"])
            P.dma("sp", abT[:], ada_bT, wr=["abT"])
            P.op("act", lambda e: e.activation(out=sc[:], in_=cT[:], func=AF.Silu), rd=["cT"], wr=["sc"])
            for l in range(depth):
                for pc in range(12):
                    fi = wst["f"]; wst["f"] = (fi + 1) % 2
                    fk = "wf%d" % fi
                    P.dma("sp", wf[fi][:], ada_w[l, :, pc * 512:(pc + 1) * 512].rearrange("(k p) c -> p k c", p=128), wr=[fk])
                    pi = nps()
                    def mm(e, fi=fi, pi=pi):
                        ins = None
                        for cc in range(4):
                            for k in range(KD):
                                ins = e.matmul(ps[pi][:, cc * 3:cc * 3 + 3], lhsT=wf[fi][:, k, cc * 128:(cc + 1) * 128], rhs=sc[:, k, :],
                                               start=(k == 0), stop=(k == KD - 1))
                        return ins
                    P.op("pe", mm, rd=[fk, "sc"], wr=[psk[pi]])
                    P.op("dve", lambda e, pi=pi, pc=pc, l=l: e.tensor_copy(
                        out=mods[:, l, pc * 4:(pc + 1) * 4, :], in_=ps[pi][:, 0:12].rearrange("p (c j) -> p c j", j=3)),
                        rd=[psk[pi]], wr=["mods"])
                for j in range(3):
                    P.op("dve", lambda e, l=l, j=j: e.tensor_tensor(out=mods[:, l, :, j], in0=mods[:, l, :, j], in1=abT[:, l, :], op=ALU.add),
                         rd=["mods", "abT"], wr=["mods"])
                    P.op("dve", lambda e, l=l, j=j: e.scalar_tensor_tensor(out=a1[:, l, :, j], in0=mods[:, l, 8:16, j], scalar=1.0, in1=g1s[:, l, :],
                                                                          op0=ALU.add, op1=ALU.mult), rd=["mods", "g1s"], wr=["a1"])
                    P.op("dve", lambda e, l=l, j=j: e.scalar_tensor_tensor(out=a2[:, l, :, j], in0=mods[:, l, 32:40, j], scalar=1.0, in1=g2s[:, l, :],
                                                                          op0=ALU.add, op1=ALU.mult), rd=["mods", "g2s"], wr=["a2"])
            P.barrier()

        with ExitStack() as s1:
            xin = [sb([128, D], F32, s1) for _ in range(2)]
            xo = [sb([128, KD, 128], F32, s1) for _ in range(2)]
            it = 0
            for j in jobs:
                for tb in range(T // 128):
                    b = it % 2; it += 1
                    P.dma("sp", xin[b][:], x_in[j, tb * 128:(tb + 1) * 128, :], wr=["xin%d" % b])
                    for hf in range(2):
                        pi = nps()
                        def tr(e, b=b, hf=hf, pi=pi):
                            ins = None
                            for q in range(4):
                                k = hf * 4 + q
                                ins = e.transpose(out=ps[pi][:, q * 128:(q + 1) * 128], in_=xin[b][:, k * 128:(k + 1) * 128], identity=ident[:])
                            return ins
                        P.op("pe", tr, rd=["xin%d" % b, "ident"], wr=[psk[pi]])
                        eng = "act" if hf == 0 else "dve"
                        if eng == "act":
                            P.op("act", lambda e, b=b, hf=hf, pi=pi: e.activation(out=xo[b][:, hf * 4:hf * 4 + 4, :], in_=ps[pi][:].rearrange("p (k t) -> p k t", t=128), func=AF.Copy),
                                 rd=[psk[pi]], wr=["xo%d%d" % (b, hf)])
                        else:
                            P.op("dve", lambda e, b=b, hf=hf, pi=pi: e.tensor_copy(out=xo[b][:, hf * 4:hf * 4 + 4, :], in_=ps[pi][:].rearrange("p (k t) -> p k t", t=128)),
                                 rd=[psk[pi]], wr=["xo%d%d" % (b, hf)])
                    P.dma("sp", xT[j, :, tb * 128:(tb + 1) * 128].rearrange("(k p) t -> p k t", p=128), xo[b][:],
                          rd=["xo%d0" % b, "xo%d1" % b], wr=[("xT", j)])
            P.barrier()

        def rms_tile(s1, src_tile, src_key, nk, a_ap_fn, b_ap_fn, dst, dst_key, dst_off, ncol, inv_n, tmp, tmpk):
            sq, rs = tmp
            P.op("act", lambda e: e.activation(out=sq[:, 0:nk, 0:ncol], in_=src_tile[:, 0:nk, 0:ncol], func=AF.Square), rd=[src_key], wr=[tmpk + "sq"])
            pi = nps()
            def mm(e):
                ins = None
                for k in range(nk):
                    ins = e.matmul(ps[pi][:, 0:ncol], lhsT=ones_bf[:], rhs=sq[:, k, 0:ncol], start=(k == 0), stop=(k == nk - 1))
                return ins
            P.op("pe", mm, rd=[tmpk + "sq", "ones"], wr=[psk[pi]])
            P.op("act", lambda e: e.activation(out=rs[:, 0:ncol], in_=ps[pi][:, 0:ncol], func=AF.Sqrt, scale=inv_n, bias=eps_t[:, 0:1]), rd=[psk[pi], "eps"], wr=[tmpk + "rs"])
            P.op("dve", lambda e: e.reciprocal(out=rs[:, 0:ncol], in_=rs[:, 0:ncol]), rd=[tmpk + "rs"], wr=[tmpk + "rs"])
            for k in range(nk):
                P.op("dve", lambda e, k=k: e.tensor_tensor(out=src_tile[:, k, 0:ncol], in0=src_tile[:, k, 0:ncol], in1=rs[:, 0:ncol], op=ALU.mult),
                     rd=[src_key, tmpk + "rs"], wr=[src_key])
                P.op("act", lambda e, k=k: e.activation(out=dst[:, k, dst_off:dst_off + ncol], in_=src_tile[:, k, 0:ncol], func=AF.Identity,
                                                        scale=a_ap_fn(k), bias=b_ap_fn(k)), rd=[src_key], wr=[dst_key])

        negM = sb([128, 8], F32)

        def sumsq_max(s1, src_fn, K, ntiles, dst_ap, rdkeys, tag, tmp=None):
            if tmp is None:
                sqb = sb([128, 512], BF16, s1)
                mx = sb([128, 32], F32, s1)
            else:
                sqb, mx = tmp
            for t in range(ntiles):
                P.op("act", lambda e, t=t: e.activation(out=sqb[0:K, :], in_=src_fn(t), func=AF.Square), rd=rdkeys, wr=[tag + "sqb"])
                pi = nps()
                P.op("pe", lambda e, pi=pi: e.matmul(ps[pi][:, :], lhsT=ones_bf[0:K, :], rhs=sqb[0:K, :], start=True, stop=True), rd=[tag + "sqb", "ones"], wr=[psk[pi]])
                P.op("dve", lambda e, pi=pi, t=t: e.tensor_reduce(out=mx[:, t:t + 1], in_=ps[pi][:, :], axis=AX.X, op=ALU.max), rd=[psk[pi]], wr=[tag + "mx"])
            P.op("dve", lambda e: e.tensor_reduce(out=dst_ap, in_=mx[:, 0:ntiles], axis=AX.X, op=ALU.max), rd=[tag + "mx"], wr=[tag + "dst"])

        def do_sgu(l, j):
            with ExitStack() as s1:
                lg = sb([128, 384], F32, s1); lb = sb([128, 384], F32, s1)
                wsf = sb([128, 4, 128], F32, s1); wsb = sb([128, 4, 128], BF16, s1)
                sbt = sb([128, 4, 128], F32, s1)
                P.dma("sp", lg[:], lng[:, l, :], wr=["lg"]); P.dma("sp", lb[:], lnb[:, l, :], wr=["lb"])
                P.dma("sp", wsf[:], wsT[l].rearrange("h q p -> q h p"), wr=["wsf"])
                P.dma("sp", sbt[:], sgb[:, l, :, :], wr=["sbt"])
                P.op("pool", lambda e: e.tensor_copy(out=wsb[:], in_=wsf[:]), rd=["wsf"], wr=["wsb"])
                cvt = [sb([128, 384], BF16, s1) for _ in range(2)]
                cut = [sb([128, 4, 128], BF16, s1) for _ in range(2)]
                vn = [sb([128, 384], F32, s1) for _ in range(2)]
                vnb = [sb([128, 384], BF16, s1) for _ in range(2)]
                stt = [sb([128, 8], F32, s1) for _ in range(2)]
                mxf = [sb([128, 4, 128], F32, s1) for _ in range(2)]
                ob = [sb([128, 4, 128], BF16, s1) for _ in range(2)]
                for n in range(T // 128):
                    b = n % 2
                    tsl = slice(n * 128, (n + 1) * 128)
                    P.dma("sp", cvt[b][:], CV[j][tsl, :], rd=[("CV", j)], wr=["cvt%d" % b])
                    P.dma("sp", cut[b][0:96], CU[j][:, tsl].rearrange("(h c) t -> c h t", h=4), rd=[("CU", j)], wr=["cut%d" % b])
                    P.op("dve", lambda e, b=b: e.bn_stats(out=stt[b][:, 0:6], in_=cvt[b][:]), rd=["cvt%d" % b], wr=["stt%d" % b])
                    P.op("dve", lambda e, b=b: e.bn_aggr(out=stt[b][:, 6:8], in_=stt[b][:, 0:6]), rd=["stt%d" % b], wr=["stt%d" % b])
                    P.op("act", lambda e, b=b: e.activation(out=stt[b][:, 7:8], in_=stt[b][:, 7:8], func=AF.Sqrt, bias=eps_t[:, 0:1]), rd=["stt%d" % b, "eps"], wr=["stt%d" % b])
                    P.op("dve", lambda e, b=b: e.reciprocal(out=stt[b][:, 7:8], in_=stt[b][:, 7:8]), rd=["stt%d" % b], wr=["stt%d" % b])
                    P.op("dve", lambda e, b=b: e.tensor_scalar(out=vn[b][:], in0=cvt[b][:], scalar1=stt[b][:, 6:7], scalar2=stt[b][:, 7:8], op0=ALU.subtract, op1=ALU.mult),
                         rd=["cvt%d" % b, "stt%d" % b], wr=["vn%d" % b])
                    P.op("dve", lambda e, b=b: e.tensor_tensor(out=vn[b][:], in0=vn[b][:], in1=lg[:], op=ALU.mult), rd=["vn%d" % b, "lg"], wr=["vn%d" % b])
                    P.op("dve", lambda e, b=b: e.tensor_tensor(out=vnb[b][:], in0=vn[b][:], in1=lb[:], op=ALU.add), rd=["vn%d" % b, "lb"], wr=["vnb%d" % b])
                    pi = nps()
                    def mm(e, pi=pi, b=b):
                        ins = None
                        for h in range(4):
                            ins = e.matmul(ps[pi][0:96, h * 128:(h + 1) * 128], lhsT=vnb[b][:, h * 96:(h + 1) * 96], rhs=wsb[:, h, :], start=True, stop=True)
                        return ins
                    P.op("pe", mm, rd=["vnb%d" % b, "wsb"], wr=[psk[pi]])
                    P.op("dve", lambda e, pi=pi, b=b: e.tensor_tensor(out=mxf[b][0:96], in0=ps[pi][0:96, :].rearrange("p (h t) -> p h t", h=4), in1=sbt[0:96], op=ALU.add),
                         rd=[psk[pi], "sbt"], wr=["mxf%d" % b])
                    P.op("pool", lambda e, b=b: e.tensor_tensor(out=ob[b][0:96], in0=mxf[b][0:96], in1=cut[b][0:96], op=ALU.mult), rd=["mxf%d" % b, "cut%d" % b], wr=["ob%d" % b])
                    P.dma("sp", MT[j][896:1280, tsl].rearrange("(h c) t -> c h t", h=4), ob[b][0:96], rd=["ob%d" % b], wr=[("MT", j)])
            P.barrier()

        def do_mla(l, j):
            nk_t = 64 if j == 0 else 16
            nkt = 128 if j == 0 else 16
            kind = 0 if j == 0 else 1
            kdst = PK[R_KM:R_KM + 384, :] if j == 0 else KM[j]
            vdst = (PK[R_VM:R_VM + 260, :].rearrange("r c -> (r c)").rearrange("(t c) -> t c", c=260)) if j == 0 else VM[j]
            with ExitStack() as s1:
                with ExitStack() as s2:
                    cq = sb([128, 6, T], BF16, s2)
                    ksw = sb([128, T], BF16, s2)
                    rp = sb([128, 2, T], BF16, s2)
                    wqf = sb([128, 3, 768], F32, s2); wqb = sb([128, 3, 768], BF16, s2)
                    wkf = sb([128, 3, 512], F32, s2); wkb = sb([128, 3, 512], BF16, s2)
                    sqb = sb([128, 6, 512], BF16, s2)
                    rq = sb([128, 512], F32, s2); rk = sb([128, 512], F32, s2)
                    qf = sb([128, 512], F32, s2); t1 = sb([128, 512], F32, s2); t2 = sb([128, 512], F32, s2)
                    qst = sb([128, 4, 512], BF16, s2)
                    kst = sb([128, 4, 512], BF16, s2)
                    kpe = sb([128, 512], BF16, s2)
                    rkt = sb([128, 4], F32, s2)
                    vst = sb([128, 4, 4, 65], BF16, s2)
                    P.dma("sp", cq[:], CQ[j].rearrange("(k p) t -> p k t", p=128), rd=[("CQ", j)], wr=["cq"])
                    P.dma("sp", ksw[64:96, :], CQ[j][736:768, :], rd=[("CQ", j)], wr=["ksw"])
                    P.dma("sp", rp[64:96], rope_in[:, :, kind, :], wr=["rp"])
                    P.dma("sp", wqf[:], w_uq[l].rearrange("(k p) c -> p k c", p=128), wr=["wqf"])
                    P.dma("sp", wkf[:, 0:2, :], w_ukv[l, 0:256, :].rearrange("(k p) c -> p k c", p=128), wr=["wkf"])
                    P.dma("sp", wkf[0:64, 2, :], w_ukv[l, 256:320, :], wr=["wkf"])
                    P.op("pool", lambda e: e.memset(vst[:], 1.0), wr=["vst"])
                    for k in range(3):
                        P.op("dve", lambda e, k=k: e.tensor_scalar(out=wqb[:, k, :], in0=wqf[:, k, :], scalar1=qngs[:, l, k:k + 1], scalar2=96.0 ** -0.5, op0=ALU.mult, op1=ALU.mult),
                             rd=["wqf", "qngs"], wr=["wqb"])
                        np_ = 128 if k < 2 else 64
                        P.op("dve", lambda e, k=k, np_=np_: e.tensor_scalar(out=wkb[0:np_, k, :], in0=wkf[0:np_, k, :], scalar1=kvgs[0:np_, l, k:k + 1], scalar2=1.0, op0=ALU.mult, op1=ALU.mult),
                             rd=["wkf", "kvgs"], wr=["wkb"])
                    for tt in range(NTT):
                        tsl = slice(tt * 512, (tt + 1) * 512)
                        P.op("act", lambda e, tsl=tsl: e.activation(out=sqb[:], in_=cq[:, :, tsl], func=AF.Square), rd=["cq"], wr=["sqb"])
                        pq, pk = nps(), nps()
                        def mmq(e, pq=pq):
                            ins = None
                            for k in range(3):
                                ins = e.matmul(ps[pq][:, :], lhsT=ones_bf[:], rhs=sqb[:, k, :], start=(k == 0), stop=(k == 2))
                            return ins
                        P.op("pe", mmq, rd=["sqb", "ones"], wr=[psk[pq]])
                        def mmk(e, pk=pk):
                            e.matmul(ps[pk][:, :], lhsT=ones_bf[:], rhs=sqb[:, 3, :], start=True, stop=False)
                            e.matmul(ps[pk][:, :], lhsT=ones_bf[:], rhs=sqb[:, 4, :], start=False, stop=False)
                            return e.matmul(ps[pk][:, :], lhsT=ones_bf[0:64, :], rhs=sqb[0:64, 5, :], start=False, stop=True)
                        P.op("pe", mmk, rd=["sqb", "ones"], wr=[psk[pk]])
                        P.op("act", lambda e, pq=pq: e.activation(out=rq[:], in_=ps[pq][:, :], func=AF.Sqrt, scale=1.0 / 384, bias=eps_t[:, 0:1]), rd=[psk[pq], "eps"], wr=["rq"])
                        P.op("dve", lambda e: e.reciprocal(out=rq[:], in_=rq[:]), rd=["rq"], wr=["rq"])
                        P.op("act", lambda e, pk=pk: e.activation(out=rk[:], in_=ps[pk][:, :], func=AF.Sqrt, scale=1.0 / 320, bias=eps_t[:, 0:1]), rd=[psk[pk], "eps"], wr=["rk"])
                        P.op("dve", lambda e: e.reciprocal(out=rk[:], in_=rk[:]), rd=["rk"], wr=["rk"])
                        pv = nps()
                        def mmv(e, pv=pv):
                            ins = None
                            for tb in range(4):
                                e.matmul(ps[pv][:, tb:tb + 1], lhsT=sqb[:, 3, tb * 128:(tb + 1) * 128], rhs=ones_bf[:, 0:1], start=True, stop=False)
                                e.matmul(ps[pv][:, tb:tb + 1], lhsT=sqb[:, 4, tb * 128:(tb + 1) * 128], rhs=ones_bf[:, 0:1], start=False, stop=False)
                                ins = e.matmul(ps[pv][:, tb:tb + 1], lhsT=sqb[0:64, 5, tb * 128:(tb + 1) * 128], rhs=ones_bf[0:64, 0:1], start=False, stop=True)
                            return ins
                        P.op("pe", mmv, rd=["sqb", "ones"], wr=[psk[pv]])
                        P.op("act", lambda e, pv=pv: e.activation(out=rkt[:], in_=ps[pv][:, 0:4], func=AF.Sqrt, scale=1.0 / 320, bias=eps_t[:, 0:1]), rd=[psk[pv], "eps"], wr=["rkt"])
                        P.op("dve", lambda e: e.reciprocal(out=rkt[:], in_=rkt[:]), rd=["rkt"], wr=["rkt"])
                        P.op("dve", lambda e, tsl=tsl: e.tensor_tensor(out=t1[64:96], in0=cq[64:96, 5, tsl], in1=rp[64:96, 0, tsl], op=ALU.mult), rd=["cq", "rp"], wr=["t1"])
                        P.op("dve", lambda e, tsl=tsl: e.tensor_tensor(out=t2[64:96], in0=ksw[64:96, tsl], in1=rp[64:96, 1, tsl], op=ALU.mult), rd=["ksw", "rp"], wr=["t2"])
                        P.op("dve", lambda e: e.tensor_tensor(out=kpe[64:96], in0=t1[64:96], in1=t2[64:96], op=ALU.add), rd=["t1", "t2"], wr=["kpe"])
                        for h in range(4):
                            p1, p2 = nps(), nps()
                            def mq(e, p1=p1, p2=p2, h=h, tsl=tsl):
                                ins = None
                                for k in range(3):
                                    e.matmul(ps[p1][0:96, :], lhsT=wqb[:, k, h * 96:(h + 1) * 96], rhs=cq[:, k, tsl], start=(k == 0), stop=(k == 2))
                                for k in range(3):
                                    ins = e.matmul(ps[p2][0:96, :], lhsT=wqb[:, k, 384 + h * 96:384 + (h + 1) * 96], rhs=cq[:, k, tsl], start=(k == 0), stop=(k == 2))
                                return ins
                            P.op("pe", mq, rd=["wqb", "cq"], wr=[psk[p1], psk[p2]])
                            P.op("dve", lambda e, p1=p1: e.tensor_tensor(out=qf[0:96], in0=ps[p1][0:96, :], in1=rq[0:96], op=ALU.mult), rd=[psk[p1], "rq"], wr=["qf"])
                            P.op("dve", lambda e, p2=p2: e.tensor_tensor(out=t1[64:96], in0=ps[p2][64:96, :], in1=rq[64:96], op=ALU.mult), rd=[psk[p2], "rq"], wr=["t1"])
                            P.op("dve", lambda e, tsl=tsl: e.tensor_tensor(out=t1[64:96], in0=t1[64:96], in1=rp[64:96, 1, tsl], op=ALU.mult), rd=["t1", "rp"], wr=["t1"])
                            P.op("dve", lambda e, tsl=tsl: e.tensor_tensor(out=t2[64:96], in0=qf[64:96], in1=rp[64:96, 0, tsl], op=ALU.mult), rd=["qf", "rp"], wr=["t2"])
                            P.op("dve", lambda e: e.tensor_tensor(out=qf[64:96], in0=t1[64:96], in1=t2[64:96], op=ALU.add), rd=["t1", "t2"], wr=["qf"])
                            P.op("act", lambda e, h=h: e.activation(out=qst[0:96, h, :], in_=qf[0:96], func=AF.Copy), rd=["qf"], wr=["qst"])
                            p3 = nps()
                            def mk(e, p3=p3, h=h, tsl=tsl):
                                e.matmul(ps[p3][0:64, :], lhsT=wkb[:, 0, h * 64:(h + 1) * 64], rhs=cq[:, 3, tsl], start=True, stop=False)
                                e.matmul(ps[p3][0:64, :], lhsT=wkb[:, 1, h * 64:(h + 1) * 64], rhs=cq[:, 4, tsl], start=False, stop=False)
                                return e.matmul(ps[p3][0:64, :], lhsT=wkb[0:64, 2, h * 64:(h + 1) * 64], rhs=cq[0:64, 5, tsl], start=False, stop=True)
                            P.op("pe", mk, rd=["wkb", "cq"], wr=[psk[p3]])
                            P.op("dve", lambda e, p3=p3, h=h: e.tensor_tensor(out=kst[0:64, h, :], in0=ps[p3][0:64, :], in1=rk[0:64], op=ALU.mult), rd=[psk[p3], "rk"], wr=["kst"])
                            P.op("act", lambda e, h=h: e.activation(out=kst[64:96, h, :], in_=kpe[64:96], func=AF.Copy), rd=["kpe"], wr=["kst"])
                        P.dma("sp", QM[j][:, tsl].rearrange("(h d) t -> d h t", h=4), qst[0:96], rd=["qst"], wr=[("QM", j)])
                        P.dma("sp", kdst[:, tsl].rearrange("(h d) t -> d h t", h=4), kst[0:96], rd=["kst"], wr=[("KM", j)])
                        for tb in range(4):
                            pv2 = nps()
                            def mv(e, pv2=pv2, tb=tb, tt=tt):
                                c0 = tt * 512 + tb * 128
                                e.matmul(ps[pv2][:, 0:256], lhsT=cq[:, 3, c0:c0 + 128], rhs=wkb[:, 0, 256:512], start=True, stop=False)
                                e.matmul(ps[pv2][:, 0:256], lhsT=cq[:, 4, c0:c0 + 128], rhs=wkb[:, 1, 256:512], start=False, stop=False)
                                return e.matmul(ps[pv2][:, 0:256], lhsT=cq[0:64, 5, c0:c0 + 128], rhs=wkb[0:64, 2, 256:512], start=False, stop=True)
                            P.op("pe", mv, rd=["wkb", "cq"], wr=[psk[pv2]])
                            P.op("dve", lambda e, pv2=pv2, tb=tb: e.tensor_scalar(out=vst[:, tb, :, 0:64], in0=ps[pv2][:, 0:256].rearrange("p (h d) -> p h d", h=4), scalar1=rkt[:, tb:tb + 1], scalar2=1.0, op0=ALU.mult, op1=ALU.mult),
                                 rd=[psk[pv2], "rkt"], wr=["vst"])
                        P.dma("sp", vdst[tt * 512:(tt + 1) * 512, :].rearrange("(tb p) c -> p tb c", p=128), vst[:].rearrange("p tb h d -> p tb (h d)"), rd=["vst"], wr=[("VM", j)])
                P.barrier()

        def do_ag():
            P.custom("pool", lambda e: e.collective_compute("AllGather", ALU.bypass, replica_groups=[list(range(NCORES))],
                                                            ins=[PK], outs=[AGB[R_TOT:9 * R_TOT, :]]),
                     rd=[("ZA", 0), ("KD", 0), ("VD", 0), ("KM", 0), ("VM", 0)], wr=["AGB"], inc=1)

        def mla_attend(l, j):
            nkt = 128 if j == 0 else 16
            with ExitStack() as s1:
                otok = sb([128, 16, 256], F32, s1)
                mqs = sb([128, 8], F32, s1)
                pT = [sb([128, 512], BF16, s1) for _ in range(3)]
                rc = sb([128, 4], F32, s1)
                kTs = [sb([128, nkt * 128], BF16, s1) for _ in range(2)]
                vvs = [sb([128, nkt, 65], BF16, s1) for _ in range(2)]
                qTs = [sb([128, T], BF16, s1) for _ in range(2)]
                sstmp = [(sb([128, 512], BF16, s1), sb([128, 32], F32, s1)) for _ in range(2)]

                def load(h):
                    hb = h % 2
                    kT, vv, qT = kTs[hb], vvs[hb], qTs[hb]
                    if j == 0:
                        for s_ in range(8):
                            base = (1 + s_) * R_TOT
                            P.dma("sp", kT[0:96, s_ * T:(s_ + 1) * T], AGB[base + R_KM + h * 96: base + R_KM + (h + 1) * 96, :], rd=["AGB"], wr=["kT%d" % hb])
                            vsrc = AGB[base + R_VM: base + R_VM + 260, :].rearrange("r c -> (r c)").rearrange("(tb p c) -> p tb c", p=128, c=260)
                            P.dma("sp", vv[:, s_ * 16:(s_ + 1) * 16, :], vsrc[:, :, h * 65:(h + 1) * 65], rd=["AGB"], wr=["vv%d" % hb])
                    else:
                        P.dma("sp", kT[0:96, :], KM[j][h * 96:(h + 1) * 96, :], rd=[("KM", j)], wr=["kT%d" % hb])
                        P.dma("sp", vv[:], VM[j].rearrange("(tb p) c -> p tb c", p=128)[:, :, h * 65:(h + 1) * 65], rd=[("VM", j)], wr=["vv%d" % hb])
                    P.dma("sp", qT[0:96, :], QM[j][h * 96:(h + 1) * 96, :], rd=[("QM", j)], wr=["qT%d" % hb])

                load(0)
                for h in range(4):
                    if h + 1 < 4:
                        load(h + 1)
                    hb = h % 2
                    kT, vv, qT = kTs[hb], vvs[hb], qTs[hb]
                    kTk, vvk, qTk = "kT%d" % hb, "vv%d" % hb, "qT%d" % hb
                    mq = mqs[:, hb * 4:hb * 4 + 4]
                    mk_ = "mq"
                    if True:
                        sumsq_max(s1, lambda t, qT=qT: qT[0:96, t * 512:(t + 1) * 512], 96, 4, mq[:, 0:1], [qTk], "mq", tmp=sstmp[0])
                        sumsq_max(s1, lambda t, kT=kT: kT[0:96, t * 512:(t + 1) * 512], 96, nkt // 4, mq[:, 1:2], [kTk], "mk", tmp=sstmp[1])
                        P.op("dve", lambda e, mq=mq: e.tensor_tensor(out=mq[:, 2:3], in0=mq[:, 0:1], in1=mq[:, 1:2], op=ALU.mult), rd=["mqdst", "mkdst"], wr=[mk_ + "2"])
                        P.op("act", lambda e, mq=mq: e.activation(out=mq[:, 2:3], in_=mq[:, 2:3], func=AF.Sqrt), rd=[mk_ + "2"], wr=[mk_ + "2"])
                        P.op("dve", lambda e, mq=mq: e.tensor_scalar(out=mq[:, 3:4], in0=mq[:, 2:3], scalar1=-1.0, scalar2=1.0, op0=ALU.mult, op1=ALU.mult), rd=[mk_ + "2"], wr=[mk_ + "3"])
                        for qt in range(4):
                            def emitS(kt, qt=qt):
                                pS = kt % 4
                                P.op("pe", lambda e, pS=pS, kt=kt, qt=qt: e.matmul(ps[pS][:, :], lhsT=kT[0:96, kt * 128:(kt + 1) * 128], rhs=qT[0:96, qt * 512:(qt + 1) * 512], start=True, stop=True),
                                     rd=[kTk, qTk], wr=[psk[pS]])
                            def emitE(kt):
                                pS = kt % 4; b = kt % 3
                                P.op("act", lambda e, pS=pS, b=b: e.activation(out=pT[b][:], in_=ps[pS][:, :], func=AF.Exp, bias=mq[:, 3:4]), rd=[psk[pS], mk_ + "3"], wr=["pT%d" % b])
                            def emitPV(kt):
                                b = kt % 3
                                def pv(e, b=b, kt=kt):
                                    ins = None
                                    for qs in range(4):
                                        ins = e.matmul(ps[4 + qs][:, 0:65], lhsT=pT[b][:, qs * 128:(qs + 1) * 128], rhs=vv[:, kt, :], start=(kt == 0), stop=(kt == nkt - 1))
                                    return ins
                                P.op("pe", pv, rd=["pT%d" % b, vvk], wr=[psk[4], psk[5], psk[6], psk[7]])
                            emitS(0); emitS(1)
                            for kt in range(nkt):
                                if kt + 2 < nkt:
                                    emitS(kt + 2)
                                emitE(kt)
                                emitPV(kt)
                            for qs in range(4):
                                P.op("dve", lambda e, qs=qs: e.reciprocal(out=rc[:, qs:qs + 1], in_=ps[4 + qs][:, 64:65]), rd=[psk[4 + qs]], wr=["rc%d" % qs])
                                P.op("dve", lambda e, qs=qs, qt=qt, h=h: e.tensor_scalar(out=otok[:, qt * 4 + qs, h * 64:(h + 1) * 64], in0=ps[4 + qs][:, 0:64], scalar1=rc[:, qs:qs + 1], scalar2=1.0, op0=ALU.mult, op1=ALU.mult),
                                     rd=[psk[4 + qs], "rc%d" % qs], wr=["otok"])
                ost = [sb([128, 2, 128], BF16, s1) for _ in range(2)]
                for tq in range(16):
                    b = tq % 2
                    pi = tq % 4
                    def tr(e, pi=pi, tq=tq):
                        e.transpose(out=ps[pi][:, 0:128], in_=otok[:, tq, 0:128], identity=ident[:])
                        return e.transpose(out=ps[pi][:, 128:256], in_=otok[:, tq, 128:256], identity=ident[:])
                    P.op("pe", tr, rd=["otok", "ident"], wr=[psk[pi]])
                    P.op("act", lambda e, pi=pi, b=b: e.activation(out=ost[b][:], in_=ps[pi][:, 0:256].rearrange("p (a t) -> p a t", a=2), func=AF.Copy), rd=[psk[pi]], wr=["ost%d" % b])
                    P.dma("sp", MT[j][1280:1536, tq * 128:(tq + 1) * 128].rearrange("(a p) t -> p a t", p=128), ost[b][:], rd=["ost%d" % b], wr=[("MT", j)])
            P.barrier()

        def setup_dil():
            with ExitStack() as s1:
                c1 = sb([128, T], BF16, s1); cz = sb([128, 2 * T], BF16, s1); cn = sb([128, T], BF16, s1)
                vrow = sb([128, 390], BF16, s1)
                P.op("pool", lambda e: e.memset(c1[:], 1.0), wr=["c1"])
                P.op("pool", lambda e: e.memset(cz[:], 0.0), wr=["cz"])
                P.op("pool", lambda e: e.memset(cn[:], NEG), wr=["cn"])
                P.op("pool", lambda e: e.memset(vrow[:], 0.0), wr=["vrow"])
                P.op("pool", lambda e: e.memset(vrow[:].rearrange("p (g d) -> p g d", d=65)[:, :, 64:65], 1.0), wr=["vrow"])
                for j in jobs:
                    P.dma("sp", QD[j].rearrange("(g d) t -> g d t", d=65)[:, 64, :], c1[0:6, :], rd=["c1"], wr=[("QD", j)])
                def fillk(dst390, ncol, neg_ranges, key):
                    for r0 in range(0, 390, 128):
                        n = min(128, 390 - r0)
                        P.dma("sp", dst390[r0:r0 + n, :], cz[0:n, 0:ncol], rd=["cz"], wr=[key])
                    for (a, b_) in neg_ranges:
                        P.dma("sp", dst390.rearrange("(g d) t -> g d t", d=65)[:, 64, a:b_], cn[0:6, 0:b_ - a], rd=["cn"], wr=[key])
                def fillv(dst_rows_fn, nblk, key):
                    for n in range(nblk):
                        P.dma("sp", dst_rows_fn(n), vrow[:], rd=["vrow"], wr=[key])
                for j in jobs:
                    if j == 0:
                        fillk(PK[R_KD:R_KD + 390, :], T, [], ("KD", 0))
                        pv = PK[R_VD:R_VD + 390, :].rearrange("r c -> (r c)").rearrange("(t c) -> t c", c=390)
                        fillv(lambda n, pv=pv: pv[n * 128:(n + 1) * 128, :], 16, ("VD", 0))
                        for s_ in (0, 9):
                            base = s_ * R_TOT
                            fillk(AGB[base + R_KD: base + R_KD + 390, :], T, [(0, T)], "AGB")
                            av = AGB[base + R_VD: base + R_VD + 390, :].rearrange("r c -> (r c)").rearrange("(t c) -> t c", c=390)
                            fillv(lambda n, av=av: av[n * 128:(n + 1) * 128, :], 16, "AGB")
                    else:
                        fillk(KDs[j], 2 * T, [(0, 1024), (3072, 4096)], ("KD", j))
                        fillv(lambda n, j=j: VDs[j][n * 128:(n + 1) * 128, :], 32, ("VD", j))
                ohs = sb([128, 32, 2, 128], BF16, s1)
                bnd = sb([128, 2, 128], F32, s1)
                rb = sb([128, 192], F32, s1)
                acc = sb([128, 2, 128], F32, s1)
                P.dma("sp", bnd[:], band_in, wr=["bnd"])
                P.dma("sp", rb[:], relb_in, wr=["rb"])
                for g in range(3):
                    P.dma("sp", ohs[:], oh_in[g], wr=["ohs"])
                    for hh in range(2):
                        gh = g * 2 + hh
                        P.op("dve", lambda e: e.tensor_copy(out=acc[:], in_=bnd[:]), rd=["bnd"], wr=["bacc"])
                        for b in range(32):
                            P.op("dve", lambda e, b=b, gh=gh: e.scalar_tensor_tensor(out=acc[:], in0=ohs[:, b, :, :], scalar=rb[:, b * 6 + gh: b * 6 + gh + 1], in1=acc[:], op0=ALU.mult, op1=ALU.add),
                                 rd=["ohs", "rb", "bacc"], wr=["bacc"])
                        P.op("act", lambda e, gh=gh: e.activation(out=biasT[:, gh, :, :], in_=acc[:], func=AF.Copy), rd=["bacc"], wr=["biasT"])
            P.barrier()

        def dyn_copy():
            pid = PID[0]
            for (c_a, c_b, so, s_a, s_b) in ((0, 1024, 0, 1024, 2048), (1024, 3072, 1, 0, 2048), (3072, 4096, 2, 0, 1024)):
                P.dma("pool", KDs[0][:, c_a:c_b], AGB[bass.ds(pid * R_TOT + (so * R_TOT + R_KD), 390), s_a:s_b], rd=["AGB"], wr=[("KD", 0)])
            vflat = VDs[0][0:2 * T, :].rearrange("t c -> (t c)").rearrange("(r w) -> r w", w=T)
            for (r_a, r_b, so, off) in ((0, 195, 0, 195), (195, 585, 1, 0), (585, 780, 2, 0)):
                P.dma("pool", vflat[r_a:r_b, :], AGB[bass.ds(pid * R_TOT + (so * R_TOT + R_VD + off), r_b - r_a), :], rd=["AGB"], wr=[("VD", 0)])

        def do_dil(l, j):
            with ExitStack() as s1:
                oacc = sb([128, 16, 65], F32, s1)
                mq = sb([128, 4], F32, s1)
                pT = [sb([128, 256], BF16, s1) for _ in range(3)]
                it = 0
                for g in range(3):
                    dil = (1, 4, 16)[g]
                    nb = T // dil // 128
                    Pd = 1024 // dil
                    for hh in range(2):
                        gh = g * 2 + hh
                        with ExitStack() as s2:
                            qT = sb([128, T], BF16, s2); kT = sb([128, 2 * T], BF16, s2)
                            vv = sb([128, dil * (nb + 1), 65], BF16, s2)
                            P.dma("sp", qT[0:65, :], QD[j][gh * 65:(gh + 1) * 65, :], rd=[("QD", j)], wr=["dqT"])
                            P.dma("sp", kT[0:65, :], KDs[j][gh * 65:(gh + 1) * 65, :], rd=[("KD", j)], wr=["dkT"])
                            for rho in range(dil):
                                base = rho + dil * (Pd - 64)
                                src = VDs[j][base: base + dil * 128 * (nb + 1), gh * 65:(gh + 1) * 65].rearrange("(m i d) c -> i m d c", i=128, d=dil)[:, :, 0, :]
                                P.dma("sp", vv[:, rho * (nb + 1):(rho + 1) * (nb + 1), :], src, rd=[("VD", j)], wr=["dvv"])
                            sumsq_max(s2, lambda t: qT[0:64, t * 512:(t + 1) * 512], 64, 4, mq[:, 0:1], ["dqT"], "dq")
                            sumsq_max(s2, lambda t: kT[0:64, t * 512:(t + 1) * 512], 64, 8, mq[:, 1:2], ["dkT"], "dk")
                            P.op("dve", lambda e: e.tensor_tensor(out=mq[:, 2:3], in0=mq[:, 0:1], in1=mq[:, 1:2], op=ALU.mult), rd=["dqdst", "dkdst"], wr=["dm2"])
                            P.op("act", lambda e: e.activation(out=mq[:, 2:3], in_=mq[:, 2:3], func=AF.Sqrt), rd=["dm2"], wr=["dm2"])
                            P.op("dve", lambda e, gh=gh: e.tensor_scalar(out=negM[:, gh:gh + 1], in0=mq[:, 2:3], scalar1=-1.0, scalar2=1.0, op0=ALU.mult, op1=ALU.mult), rd=["dm2"], wr=["negM"])
                            blocks = [(rho, b) for rho in range(dil) for b in range(nb)]
                            sbank = {}
                            def emit_ms(i, dil=dil, Pd=Pd, gh=gh):
                                rho, b = blocks[i]
                                pS = nps(); sbank[i] = pS
                                def ms(e, pS=pS, rho=rho, b=b):
                                    ins = None
                                    q0 = rho + dil * 128 * b
                                    for mi in range(2):
                                        k0 = rho + dil * (Pd - 64 + 128 * (b + mi))
                                        e.matmul(ps[pS][:, mi * 128:(mi + 1) * 128], lhsT=kT[0:65, k0:k0 + 127 * dil + 1:dil], rhs=qT[0:65, q0:q0 + 127 * dil + 1:dil], start=True, stop=False)
                                        ins = e.matmul(ps[pS][:, mi * 128:(mi + 1) * 128], lhsT=ident_bf[:], rhs=biasT[:, gh, mi, :], start=False, stop=True)
                                    return ins
                                P.op("pe", ms, rd=["dkT", "dqT", "identbf", "biasT"], wr=[psk[pS]])
                            def emit_rest(i, nb=nb, gh=gh):
                                rho, b = blocks[i]
                                slot = rho * nb + b
                                pS = sbank[i]; pb = i % 3
                                P.op("act", lambda e, pS=pS, pb=pb: e.activation(out=pT[pb][:], in_=ps[pS][:, 0:256], func=AF.Exp, bias=negM[:, gh:gh + 1]), rd=[psk[pS], "negM"], wr=["dpT%d" % pb])
                                pO = nps()
                                def mo(e, pO=pO, pb=pb, rho=rho, b=b):
                                    e.matmul(ps[pO][:, 0:65], lhsT=pT[pb][:, 0:128], rhs=vv[:, rho * (nb + 1) + b, :], start=True, stop=False)
                                    return e.matmul(ps[pO][:, 0:65], lhsT=pT[pb][:, 128:256], rhs=vv[:, rho * (nb + 1) + b + 1, :], start=False, stop=True)
                                P.op("pe", mo, rd=["dpT%d" % pb, "dvv"], wr=[psk[pO]])
                                P.op("dve", lambda e, pO=pO, slot=slot: e.tensor_copy(out=oacc[:, slot, :], in_=ps[pO][:, 0:65]), rd=[psk[pO]], wr=["oacc"])
                            emit_ms(0)
                            if len(blocks) > 1:
                                emit_ms(1)
                            for i in range(len(blocks)):
                                if i + 2 < len(blocks):
                                    emit_ms(i + 2)
                                emit_rest(i)
                            for rho in range(dil):
                                dst = DACC[j][g, rho: rho + T, hh * 65:(hh + 1) * 65].rearrange("(b i d) c -> i b d c", i=128, d=dil)[:, :, 0, :]
                                P.dma("sp", dst, oacc[:, rho * nb:(rho + 1) * nb, :], rd=["oacc"], wr=[("DACC", j)])
                w = sb([128, 8], F32, s1); mn = sb([128, 2], F32, s1)
                for hh in range(2):
                    P.op("dve", lambda e, hh=hh: e.tensor_tensor(out=mn[:, hh:hh + 1], in0=negM[:, hh:hh + 1], in1=negM[:, 2 + hh:3 + hh], op=ALU.min), rd=["negM"], wr=["mn"])
                    P.op("dve", lambda e, hh=hh: e.tensor_tensor(out=mn[:, hh:hh + 1], in0=mn[:, hh:hh + 1], in1=negM[:, 4 + hh:5 + hh], op=ALU.min), rd=["negM", "mn"], wr=["mn"])
                    for g in range(3):
                        P.op("act", lambda e, hh=hh, g=g: e.activation(out=w[:, g * 2 + hh:g * 2 + hh + 1], in_=negM[:, g * 2 + hh:g * 2 + hh + 1], func=AF.Exp, scale=-1.0, bias=mn[:, hh:hh + 1]),
                             rd=["negM", "mn"], wr=["dw"])
                da = [sb([128, 3, 130], F32, s1) for _ in range(2)]
                ss = [sb([128, 130], F32, s1) for _ in range(2)]
                oo = [sb([128, 128], F32, s1) for _ in range(2)]
                ob = [sb([128, 128], BF16, s1) for _ in range(2)]
                rc = [sb([128, 2], F32, s1) for _ in range(2)]
                for tb in range(T // 128):
                    b = tb % 2
                    P.dma("sp", da[b][:], DACC[j][:, tb * 128:(tb + 1) * 128, :].rearrange("g t c -> t g c"), rd=[("DACC", j)], wr=["da%d" % b])
                    for g in range(3):
                        for hh in range(2):
                            P.op("dve", lambda e, b=b, g=g, hh=hh: e.tensor_scalar(out=da[b][:, g, hh * 65:(hh + 1) * 65], in0=da[b][:, g, hh * 65:(hh + 1) * 65],
                                                                                 scalar1=w[:, g * 2 + hh:g * 2 + hh + 1], scalar2=1.0, op0=ALU.mult, op1=ALU.mult), rd=["da%d" % b, "dw"], wr=["da%d" % b])
                    P.op("pool", lambda e, b=b: e.tensor_tensor(out=ss[b][:], in0=da[b][:, 0, :], in1=da[b][:, 1, :], op=ALU.add), rd=["da%d" % b], wr=["ss%d" % b])
                    P.op("pool", lambda e, b=b: e.tensor_tensor(out=ss[b][:], in0=ss[b][:], in1=da[b][:, 2, :], op=ALU.add), rd=["da%d" % b, "ss%d" % b], wr=["ss%d" % b])
                    for hh in range(2):
                        P.op("dve", lambda e, b=b, hh=hh: e.reciprocal(out=rc[b][:, hh:hh + 1], in_=ss[b][:, hh * 65 + 64:hh * 65 + 65]), rd=["ss%d" % b], wr=["drc%d" % b])
                        P.op("dve", lambda e, b=b, hh=hh: e.tensor_scalar(out=oo[b][:, hh * 64:(hh + 1) * 64], in0=ss[b][:, hh * 65:hh * 65 + 64], scalar1=rc[b][:, hh:hh + 1], scalar2=1.0, op0=ALU.mult, op1=ALU.mult),
                             rd=["ss%d" % b, "drc%d" % b], wr=["oo%d" % b])
                    pi = nps()
                    P.op("pe", lambda e, pi=pi, b=b: e.transpose(out=ps[pi][:, 0:128], in_=oo[b][:], identity=ident[:]), rd=["oo%d" % b, "ident"], wr=[psk[pi]])
                    P.op("act", lambda e, pi=pi, b=b: e.activation(out=ob[b][:], in_=ps[pi][:, 0:128], func=AF.Copy), rd=[psk[pi]], wr=["dob%d" % b])
                    P.dma("sp", MT[j][768:896, tb * 128:(tb + 1) * 128], ob[b][:], rd=["dob%d" % b], wr=[("MT", j)])
            P.barrier()

        def do_fnet(l, j):
            S_tiles = 128 if j == 0 else 16
            tab = dftP if j == 0 else dftS

            def xsrc(nt):
                if j == 0:
                    s_, r_ = divmod(nt, 16)
                    base = (1 + s_) * R_TOT + R_ZA
                    v = AGB[base:base + 768, :].rearrange("r c -> (r c)").rearrange("(t c) -> t c", c=768)
                    return v[r_ * 128:(r_ + 1) * 128, :]
                v = ZA[j].rearrange("r c -> (r c)").rearrange("(t c) -> t c", c=768)
                return v[nt * 128:(nt + 1) * 128, :]
            xkey = "AGB" if j == 0 else ("ZA", j)
            variants = [(0, 128, 0), (0, 64, 128), (64, 64, 0), (0, 128, 64)]
            with ExitStack() as s1:
                chf = sb([128, 4, 2, 192], F32, s1); cht = sb([128, 4, 2, 192], BF16, s1)
                P.op("pool", lambda e: e.memset(chf[:], 0.0), wr=["chf"])
                for v, (p0, n, ch0) in enumerate(variants):
                    for cs_ in range(2):
                        P.dma("sp", chf[p0:p0 + n, v, cs_, :], chdft_in[cs_, ch0:ch0 + n, :], wr=["chf"])
                P.op("pool", lambda e: e.tensor_copy(out=cht[:], in_=chf[:]), rd=["chf"], wr=["cht"])
                uv = sb([128, 2, 6, 512], BF16, s1)
                xt = [sb([128, 384], BF16, s1) for _ in range(3)]
                ct = [sb([128, 512], BF16, s1) for _ in range(3)]
                st_ = [sb([128, 512], BF16, s1) for _ in range(3)]
                fo = [sb([128, 512], BF16, s1) for _ in range(2)]
                for kt in range(4):
                    for cg in range(2):
                        for nt in range(S_tiles):
                            b = nt % 3
                            P.dma("sp", xt[b][:], xsrc(nt)[:, cg * 384:(cg + 1) * 384], rd=[xkey], wr=["fx%d" % b])
                            P.dma("sp", ct[b][:], tab[0, nt * 128:(nt + 1) * 128, kt * 512:(kt + 1) * 512], wr=["fc%d" % b])
                            P.dma("sp", st_[b][:], tab[1, nt * 128:(nt + 1) * 128, kt * 512:(kt + 1) * 512], wr=["fs%d" % b])
                            def mm(e, b=b, nt=nt):
                                ins = None
                                for cc in range(3):
                                    e.matmul(ps[cc][:, :], lhsT=xt[b][:, cc * 128:(cc + 1) * 128], rhs=ct[b][:], start=(nt == 0), stop=(nt == S_tiles - 1))
                                    ins = e.matmul(ps[3 + cc][:, :], lhsT=xt[b][:, cc * 128:(cc + 1) * 128], rhs=st_[b][:], start=(nt == 0), stop=(nt == S_tiles - 1))
                                return ins
                            P.op("pe", mm, rd=["fx%d" % b, "fc%d" % b, "fs%d" % b], wr=[psk[i] for i in range(6)])
                        for cc in range(3):
                            P.op("act", lambda e, cc=cc, cg=cg: e.activation(out=uv[:, 0, cg * 3 + cc, :], in_=ps[cc][:, :], func=AF.Copy), rd=[psk[cc]], wr=["uv"])
                            P.op("dve", lambda e, cc=cc, cg=cg: e.tensor_copy(out=uv[:, 1, cg * 3 + cc, :], in_=ps[3 + cc][:, :]), rd=[psk[3 + cc]], wr=["uv"])
                    for oc in range(6):
                        pb = 6 + oc % 2
                        def chm(e, oc=oc, pb=pb):
                            ins = None
                            for gi in range(4):
                                c0 = 192 * gi
                                cA, offA = c0 // 128, c0 % 128
                                if offA == 0:
                                    pieces = [(cA, 0, 128, 0, 0), (cA + 1, 0, 64, 128, 1)]
                                else:
                                    pieces = [(cA, 64, 64, 0, 2), (cA + 1, 0, 128, 64, 3)]
                                for (mchunk, mp0, mn_, mch0, _) in pieces:
                                    if mchunk != oc:
                                        continue
                                    first = True
                                    for pi_, (kchunk, kp0, kn, kch0, kv) in enumerate(pieces):
                                        for cs_ in range(2):
                                            last = (pi_ == 1 and cs_ == 1)
                                            ins = e.matmul(ps[pb][mp0:mp0 + mn_, :], lhsT=cht[kp0:kp0 + kn, kv, cs_, mch0:mch0 + mn_], rhs=uv[kp0:kp0 + kn, cs_, kchunk, :], start=first, stop=last)
                                            first = False
                            return ins
                        P.op("pe", chm, rd=["cht", "uv"], wr=[psk[pb]])
                        fb = oc % 2
                        P.op("act", lambda e, pb=pb, fb=fb: e.activation(out=fo[fb][:], in_=ps[pb][:, :], func=AF.Copy), rd=[psk[pb]], wr=["fo%d" % fb])
                        P.dma("sp", MT[j][oc * 128:(oc + 1) * 128, kt * 512:(kt + 1) * 512], fo[fb][:], rd=["fo%d" % fb], wr=[("MT", j)])
            P.barrier()

        if flags['dil']:
            setup_dil()
        convert_weights(0)
        for l in range(depth):
            for j in jobs:
                with ExitStack() as s1:
                    hT = sb([128, KD, T], BF16, s1)
                    xt = [sb([128, KD, 512], F32, s1) for _ in range(2)]
                    sq = sb([128, KD, 512], BF16, s1)
                    rs = sb([128, 512], F32, s1)
                    stg = [sb([128, 512], BF16, s1) for _ in range(4)]
                    sgi = [0]
                    for tt in range(NTT):
                        b = tt % 2
                        P.dma("sp", xt[b][:], xT[j, :, tt * 512:(tt + 1) * 512].rearrange("(k p) t -> p k t", p=128), rd=[("xT", j)], wr=["xt%d" % b])
                        rms_tile(s1, xt[b], "xt%d" % b, KD, lambda k: a1[:, l, k, j:j + 1], lambda k: mods[:, l, k, j:j + 1],
                                 hT, "hT", tt * 512, 512, 1.0 / D, (sq, rs), "A")

                    def store(pi, nrow, ncol, dst, func=AF.Copy, scale=1.0, wrk=None):
                        si = sgi[0]; sgi[0] = (si + 1) % 4
                        P.op("act", lambda e: e.activation(out=stg[si][0:nrow, 0:ncol], in_=ps[pi][0:nrow, 0:ncol], func=func, scale=scale),
                             rd=[psk[pi]], wr=["stg%d" % si])
                        P.dma("act", dst, stg[si][0:nrow, 0:ncol], rd=["stg%d" % si], wr=[wrk])

                    pieces = [(0, 384), (384, 384), (768, 384), (1152, 384), (1536, 384), (1920, 384), (2304, 384), (2688, 384), (3072, 384)]
                    pieces += [(C_ZG + i * 512, 512) for i in range(8)]
                    kch = [(k * 128, 128) for k in range(KD)]
                    nxt = wload("w_in", l, kch, *pieces[0])
                    for pi_, (c0, ncols) in enumerate(pieces):
                        wt, wk = nxt
                        if pi_ + 1 < len(pieces):
                            nxt = wload("w_in", l, kch, *pieces[pi_ + 1])
                        if c0 < C_BQ:
                            for tb in range(T // 128):
                                pi = nps()
                                def mm(e, pi=pi, tb=tb, wt=wt, ncols=ncols):
                                    ins = None
                                    for k in range(KD):
                                        ins = e.matmul(ps[pi][:, 0:ncols], lhsT=hT[:, k, tb * 128:(tb + 1) * 128], rhs=wt[:, k, 0:ncols], start=(k == 0), stop=(k == KD - 1))
                                    return ins
                                P.op("pe", mm, rd=[wk, "hT"], wr=[psk[pi]])
                                if c0 < C_BV:
                                    zdst = (PK[R_ZA:R_ZA + 768, :] if j == 0 else ZA[j]).rearrange("r c -> (r c)").rearrange("(t c) -> t c", c=768)
                                    store(pi, 128, 384, zdst[tb * 128:(tb + 1) * 128, c0:c0 + 384], wrk=("ZA", j))
                                elif c0 < C_CV:
                                    if j == 0:
                                        vdst = PK[R_VD:R_VD + 390, :].rearrange("r c -> (r c)").rearrange("(t g h d) -> t g h d", g=3, h=2, d=65)
                                        vdst = vdst[tb * 128:(tb + 1) * 128, :, :, 0:64]
                                    else:
                                        vdst = VDs[j].rearrange("t (g h d) -> t g h d", g=3, h=2)[T // 2 + tb * 128:T // 2 + (tb + 1) * 128, :, :, 0:64]
                                    si = sgi[0]; sgi[0] = (si + 1) % 4
                                    P.op("act", lambda e, si=si, pi=pi: e.activation(out=stg[si][:, 0:384], in_=ps[pi][:, 0:384], func=AF.Copy), rd=[psk[pi]], wr=["stg%d" % si])
                                    P.dma("act", vdst, stg[si][:, 0:384].rearrange("p (g h d) -> p g h d", g=3, h=2), rd=["stg%d" % si], wr=[("VD", j)])
                                else:
                                    store(pi, 128, 384, CV[j][tb * 128:(tb + 1) * 128, :], wrk=("CV", j))
                        else:
                            if c0 == C_CU:
                                chunks = [(i * 96, 96) for i in range(4)]
                            elif c0 == C_CKV:
                                chunks = [(0, 128), (128, 128), (256, 64), (320, 64)]
                            else:
                                chunks = [(i * 128, 128) for i in range(ncols // 128)]
                            for ci, (co, M) in enumerate(chunks):
                                for tt in range(NTT):
                                    pi = nps()
                                    def mm(e, pi=pi, tt=tt, wt=wt, co=co, M=M):
                                        ins = None
                                        for k in range(KD):
                                            ins = e.matmul(ps[pi][0:M, :], lhsT=wt[:, k, co:co + M], rhs=hT[:, k, tt * 512:(tt + 1) * 512], start=(k == 0), stop=(k == KD - 1))
                                        return ins
                                    P.op("pe", mm, rd=[wk, "hT"], wr=[psk[pi]])
                                    tsl = slice(tt * 512, (tt + 1) * 512)
                                    if c0 == C_BQ:
                                        dst = QD[j].rearrange("(g h d) t -> g h d t", g=3, h=2)[ci, :, 0:64, tsl]
                                        si = sgi[0]; sgi[0] = (si + 1) % 4
                                        P.op("act", lambda e, si=si, pi=pi: e.activation(out=stg[si][:, :], in_=ps[pi][:, :], func=AF.Copy, scale=0.125), rd=[psk[pi]], wr=["stg%d" % si])
                                        for hh in range(2):
                                            P.dma("act", dst[hh], stg[si][hh * 64:(hh + 1) * 64, :], rd=["stg%d" % si], wr=[("QD", j)])
                                    elif c0 == C_BK:
                                        if j == 0:
                                            dst = PK[R_KD:R_KD + 390, :].rearrange("(g h d) t -> g h d t", g=3, h=2)[ci, :, 0:64, tsl]
                                        else:
                                            dst = KDs[j].rearrange("(g h d) t -> g h d t", g=3, h=2)[ci, :, 0:64, T // 2 + tt * 512:T // 2 + (tt + 1) * 512]
                                        si = sgi[0]; sgi[0] = (si + 1) % 4
                                        P.op("act", lambda e, si=si, pi=pi: e.activation(out=stg[si][:, :], in_=ps[pi][:, :], func=AF.Copy), rd=[psk[pi]], wr=["stg%d" % si])
                                        for hh in range(2):
                                            P.dma("act", dst[hh], stg[si][hh * 64:(hh + 1) * 64, :], rd=["stg%d" % si], wr=[("KD", j)])
                                    elif c0 == C_CU:
                                        store(pi, 96, 512, CU[j][ci * 96:(ci + 1) * 96, tsl], wrk=("CU", j))
                                    elif c0 == C_CQ:
                                        store(pi, 128, 512, CQ[j][ci * 128:(ci + 1) * 128, tsl], wrk=("CQ", j))
                                    elif c0 == C_CKV:
                                        store(pi, M, 512, CQ[j][384 + co:384 + co + M, tsl], wrk=("CQ", j))
                                    else:
                                        zc = (c0 - C_ZG) // 128 + ci
                                        store(pi, 128, 512, GT[j][zc * 128:(zc + 1) * 128, tsl], func=AF.Sigmoid, wrk=("GT", j))
                P.barrier()

            if l + 1 < depth:
                convert_weights(l + 1)
            if flags["mla"]:
                for j in jobs:
                    do_mla(l, j)
            if 0 in jobs:
                do_ag()
            if 0 in jobs and flags["dil"]:
                dyn_copy()
            for j in jobs:
                if flags["sgu"]:
                    do_sgu(l, j)
            for j in jobs:
                if flags["dil"]:
                    do_dil(l, j)
            for j in jobs:
                if flags["fnet"]:
                    do_fnet(l, j)
            for j in jobs:
                if flags["mla"]:
                    mla_attend(l, j)

            for j in jobs:
                for hf in range(2):
                    t0 = hf * 1024
                    with ExitStack() as s1:
                        xr = sb([128, KD, 1024], F32, s1)
                        mg = sb([128, KD, 1024], BF16, s1)
                        h2 = mg
                        sq = sb([128, KD, 512], BF16, s1)
                        rs = sb([128, 512], F32, s1)
                        xc = sb([128, KD, 512], F32, s1)
                        P.dma("sp", xr[:], xT[j, :, t0:t0 + 1024].rearrange("(k p) t -> p k t", p=128), rd=[("xT", j)], wr=["xr"])
                        any_mix = flags["fnet"] or flags["dil"] or flags["sgu"] or flags["mla"]
                        if any_mix:
                            with ExitStack() as s2:
                                br = sb([128, 13, 1024], BF16, s2)
                                gt = [sb([128, 1024], BF16, s2) for _ in range(2)]
                                tmpf = [sb([128, 1024], F32, s2) for _ in range(2)]
                                acc = sb([128, 1024], F32, s2)
                                brk = [(i * 128, 128) for i in range(6)] + [(768, 128)] + [(896 + i * 96, 96) for i in range(4)] + [(1280, 128), (1408, 128)]
                                for i, (r0, n) in enumerate(brk):
                                    P.dma("sp", br[0:n, i, :], MT[j][r0:r0 + n, t0:t0 + 1024], rd=[("MT", j)], wr=["br"])
                                bsets = [(list(range(0, 6)), flags["fnet"]), ([6], flags["dil"]), (list(range(7, 11)), flags["sgu"]), ([11, 12], flags["mla"])]
                                gi = 0
                                for dc in range(KD):
                                    first = True
                                    for bi_, (ks, en) in enumerate(bsets):
                                        if not en:
                                            continue
                                        wt, wk = wload("p_all", l, [brk[k] for k in ks], dc * 128, 128)
                                        g = gi % 2; gi += 1
                                        P.dma("sp", gt[g][:], GT[j][bi_ * D + dc * 128: bi_ * D + (dc + 1) * 128, t0:t0 + 1024], rd=[("GT", j)], wr=["gt%d" % g])
                                        for th in range(2):
                                            pi = nps()
                                            def mm(e, pi=pi, th=th, wt=wt, ks=ks):
                                                ins = None
                                                for ii, k in enumerate(ks):
                                                    n = brk[k][1]
                                                    ins = e.matmul(ps[pi][:, :], lhsT=wt[0:n, ii, 0:128], rhs=br[0:n, k, th * 512:(th + 1) * 512], start=(ii == 0), stop=(ii == len(ks) - 1))
                                                return ins
                                            P.op("pe", mm, rd=[wk, "br"], wr=[psk[pi]])
                                            tsl = slice(th * 512, (th + 1) * 512)
                                            if first:
                                                P.op("dve", lambda e, pi=pi, g=g, tsl=tsl: e.tensor_tensor(out=acc[:, tsl], in0=ps[pi][:, :], in1=gt[g][:, tsl], op=ALU.mult),
                                                     rd=[psk[pi], "gt%d" % g], wr=["acc%d" % th])
                                            else:
                                                P.op("dve", lambda e, pi=pi, g=g, tsl=tsl, th=th: e.tensor_tensor(out=tmpf[th][:, 0:512], in0=ps[pi][:, :], in1=gt[g][:, tsl], op=ALU.mult),
                                                     rd=[psk[pi], "gt%d" % g], wr=["tmpf%d" % th])
                                                P.op("dve", lambda e, tsl=tsl, th=th: e.tensor_tensor(out=acc[:, tsl], in0=acc[:, tsl], in1=tmpf[th][:, 0:512], op=ALU.add),
                                                     rd=["tmpf%d" % th, "acc%d" % th], wr=["acc%d" % th])
                                        first = False
                                    P.op("act", lambda e, dc=dc: e.activation(out=mg[:, dc, :], in_=acc[:], func=AF.Copy), rd=["acc0", "acc1"], wr=["mg"])
                                P.barrier()
                            for dc2 in range(2):
                                wt, wk = wload("w_o", l, [(k * 128, 128) for k in range(KD)], dc2 * 512, 512)
                                for cc in range(4):
                                    dc = dc2 * 4 + cc
                                    for th in range(2):
                                        pi = nps()
                                        def mm(e, pi=pi, th=th, wt=wt, cc=cc):
                                            ins = None
                                            for k in range(KD):
                                                ins = e.matmul(ps[pi][:, :], lhsT=wt[:, k, cc * 128:(cc + 1) * 128], rhs=mg[:, k, th * 512:(th + 1) * 512], start=(k == 0), stop=(k == KD - 1))
                                            return ins
                                        P.op("pe", mm, rd=[wk, "mg"], wr=[psk[pi]])
                                        P.op("dve", lambda e, pi=pi, dc=dc, th=th: e.scalar_tensor_tensor(
                                            out=xr[:, dc, th * 512:(th + 1) * 512], in0=ps[pi][:, :], scalar=mods[:, l, 16 + dc, j:j + 1], in1=xr[:, dc, th * 512:(th + 1) * 512],
                                            op0=ALU.mult, op1=ALU.add), rd=[psk[pi], "xr"], wr=["xr"])
                        for th in range(2):
                            P.op("act", lambda e, th=th: e.activation(out=xc[:], in_=xr[:, :, th * 512:(th + 1) * 512], func=AF.Copy), rd=["xr"], wr=["xc"])
                            rms_tile(s1, xc, "xc", KD, lambda k: a2[:, l, k, j:j + 1], lambda k: mods[:, l, 24 + k, j:j + 1], h2, "mg", th * 512, 512, 1.0 / D, (sq, rs), "C")
                        aT = sb([128, 32, 1024], BF16, s1)
                        kch = [(k * 128, 128) for k in range(KD)]
                        nxt = wload("w1", l, kch, 0, 512)
                        for pc in range(8):
                            wt, wk = nxt
                            if pc + 1 < 8:
                                nxt = wload("w1", l, kch, (pc + 1) * 512, 512)
                            for cc in range(4):
                                hc = pc * 4 + cc
                                for th in range(2):
                                    pi = nps()
                                    def mm(e, pi=pi, th=th, wt=wt, cc=cc):
                                        ins = None
                                        for k in range(KD):
                                            ins = e.matmul(ps[pi][:, :], lhsT=wt[:, k, cc * 128:(cc + 1) * 128], rhs=h2[:, k, th * 512:(th + 1) * 512], start=(k == 0), stop=(k == KD - 1))
                                        return ins
                                    P.op("pe", mm, rd=[wk, "mg"], wr=[psk[pi]])
                                    P.op("act", lambda e, pi=pi, hc=hc, th=th: e.activation(out=aT[:, hc, th * 512:(th + 1) * 512], in_=ps[pi][:, :], func=AF.Relu),
                                         rd=[psk[pi]], wr=[("aT", hc, th)])
                                    P.op("dve", lambda e, hc=hc, th=th: e.tensor_tensor(out=aT[:, hc, th * 512:(th + 1) * 512], in0=aT[:, hc, th * 512:(th + 1) * 512],
                                                                                       in1=aT[:, hc, th * 512:(th + 1) * 512], op=ALU.mult),
                                         rd=[("aT", hc, th)], wr=[("aT", hc, th)])
                        for dc in range(KD):
                            wts = [wload("w2", l, [((kg * 8 + k) * 128, 128) for k in range(8)], dc * 128, 128) for kg in range(3)]
                            pis = [nps(), nps()]
                            for kg in range(4):
                                if kg == 3:
                                    wt, wk = wload("w2", l, [((kg * 8 + k) * 128, 128) for k in range(8)], dc * 128, 128)
                                else:
                                    wt, wk = wts[kg]
                                for th in range(2):
                                    pi = pis[th]
                                    def mm(e, pi=pi, th=th, wt=wt, kg=kg):
                                        ins = None
                                        for k in range(8):
                                            ins = e.matmul(ps[pi][:, :], lhsT=wt[:, k, 0:128], rhs=aT[:, kg * 8 + k, th * 512:(th + 1) * 512],
                                                           start=(kg == 0 and k == 0), stop=(kg == 3 and k == 7))
                                        return ins
                                    P.op("pe", mm, rd=[wk] + [("aT", kg * 8 + k, th) for k in range(8)], wr=[psk[pi]])
                            for th in range(2):
                                pi = pis[th]
                                P.op("dve", lambda e, pi=pi, dc=dc, th=th: e.scalar_tensor_tensor(
                                    out=xr[:, dc, th * 512:(th + 1) * 512], in0=ps[pi][:, :], scalar=mods[:, l, 40 + dc, j:j + 1], in1=xr[:, dc, th * 512:(th + 1) * 512],
                                    op0=ALU.mult, op1=ALU.add), rd=[psk[pi], "xr"], wr=["xr"])
                        P.dma("sp", xT[j, :, t0:t0 + 1024].rearrange("(k p) t -> p k t", p=128), xr[:], rd=["xr"], wr=[("xT", j)])
                        P.barrier()

        with ExitStack() as s1:
            xt = [sb([128, KD, 512], F32, s1) for _ in range(2)]
            sqf = sb([128, KD, 512], BF16, s1)
            rsf = sb([128, 512], F32, s1)
            yo = [sb([128, D], F32, s1) for _ in range(2)]
            hf32 = sb([128, KD, 512], F32, s1)
            it = 0
            for j in jobs:
                for tt in range(NTT):
                    b = tt % 2
                    P.dma("sp", xt[b][:], xT[j, :, tt * 512:(tt + 1) * 512].rearrange("(k p) t -> p k t", p=128), rd=[("xT", j)], wr=["xt%d" % b])
                    rms_tile(s1, xt[b], "xt%d" % b, KD, lambda k: gfs[:, k:k + 1], lambda k: zero_t[:, 0:1], hf32, "hf32", 0, 512, 1.0 / D, (sqf, rsf), "F")
                    for tb in range(4):
                        ob = it % 2; it += 1
                        for hf in range(2):
                            pi = nps()
                            def tr(e, pi=pi, hf=hf, tb=tb):
                                ins = None
                                for q in range(4):
                                    k = hf * 4 + q
                                    ins = e.transpose(out=ps[pi][:, q * 128:(q + 1) * 128], in_=hf32[:, k, tb * 128:(tb + 1) * 128], identity=ident[:])
                                return ins
                            P.op("pe", tr, rd=["hf32", "ident"], wr=[psk[pi]])
                            if hf == 0:
                                P.op("act", lambda e, pi=pi, ob=ob: e.activation(out=yo[ob][:, 0:512], in_=ps[pi][:, :], func=AF.Copy), rd=[psk[pi]], wr=["yo%d0" % ob])
                            else:
                                P.op("dve", lambda e, pi=pi, ob=ob: e.tensor_copy(out=yo[ob][:, 512:1024], in_=ps[pi][:, :]), rd=[psk[pi]], wr=["yo%d1" % ob])
                        P.dma("sp", y_out[j, tt * 512 + tb * 128: tt * 512 + (tb + 1) * 128, :], yo[ob][:], rd=["yo%d0" % ob, "yo%d1" % ob], wr=["y"])
        P.barrier(["sp"])
    return nc


def prep_shared(inp):
    f = lambda a: np.ascontiguousarray(np.asarray(a, dtype=np.float32))
    sh = {}
    sh["ada_w"] = f(inp["ada_w"])
    sh["ada_bT"] = f(np.asarray(inp["ada_b"]).reshape(DEPTH, 48, 128).transpose(2, 0, 1))
    sh["g1T"] = f(np.asarray(inp["norm1_g"]).reshape(DEPTH, KD, 128).transpose(2, 0, 1))
    sh["g2T"] = f(np.asarray(inp["norm2_g"]).reshape(DEPTH, KD, 128).transpose(2, 0, 1))
    sh["gfT"] = f(np.asarray(inp["final_g"]).reshape(KD, 128).transpose(1, 0))
    w = np.asarray(inp["w_in"])
    za, bq, bk, bv, cu, cv, dcq, dckv, dkr, zg = np.split(w, np.cumsum([768, 384, 384, 384, 384, 384, 384, 320, 32])[:].tolist(), axis=-1)
    dkr_sw = np.concatenate([dkr[..., 16:32], dkr[..., 0:16]], -1)
    sh["w_in"] = f(np.concatenate([za, bv, cv, bq, bk, cu, dcq, dckv, dkr, dkr_sw, zg], -1))
    assert sh["w_in"].shape[-1] == C_END
    qg = np.asarray(inp["mla_q_norm_g"]).reshape(DEPTH, 3, 128).transpose(2, 0, 1)
    sh["qng"] = f(qg)
    kg = np.zeros((DEPTH, 384), np.float32)
    kg[:, :320] = np.asarray(inp["mla_kv_norm_g"])
    sh["kvg"] = f(kg.reshape(DEPTH, 3, 128).transpose(2, 0, 1))
    wq = np.asarray(inp["mla_w_uq"]).reshape(DEPTH, 384, 4, 96)
    sw = np.concatenate([np.zeros_like(wq[..., 0:64]), wq[..., 80:96], wq[..., 64:80]], -1)
    sh["w_uq"] = f(np.concatenate([wq.reshape(DEPTH, 384, 384), sw.reshape(DEPTH, 384, 384)], -1))
    wkv = np.asarray(inp["mla_w_ukv"]).reshape(DEPTH, 320, 4, 128)
    sh["w_ukv"] = f(np.concatenate([wkv[..., 0:64].reshape(DEPTH, 320, 256), wkv[..., 64:128].reshape(DEPTH, 320, 256)], -1))
    sh["lng"] = f(np.broadcast_to(np.asarray(inp["sgu_ln_g"])[None], (128, DEPTH, 384)))
    sh["lnb"] = f(np.broadcast_to(np.asarray(inp["sgu_ln_b"])[None], (128, DEPTH, 384)))
    sh["wsT"] = f(np.asarray(inp["sgu_w"]).transpose(0, 1, 3, 2))
    sh["sgb"] = f(np.broadcast_to(np.asarray(inp["sgu_b"])[None], (128, DEPTH, 4, 128)))
    sh["p_all"] = f(np.concatenate([np.asarray(inp["p_a"]), np.asarray(inp["p_b"]), np.asarray(inp["p_c"]), np.asarray(inp["p_d"])], 1))
    sh["w_o"] = f(inp["w_o"])
    sh["w1"] = f(inp["mlp_w1"])
    sh["w2"] = f(inp["mlp_w2"])
    sh["relb"] = f(np.broadcast_to(np.asarray(inp["rel_bias"]).reshape(1, 192), (128, 192)))
    sh["dftS"] = dft_table(T, 0, T)
    return sh


def kernel(**inp):
    flags = FLAGS
    sh = prep_shared(inp)
    xp = np.asarray(inp["x_prompt"], np.float32)
    xs = np.asarray(inp["x_sample"], np.float32)
    cp = np.asarray(inp["c_prompt"], np.float32)
    cs = np.asarray(inp["c_sample"], np.float32)
    in_maps = []
    for r in range(NCORES):
        m = dict(sh)
        m.update(host_tables(r))
        m["x"] = np.ascontiguousarray(np.stack([xp[0, r * T:(r + 1) * T], xs[2 * r], xs[2 * r + 1]], 0))
        c3 = np.stack([cp[0], cs[2 * r], cs[2 * r + 1]], 0)
        m["cT"] = np.ascontiguousarray(c3.reshape(3, KD, 128).transpose(2, 1, 0))
        m["dftP"] = dft_table(8 * T, r * T, T)
        in_maps.append(m)
    nc = build(flags)
    res = run_bass_kernel_spmd(nc, in_maps, core_ids=list(range(NCORES)))
    yp = np.zeros((1, 8 * T, D), np.float32)
    ys = np.zeros((16, T, D), np.float32)
    for r in range(NCORES):
        y = np.asarray(res.results[r]["y"], np.float32)
        yp[0, r * T:(r + 1) * T] = y[0]
        ys[2 * r] = y[1]
        ys[2 * r + 1] = y[2]
    return (yp, ys)
```

```python
import math
from contextlib import ExitStack
import numpy as np
import ml_dtypes
import concourse.bass as bass
import concourse.mybir as mybir
from concourse.bass_utils import run_bass_kernel_spmd

F32 = mybir.dt.float32
BF16 = mybir.dt.bfloat16
AF = mybir.ActivationFunctionType
ALU = mybir.AluOpType
AX = mybir.AxisListType
NPBF = ml_dtypes.bfloat16

NCORES = 8
D = 1024
KD = 8
T = 2048
NTT = 4
DEPTH = 4
EPS = 1e-6
NEG = -30000.0
FLAGS = {"depth": 4, "fnet": True, "dil": True, "sgu": True, "mla": True, "jobs": (0, 1, 2)}

C_ZA, C_BV, C_CV, C_BQ, C_BK, C_CU, C_CQ, C_CKV, C_KR, C_ZG, C_END = 0, 768, 1152, 1536, 1920, 2304, 2688, 3072, 3392, 3456, 7552
R_ZA, R_KD, R_VD, R_KM, R_VM, R_TOT = 0, 768, 768 + 390, 768 + 780, 768 + 780 + 384, 768 + 780 + 384 + 260


class Res:
    __slots__ = ("w", "r", "pw")

    def __init__(self):
        self.w = None
        self.r = {}
        self.pw = {}


class Prog:
    NDMA = 56

    def __init__(self, nc, stack):
        self.nc = nc
        self.eng = {"pe": nc.tensor, "act": nc.scalar, "dve": nc.vector, "pool": nc.gpsimd, "sp": nc.sync}
        self.sem = {k: stack.enter_context(nc.semaphore("sem_" + k)) for k in self.eng}
        self.dsem = [stack.enter_context(nc.semaphore("dsem%d" % i)) for i in range(self.NDMA)]
        self.cnt = {k: 0 for k in self.eng}
        self.dval = [0] * self.NDMA
        self.drr = 0
        self.waited = {k: {} for k in self.eng}
        self._res = {}

    def res(self, key):
        r = self._res.get(key)
        if r is None:
            r = Res()
            self._res[key] = r
        return r

    def _semobj(self, key):
        return self.dsem[key[1]] if isinstance(key, tuple) else self.sem[key]

    def _deps(self, rd, wr, par=False):
        deps = {}
        for r in rd:
            r = self.res(r)
            if r.w is not None and deps.get(r.w[0], 0) < r.w[1]:
                deps[r.w[0]] = r.w[1]
            for k, v in r.pw.items():
                if deps.get(k, 0) < v:
                    deps[k] = v
        for r in wr:
            r = self.res(r)
            if r.w is not None and deps.get(r.w[0], 0) < r.w[1]:
                deps[r.w[0]] = r.w[1]
            if not par:
                for k, v in r.pw.items():
                    if deps.get(k, 0) < v:
                        deps[k] = v
            for k, v in r.r.items():
                if deps.get(k, 0) < v:
                    deps[k] = v
        return deps

    def _wait(self, e, deps):
        w = self.waited[e]
        for k, v in deps.items():
            if e == "pe" and k == "pe":
                continue
            if w.get(k, 0) >= v:
                continue
            self.eng[e].wait_ge(self._semobj(k), v)
            w[k] = v

    def _mark(self, rd, wr, key, val, par=False):
        for r in rd:
            r = self.res(r)
            if r.r.get(key, 0) < val:
                r.r[key] = val
        for r in wr:
            r = self.res(r)
            if par:
                if r.pw.get(key, 0) < val:
                    r.pw[key] = val
            else:
                r.w = (key, val)
                r.r = {}
                r.pw = {}

    def op(self, e, fn, rd=(), wr=()):
        self._wait(e, self._deps(rd, wr))
        ins = fn(self.eng[e])
        self.cnt[e] += 1
        ins.then_inc(self.sem[e], 1)
        self._mark(rd, wr, e, self.cnt[e])
        return ins

    def _slot(self, q, rd, wr, par=False):
        i = self.drr
        self.drr = (self.drr + 1) % self.NDMA
        deps = self._deps(rd, wr, par)
        if self.dval[i] > 0:
            deps[("d", i)] = max(deps.get(("d", i), 0), self.dval[i])
        self._wait(q, deps)
        return i

    def dma(self, q, out, in_, rd=(), wr=(), par=False, **kw):
        i = self._slot(q, rd, wr, par)
        ins = self.eng[q].dma_start(out=out, in_=in_, **kw)
        self.dval[i] += 16
        ins.then_inc(self.dsem[i], 16)
        self._mark(rd, wr, ("d", i), self.dval[i], par)
        return ins

    def custom(self, q, fn, rd=(), wr=(), inc=1):
        i = self._slot(q, rd, wr)
        ins = fn(self.eng[q])
        self.dval[i] += inc
        ins.then_inc(self.dsem[i], inc)
        self._mark(rd, wr, ("d", i), self.dval[i])
        return ins

    def barrier(self, engines=None):
        deps = {k: v for k, v in self.cnt.items() if v > 0}
        for i in range(self.NDMA):
            if self.dval[i] > 0:
                deps[("d", i)] = self.dval[i]
        for e in (engines or list(self.eng)):
            d = dict(deps)
            d.pop(e, None)
            self._wait(e, d)


def _t5_bucket_np(rel):
    half, max_exact = 16, 8
    ret = np.where(rel > 0, half, 0)
    n = np.abs(rel)
    large = max_exact + (np.log(np.maximum(n, max_exact).astype(np.float32) / max_exact)
                         / math.log(1024 / max_exact) * (half - max_exact)).astype(np.int32)
    large = np.minimum(large, half - 1)
    return ret + np.where(n < max_exact, n, large)


def host_tables(r):
    tb = {}
    tb["ident"] = np.eye(128, dtype=np.float32)
    half = 16
    inv = 10000.0 ** (-np.arange(half, dtype=np.float64) / half)
    rp = np.zeros((32, 2, 2, T), np.float32)
    for kind in range(2):
        pos = (np.arange(T) + (T * r if kind == 0 else 0)).astype(np.float64)
        ang = pos[None, :] * inv[:, None]
        cs, sn = np.cos(ang), np.sin(ang)
        rp[:, 0, kind] = np.concatenate([cs, cs], 0)
        rp[:, 1, kind] = np.concatenate([-sn, sn], 0)
    tb["rope"] = rp.astype(NPBF)
    kk = np.arange(256)[:, None]
    qq = np.arange(128)[None, :]
    rel = kk - 64 - qq
    oh = np.zeros((3, 128, 32, 2, 128), np.float32)
    for g, dil in enumerate((1, 4, 16)):
        bk = _t5_bucket_np(rel * dil)
        for b in range(32):
            m = (bk == b).astype(np.float32)
            oh[g, :, b, 0] = m[0:128]
            oh[g, :, b, 1] = m[128:256]
    tb["oh"] = oh.astype(NPBF)
    band = np.where(np.abs(rel) <= 64, 0.0, NEG).astype(np.float32)
    tb["band"] = np.stack([band[0:128], band[128:256]], 1).copy()
    j = np.arange(192)
    ang = 2 * np.pi * np.outer(j, j) / 192.0
    ch = np.zeros((2, 192, 192), np.float32)
    ch[0] = np.cos(ang) / math.sqrt(192)
    ch[1] = -np.sin(ang) / math.sqrt(192)
    tb["chdft"] = ch
    return tb


_DFT_CACHE = {}


def dft_table(S, k0, nk):
    key = (S, k0, nk)
    if key not in _DFT_CACHE:
        n = np.arange(S, dtype=np.int64)[:, None]
        k = (k0 + np.arange(nk, dtype=np.int64))[None, :]
        ph = ((n * k) % S).astype(np.float64) * (2 * np.pi / S)
        out = np.empty((2, S, nk), NPBF)
        out[0] = (np.cos(ph) / math.sqrt(S)).astype(NPBF)
        out[1] = (np.sin(ph) / math.sqrt(S)).astype(NPBF)
        _DFT_CACHE[key] = out
    return _DFT_CACHE[key]


def build(flags):
    depth = flags["depth"]
    jobs = flags["jobs"]
    nc = bass.Bass("TRN2", target_bir_lowering=False)

    def din(name, shape, dt=F32):
        return nc.dram_tensor(name, list(shape), dt, kind="ExternalInput").ap()

    x_in = din("x", [3, T, D])
    cT_in = din("cT", [128, KD, 3])
    relb_in = din("relb", [128, 192])
    ada_w = din("ada_w", [DEPTH, D, 6 * D])
    ada_bT = din("ada_bT", [128, DEPTH, 48])
    g1T = din("g1T", [128, DEPTH, KD])
    g2T = din("g2T", [128, DEPTH, KD])
    gfT = din("gfT", [128, KD])
    w_in = din("w_in", [DEPTH, D, C_END])
    qng = din("qng", [128, DEPTH, 3])
    kvg = din("kvg", [128, DEPTH, 3])
    w_uq = din("w_uq", [DEPTH, 384, 768])
    w_ukv = din("w_ukv", [DEPTH, 320, 512])
    lng = din("lng", [128, DEPTH, 384])
    lnb = din("lnb", [128, DEPTH, 384])
    wsT = din("wsT", [DEPTH, 4, 128, 128])
    sgb = din("sgb", [128, DEPTH, 4, 128])
    p_all = din("p_all", [DEPTH, 1536, D])
    w_o = din("w_o", [DEPTH, D, D])
    w1 = din("w1", [DEPTH, D, 4 * D])
    w2 = din("w2", [DEPTH, 4 * D, D])
    ident_in = din("ident", [128, 128])
    rope_in = din("rope", [32, 2, 2, T], BF16)
    oh_in = din("oh", [3, 128, 32, 2, 128], BF16)
    band_in = din("band", [128, 2, 128])
    chdft_in = din("chdft", [2, 192, 192])
    dftS = din("dftS", [2, T, T], BF16)
    dftP = din("dftP", [2, 8 * T, T], BF16)
    y_out = nc.dram_tensor("y", [3, T, D], F32, kind="ExternalOutput").ap()

    def dscr(name, shape, dt=BF16):
        return nc.dram_tensor(name, list(shape), dt).ap()

    xT = dscr("xT", [3, D, T], F32)
    W16 = {"w_in": dscr("w16_in", [DEPTH, D, C_END]), "p_all": dscr("w16_p", [DEPTH, 1536, D]), "w_o": dscr("w16_o", [DEPTH, D, D]),
           "w1": dscr("w16_1", [DEPTH, D, 4 * D]), "w2": dscr("w16_2", [DEPTH, 4 * D, D])}
    W32 = {"w_in": w_in, "p_all": p_all, "w_o": w_o, "w1": w1, "w2": w2}
    PK = dscr("PK", [R_TOT, T])
    AGB = dscr("AGB", [10 * R_TOT, T])
    ZA = {1: dscr("ZA1", [768, T]), 2: dscr("ZA2", [768, T])}
    QD = [dscr("QD%d" % j, [390, T]) for j in range(3)]
    KDs = [dscr("KD%d" % j, [390, 2 * T]) for j in range(3)]
    VDs = [dscr("VD%d" % j, [2 * T + 16, 390]) for j in range(3)]
    CU = [dscr("CU%d" % j, [384, T]) for j in range(3)]
    CV = [dscr("CV%d" % j, [T, 384]) for j in range(3)]
    CQ = [dscr("CQ%d" % j, [768, T]) for j in range(3)]
    GT = [dscr("GT%d" % j, [4 * D, T]) for j in range(3)]
    MT = [dscr("MT%d" % j, [1536, T]) for j in range(3)]
    KM = {1: dscr("KM1", [384, T]), 2: dscr("KM2", [384, T])}
    VM = {1: dscr("VM1", [T, 260]), 2: dscr("VM2", [T, 260])}
    QM = [dscr("QM%d" % j, [384, T]) for j in range(3)]
    DACC = [dscr("DACC%d" % j, [3, T + 16, 130], F32) for j in range(3)]

    with ExitStack() as st:
        PID = [nc.gpsimd.partition_id()]
        P = Prog(nc, st)
        uid = [0]

        def sb(shape, dt, stack=st, name=None):
            uid[0] += 1
            return stack.enter_context(nc.sbuf_tensor(name or ("t%d" % uid[0]), list(shape), dt))

        ident = sb([128, 128], F32)
        ones_bf = sb([128, 128], BF16)
        eps_t = sb([128, 1], F32)
        zero_t = sb([128, 1], F32)
        mods = sb([128, DEPTH, 48, 3], F32)
        a1 = sb([128, DEPTH, KD, 3], F32)
        a2 = sb([128, DEPTH, KD, 3], F32)
        g1s = sb([128, DEPTH, KD], F32)
        g2s = sb([128, DEPTH, KD], F32)
        gfs = sb([128, KD], F32)
        qngs = sb([128, DEPTH, 3], F32)
        kvgs = sb([128, DEPTH, 3], F32)
        biasT = sb([128, 6, 2, 128], BF16)
        ident_bf = sb([128, 128], BF16)
        cf = [sb([128, 2048], F32) for _ in range(2)]
        wb = [sb([128, 8, 512], BF16) for _ in range(4)]
        cvb = [sb([128, 2048], BF16) for _ in range(2)]
        ps = [st.enter_context(nc.psum_tensor("ps%d" % i, [128, 512], F32)) for i in range(8)]
        psk = ["ps%d" % i for i in range(8)]
        psrr = [0]

        def nps():
            i = psrr[0]
            psrr[0] = (i + 1) % 8
            return i

        P.dma("sp", ident[:], ident_in, wr=["ident"])
        P.op("pool", lambda e: e.memset(ones_bf[:], 1.0), wr=["ones"])
        P.op("pool", lambda e: e.memset(eps_t[:], EPS), wr=["eps"])
        P.op("pool", lambda e: e.memset(zero_t[:], 0.0), wr=["zero"])
        P.op("pool", lambda e: e.tensor_copy(out=ident_bf[:], in_=ident[:]), rd=["ident"], wr=["identbf"])
        P.dma("sp", g1s[:], g1T, wr=["g1s"])
        P.dma("sp", g2s[:], g2T, wr=["g2s"])
        P.dma("sp", gfs[:], gfT, wr=["gfs"])
        P.dma("sp", qngs[:], qng, wr=["qngs"])
        P.dma("sp", kvgs[:], kvg, wr=["kvgs"])

        wst = {"f": 0, "b": 0}

        def wload(wname, l, kch, c0, ncols):
            src = W16[wname][l]
            bi = wst["b"]; wst["b"] = (bi + 1) % 4
            bk = "wb%d" % bi
            full = all(n == 128 for _, n in kch) and all(kch[i][0] == kch[0][0] + 128 * i for i in range(len(kch)))
            if full:
                r0 = kch[0][0]
                P.dma("sp", wb[bi][:, 0:len(kch), 0:ncols],
                      src[r0:r0 + 128 * len(kch), c0:c0 + ncols].rearrange("(k p) c -> p k c", p=128), rd=[("W16", wname, l)], wr=[bk])
            else:
                for i, (r0, n) in enumerate(kch):
                    P.dma("sp", wb[bi][0:n, i, 0:ncols], src[r0:r0 + n, c0:c0 + ncols], rd=[("W16", wname, l)], wr=[bk])
            return wb[bi], bk

        cvi = [0]

        def convert_weights(l):
            for wname in ("w_in", "p_all", "w_o", "w1", "w2"):
                src, dst = W32[wname][l], W16[wname][l]
                nr, ncol = src.shape[0], src.shape[1]
                npc = (ncol + 2047) // 2048
                cw = (ncol + npc - 1) // npc
                for r0 in range(0, nr, 128):
                    for c0 in range(0, ncol, cw):
                        n = min(cw, ncol - c0)
                        i = cvi[0]; cvi[0] += 1
                        fi = i % 2
                        fk, ck = "cf%d" % fi, "cvb%d" % fi
                        fv = cf[fi]
                        P.dma("sp", fv[:, 0:n], src[r0:r0 + 128, c0:c0 + n], wr=[fk])
                        if i % 2 == 0:
                            P.op("dve", lambda e, fi=fi, n=n, fv=fv: e.tensor_copy(out=cvb[fi][:, 0:n], in_=fv[:, 0:n]), rd=[fk], wr=[ck])
                        else:
                            P.op("act", lambda e, fi=fi, n=n, fv=fv: e.activation(out=cvb[fi][:, 0:n], in_=fv[:, 0:n], func=AF.Copy), rd=[fk], wr=[ck])
                        P.dma("sp", dst[r0:r0 + 128, c0:c0 + n], cvb[fi][:, 0:n], rd=[ck], wr=[("W16", wname, l)])

        with ExitStack() as s1:
            wf = [sb([128, 8, 512], F32, s1) for _ in range(2)]
            cT = sb([128, KD, 3], F32, s1)
            sc = sb([128, KD, 3], F32, s1)
            abT = sb([128, DEPTH, 48], F32, s1)
            P.dma("sp", cT[:], cT_in, wr=["cT"])
            P.dma("sp", abT[:], ada_bT, wr=["abT"])
            P.op("act", lambda e: e.activation(out=sc[:], in_=cT[:], func=AF.Silu), rd=["cT"], wr=["sc"])
            for l in range(depth):
                for pc in range(12):
                    fi = wst["f"]; wst["f"] = (fi + 1) % 2
                    fk = "wf%d" % fi
                    P.dma("sp", wf[fi][:], ada_w[l, :, pc * 512:(pc + 1) * 512].rearrange("(k p) c -> p k c", p=128), wr=[fk])
                    pi = nps()
                    def mm(e, fi=fi, pi=pi):
                        ins = None
                        for cc in range(4):
                            for k in range(KD):
                                ins = e.matmul(ps[pi][:, cc * 3:cc * 3 + 3], lhsT=wf[fi][:, k, cc * 128:(cc + 1) * 128], rhs=sc[:, k, :],
                                               start=(k == 0), stop=(k == KD - 1))
                        return ins
                    P.op("pe", mm, rd=[fk, "sc"], wr=[psk[pi]])
                    P.op("dve", lambda e, pi=pi, pc=pc, l=l: e.tensor_copy(
                        out=mods[:, l, pc * 4:(pc + 1) * 4, :], in_=ps[pi][:, 0:12].rearrange("p (c j) -> p c j", j=3)),
                        rd=[psk[pi]], wr=["mods"])
                for j in range(3):
                    P.op("dve", lambda e, l=l, j=j: e.tensor_tensor(out=mods[:, l, :, j], in0=mods[:, l, :, j], in1=abT[:, l, :], op=ALU.add),
                         rd=["mods", "abT"], wr=["mods"])
                    P.op("dve", lambda e, l=l, j=j: e.scalar_tensor_tensor(out=a1[:, l, :, j], in0=mods[:, l, 8:16, j], scalar=1.0, in1=g1s[:, l, :],
                                                                          op0=ALU.add, op1=ALU.mult), rd=["mods", "g1s"], wr=["a1"])
                    P.op("dve", lambda e, l=l, j=j: e.scalar_tensor_tensor(out=a2[:, l, :, j], in0=mods[:, l, 32:40, j], scalar=1.0, in1=g2s[:, l, :],
                                                                          op0=ALU.add, op1=ALU.mult), rd=["mods", "g2s"], wr=["a2"])
            P.barrier()

        with ExitStack() as s1:
            xin = [sb([128, D], F32, s1) for _ in range(2)]
            xo = [sb([128, KD, 128], F32, s1) for _ in range(2)]
            it = 0
            for j in jobs:
                for tb in range(T // 128):
                    b = it % 2; it += 1
                    P.dma("sp", xin[b][:], x_in[j, tb * 128:(tb + 1) * 128, :], wr=["xin%d" % b])
                    for hf in range(2):
                        pi = nps()
                        def tr(e, b=b, hf=hf, pi=pi):
                            ins = None
                            for q in range(4):
                                k = hf * 4 + q
                                ins = e.transpose(out=ps[pi][:, q * 128:(q + 1) * 128], in_=xin[b][:, k * 128:(k + 1) * 128], identity=ident[:])
                            return ins
                        P.op("pe", tr, rd=["xin%d" % b, "ident"], wr=[psk[pi]])
                        eng = "act" if hf == 0 else "dve"
                        if eng == "act":
                            P.op("act", lambda e, b=b, hf=hf, pi=pi: e.activation(out=xo[b][:, hf * 4:hf * 4 + 4, :], in_=ps[pi][:].rearrange("p (k t) -> p k t", t=128), func=AF.Copy),
                                 rd=[psk[pi]], wr=["xo%d%d" % (b, hf)])
                        else:
                            P.op("dve", lambda e, b=b, hf=hf, pi=pi: e.tensor_copy(out=xo[b][:, hf * 4:hf * 4 + 4, :], in_=ps[pi][:].rearrange("p (k t) -> p k t", t=128)),
                                 rd=[psk[pi]], wr=["xo%d%d" % (b, hf)])
                    P.dma("sp", xT[j, :, tb * 128:(tb + 1) * 128].rearrange("(k p) t -> p k t", p=128), xo[b][:],
                          rd=["xo%d0" % b, "xo%d1" % b], wr=[("xT", j)])
            P.barrier()

        def rms_tile(s1, src_tile, src_key, nk, a_ap_fn, b_ap_fn, dst, dst_key, dst_off, ncol, inv_n, tmp, tmpk):
            sq, rs = tmp
            P.op("act", lambda e: e.activation(out=sq[:, 0:nk, 0:ncol], in_=src_tile[:, 0:nk, 0:ncol], func=AF.Square), rd=[src_key], wr=[tmpk + "sq"])
            pi = nps()
            def mm(e):
                ins = None
                for k in range(nk):
                    ins = e.matmul(ps[pi][:, 0:ncol], lhsT=ones_bf[:], rhs=sq[:, k, 0:ncol], start=(k == 0), stop=(k == nk - 1))
                return ins
            P.op("pe", mm, rd=[tmpk + "sq", "ones"], wr=[psk[pi]])
            P.op("act", lambda e: e.activation(out=rs[:, 0:ncol], in_=ps[pi][:, 0:ncol], func=AF.Sqrt, scale=inv_n, bias=eps_t[:, 0:1]), rd=[psk[pi], "eps"], wr=[tmpk + "rs"])
            P.op("dve", lambda e: e.reciprocal(out=rs[:, 0:ncol], in_=rs[:, 0:ncol]), rd=[tmpk + "rs"], wr=[tmpk + "rs"])
            for k in range(nk):
                P.op("dve", lambda e, k=k: e.tensor_tensor(out=src_tile[:, k, 0:ncol], in0=src_tile[:, k, 0:ncol], in1=rs[:, 0:ncol], op=ALU.mult),
                     rd=[src_key, tmpk + "rs"], wr=[src_key])
                P.op("act", lambda e, k=k: e.activation(out=dst[:, k, dst_off:dst_off + ncol], in_=src_tile[:, k, 0:ncol], func=AF.Identity,
                                                        scale=a_ap_fn(k), bias=b_ap_fn(k)), rd=[src_key], wr=[dst_key])

        negM = sb([128, 8], F32)

        def sumsq_max(s1, src_fn, K, ntiles, dst_ap, rdkeys, tag, tmp=None):
            if tmp is None:
                sqb = sb([128, 512], BF16, s1)
                mx = sb([128, 32], F32, s1)
            else:
                sqb, mx = tmp
            for t in range(ntiles):
                P.op("act", lambda e, t=t: e.activation(out=sqb[0:K, :], in_=src_fn(t), func=AF.Square), rd=rdkeys, wr=[tag + "sqb"])
                pi = nps()
                P.op("pe", lambda e, pi=pi: e.matmul(ps[pi][:, :], lhsT=ones_bf[0:K, :], rhs=sqb[0:K, :], start=True, stop=True), rd=[tag + "sqb", "ones"], wr=[psk[pi]])
                P.op("dve", lambda e, pi=pi, t=t: e.tensor_reduce(out=mx[:, t:t + 1], in_=ps[pi][:, :], axis=AX.X, op=ALU.max), rd=[psk[pi]], wr=[tag + "mx"])
            P.op("dve", lambda e: e.tensor_reduce(out=dst_ap, in_=mx[:, 0:ntiles], axis=AX.X, op=ALU.max), rd=[tag + "mx"], wr=[tag + "dst"])

        def do_sgu(l, j):
            with ExitStack() as s1:
                lg = sb([128, 384], F32, s1); lb = sb([128, 384], F32, s1)
                wsf = sb([128, 4, 128], F32, s1); wsb = sb([128, 4, 128], BF16, s1)
                sbt = sb([128, 4, 128], F32, s1)
                P.dma("sp", lg[:], lng[:, l, :], wr=["lg"]); P.dma("sp", lb[:], lnb[:, l, :], wr=["lb"])
                P.dma("sp", wsf[:], wsT[l].rearrange("h q p -> q h p"), wr=["wsf"])
                P.dma("sp", sbt[:], sgb[:, l, :, :], wr=["sbt"])
                P.op("pool", lambda e: e.tensor_copy(out=wsb[:], in_=wsf[:]), rd=["wsf"], wr=["wsb"])
                cvt = [sb([128, 384], BF16, s1) for _ in range(2)]
                cut = [sb([128, 4, 128], BF16, s1) for _ in range(2)]
                vn = [sb([128, 384], F32, s1) for _ in range(2)]
                vnb = [sb([128, 384], BF16, s1) for _ in range(2)]
                stt = [sb([128, 8], F32, s1) for _ in range(2)]
                mxf = [sb([128, 4, 128], F32, s1) for _ in range(2)]
                ob = [sb([128, 4, 128], BF16, s1) for _ in range(2)]
                for n in range(T // 128):
                    b = n % 2
                    tsl = slice(n * 128, (n + 1) * 128)
                    P.dma("sp", cvt[b][:], CV[j][tsl, :], rd=[("CV", j)], wr=["cvt%d" % b])
                    P.dma("sp", cut[b][0:96], CU[j][:, tsl].rearrange("(h c) t -> c h t", h=4), rd=[("CU", j)], wr=["cut%d" % b])
                    P.op("dve", lambda e, b=b: e.bn_stats(out=stt[b][:, 0:6], in_=cvt[b][:]), rd=["cvt%d" % b], wr=["stt%d" % b])
                    P.op("dve", lambda e, b=b: e.bn_aggr(out=stt[b][:, 6:8], in_=stt[b][:, 0:6]), rd=["stt%d" % b], wr=["stt%d" % b])
                    P.op("act", lambda e, b=b: e.activation(out=stt[b][:, 7:8], in_=stt[b][:, 7:8], func=AF.Sqrt, bias=eps_t[:, 0:1]), rd=["stt%d" % b, "eps"], wr=["stt%d" % b])
                    P.op("dve", lambda e, b=b: e.reciprocal(out=stt[b][:, 7:8], in_=stt[b][:, 7:8]), rd=["stt%d" % b], wr=["stt%d" % b])
                    P.op("dve", lambda e, b=b: e.tensor_scalar(out=vn[b][:], in0=cvt[b][:], scalar1=stt[b][:, 6:7], scalar2=stt[b][:, 7:8], op0=ALU.subtract, op1=ALU.mult),
                         rd=["cvt%d" % b, "stt%d" % b], wr=["vn%d" % b])
                    P.op("dve", lambda e, b=b: e.tensor_tensor(out=vn[b][:], in0=vn[b][:], in1=lg[:], op=ALU.mult), rd=["vn%d" % b, "lg"], wr=["vn%d" % b])
                    P.op("dve", lambda e, b=b: e.tensor_tensor(out=vnb[b][:], in0=vn[b][:], in1=lb[:], op=ALU.add), rd=["vn%d" % b, "lb"], wr=["vnb%d" % b])
                    pi = nps()
                    def mm(e, pi=pi, b=b):
                        ins = None
                        for h in range(4):
                            ins = e.matmul(ps[pi][0:96, h * 128:(h + 1) * 128], lhsT=vnb[b][:, h * 96:(h + 1) * 96], rhs=wsb[:, h, :], start=True, stop=True)
                        return ins
                    P.op("pe", mm, rd=["vnb%d" % b, "wsb"], wr=[psk[pi]])
                    P.op("dve", lambda e, pi=pi, b=b: e.tensor_tensor(out=mxf[b][0:96], in0=ps[pi][0:96, :].rearrange("p (h t) -> p h t", h=4), in1=sbt[0:96], op=ALU.add),
                         rd=[psk[pi], "sbt"], wr=["mxf%d" % b])
                    P.op("pool", lambda e, b=b: e.tensor_tensor(out=ob[b][0:96], in0=mxf[b][0:96], in1=cut[b][0:96], op=ALU.mult), rd=["mxf%d" % b, "cut%d" % b], wr=["ob%d" % b])
                    P.dma("sp", MT[j][896:1280, tsl].rearrange("(h c) t -> c h t", h=4), ob[b][0:96], rd=["ob%d" % b], wr=[("MT", j)], par=True)
            P.barrier()

        def do_mla(l, j):
            nk_t = 64 if j == 0 else 16
            nkt = 128 if j == 0 else 16
            kind = 0 if j == 0 else 1
            kdst = PK[R_KM:R_KM + 384, :] if j == 0 else KM[j]
            vdst = (PK[R_VM:R_VM + 260, :].rearrange("r c -> (r c)").rearrange("(t c) -> t c", c=260)) if j == 0 else VM[j]
            with ExitStack() as s1:
                with ExitStack() as s2:
                    cq = sb([128, 6, T], BF16, s2)
                    ksw = sb([128, T], BF16, s2)
                    rp = sb([128, 2, T], BF16, s2)
                    wqf = sb([128, 3, 768], F32, s2); wqb = sb([128, 3, 768], BF16, s2)
                    wkf = sb([128, 3, 512], F32, s2); wkb = sb([128, 3, 512], BF16, s2)
                    sqb = sb([128, 6, 512], BF16, s2)
                    rq = sb([128, 512], F32, s2); rk = sb([128, 512], F32, s2)
                    qf = sb([128, 512], F32, s2); t1 = sb([128, 512], F32, s2); t2 = sb([128, 512], F32, s2)
                    qst = sb([128, 4, 512], BF16, s2)
                    kst = sb([128, 4, 512], BF16, s2)
                    kpe = sb([128, 512], BF16, s2)
                    rkt = sb([128, 4], F32, s2)
                    vst = sb([128, 4, 4, 65], BF16, s2)
                    P.dma("sp", cq[:], CQ[j].rearrange("(k p) t -> p k t", p=128), rd=[("CQ", j)], wr=["cq"])
                    P.dma("sp", ksw[64:96, :], CQ[j][736:768, :], rd=[("CQ", j)], wr=["ksw"])
                    P.dma("sp", rp[64:96], rope_in[:, :, kind, :], wr=["rp"])
                    P.dma("sp", wqf[:], w_uq[l].rearrange("(k p) c -> p k c", p=128), wr=["wqf"])
                    P.dma("sp", wkf[:, 0:2, :], w_ukv[l, 0:256, :].rearrange("(k p) c -> p k c", p=128), wr=["wkf"])
                    P.dma("sp", wkf[0:64, 2, :], w_ukv[l, 256:320, :], wr=["wkf"])
                    P.op("pool", lambda e: e.memset(vst[:], 1.0), wr=["vst"])
                    for k in range(3):
                        P.op("dve", lambda e, k=k: e.tensor_scalar(out=wqb[:, k, :], in0=wqf[:, k, :], scalar1=qngs[:, l, k:k + 1], scalar2=96.0 ** -0.5, op0=ALU.mult, op1=ALU.mult),
                             rd=["wqf", "qngs"], wr=["wqb"])
                        np_ = 128 if k < 2 else 64
                        P.op("dve", lambda e, k=k, np_=np_: e.tensor_scalar(out=wkb[0:np_, k, :], in0=wkf[0:np_, k, :], scalar1=kvgs[0:np_, l, k:k + 1], scalar2=1.0, op0=ALU.mult, op1=ALU.mult),
                             rd=["wkf", "kvgs"], wr=["wkb"])
                    for tt in range(NTT):
                        tsl = slice(tt * 512, (tt + 1) * 512)
                        P.op("act", lambda e, tsl=tsl: e.activation(out=sqb[:], in_=cq[:, :, tsl], func=AF.Square), rd=["cq"], wr=["sqb"])
                        pq, pk = nps(), nps()
                        def mmq(e, pq=pq):
                            ins = None
                            for k in range(3):
                                ins = e.matmul(ps[pq][:, :], lhsT=ones_bf[:], rhs=sqb[:, k, :], start=(k == 0), stop=(k == 2))
                            return ins
                        P.op("pe", mmq, rd=["sqb", "ones"], wr=[psk[pq]])
                        def mmk(e, pk=pk):
                            e.matmul(ps[pk][:, :], lhsT=ones_bf[:], rhs=sqb[:, 3, :], start=True, stop=False)
                            e.matmul(ps[pk][:, :], lhsT=ones_bf[:], rhs=sqb[:, 4, :], start=False, stop=False)
                            return e.matmul(ps[pk][:, :], lhsT=ones_bf[0:64, :], rhs=sqb[0:64, 5, :], start=False, stop=True)
                        P.op("pe", mmk, rd=["sqb", "ones"], wr=[psk[pk]])
                        P.op("act", lambda e, pq=pq: e.activation(out=rq[:], in_=ps[pq][:, :], func=AF.Sqrt, scale=1.0 / 384, bias=eps_t[:, 0:1]), rd=[psk[pq], "eps"], wr=["rq"])
                        P.op("dve", lambda e: e.reciprocal(out=rq[:], in_=rq[:]), rd=["rq"], wr=["rq"])
                        P.op("act", lambda e, pk=pk: e.activation(out=rk[:], in_=ps[pk][:, :], func=AF.Sqrt, scale=1.0 / 320, bias=eps_t[:, 0:1]), rd=[psk[pk], "eps"], wr=["rk"])
                        P.op("dve", lambda e: e.reciprocal(out=rk[:], in_=rk[:]), rd=["rk"], wr=["rk"])
                        pv = nps()
                        def mmv(e, pv=pv):
                            ins = None
                            for tb in range(4):
                                e.matmul(ps[pv][:, tb:tb + 1], lhsT=sqb[:, 3, tb * 128:(tb + 1) * 128], rhs=ones_bf[:, 0:1], start=True, stop=False)
                                e.matmul(ps[pv][:, tb:tb + 1], lhsT=sqb[:, 4, tb * 128:(tb + 1) * 128], rhs=ones_bf[:, 0:1], start=False, stop=False)
                                ins = e.matmul(ps[pv][:, tb:tb + 1], lhsT=sqb[0:64, 5, tb * 128:(tb + 1) * 128], rhs=ones_bf[0:64, 0:1], start=False, stop=True)
                            return ins
                        P.op("pe", mmv, rd=["sqb", "ones"], wr=[psk[pv]])
                        P.op("act", lambda e, pv=pv: e.activation(out=rkt[:], in_=ps[pv][:, 0:4], func=AF.Sqrt, scale=1.0 / 320, bias=eps_t[:, 0:1]), rd=[psk[pv], "eps"], wr=["rkt"])
                        P.op("dve", lambda e: e.reciprocal(out=rkt[:], in_=rkt[:]), rd=["rkt"], wr=["rkt"])
                        P.op("dve", lambda e, tsl=tsl: e.tensor_tensor(out=t1[64:96], in0=cq[64:96, 5, tsl], in1=rp[64:96, 0, tsl], op=ALU.mult), rd=["cq", "rp"], wr=["t1"])
                        P.op("dve", lambda e, tsl=tsl: e.tensor_tensor(out=t2[64:96], in0=ksw[64:96, tsl], in1=rp[64:96, 1, tsl], op=ALU.mult), rd=["ksw", "rp"], wr=["t2"])
                        P.op("dve", lambda e: e.tensor_tensor(out=kpe[64:96], in0=t1[64:96], in1=t2[64:96], op=ALU.add), rd=["t1", "t2"], wr=["kpe"])
                        for h in range(4):
                            p1, p2 = nps(), nps()
                            def mq(e, p1=p1, p2=p2, h=h, tsl=tsl):
                                ins = None
                                for k in range(3):
                                    e.matmul(ps[p1][0:96, :], lhsT=wqb[:, k, h * 96:(h + 1) * 96], rhs=cq[:, k, tsl], start=(k == 0), stop=(k == 2))
                                for k in range(3):
                                    ins = e.matmul(ps[p2][0:96, :], lhsT=wqb[:, k, 384 + h * 96:384 + (h + 1) * 96], rhs=cq[:, k, tsl], start=(k == 0), stop=(k == 2))
                                return ins
                            P.op("pe", mq, rd=["wqb", "cq"], wr=[psk[p1], psk[p2]])
                            P.op("dve", lambda e, p1=p1: e.tensor_tensor(out=qf[0:96], in0=ps[p1][0:96, :], in1=rq[0:96], op=ALU.mult), rd=[psk[p1], "rq"], wr=["qf"])
                            P.op("dve", lambda e, p2=p2: e.tensor_tensor(out=t1[64:96], in0=ps[p2][64:96, :], in1=rq[64:96], op=ALU.mult), rd=[psk[p2], "rq"], wr=["t1"])
                            P.op("dve", lambda e, tsl=tsl: e.tensor_tensor(out=t1[64:96], in0=t1[64:96], in1=rp[64:96, 1, tsl], op=ALU.mult), rd=["t1", "rp"], wr=["t1"])
                            P.op("dve", lambda e, tsl=tsl: e.tensor_tensor(out=t2[64:96], in0=qf[64:96], in1=rp[64:96, 0, tsl], op=ALU.mult), rd=["qf", "rp"], wr=["t2"])
                            P.op("dve", lambda e: e.tensor_tensor(out=qf[64:96], in0=t1[64:96], in1=t2[64:96], op=ALU.add), rd=["t1", "t2"], wr=["qf"])
                            P.op("act", lambda e, h=h: e.activation(out=qst[0:96, h, :], in_=qf[0:96], func=AF.Copy), rd=["qf"], wr=["qst"])
                            p3 = nps()
                            def mk(e, p3=p3, h=h, tsl=tsl):
                                e.matmul(ps[p3][0:64, :], lhsT=wkb[:, 0, h * 64:(h + 1) * 64], rhs=cq[:, 3, tsl], start=True, stop=False)
                                e.matmul(ps[p3][0:64, :], lhsT=wkb[:, 1, h * 64:(h + 1) * 64], rhs=cq[:, 4, tsl], start=False, stop=False)
                                return e.matmul(ps[p3][0:64, :], lhsT=wkb[0:64, 2, h * 64:(h + 1) * 64], rhs=cq[0:64, 5, tsl], start=False, stop=True)
                            P.op("pe", mk, rd=["wkb", "cq"], wr=[psk[p3]])
                            P.op("dve", lambda e, p3=p3, h=h: e.tensor_tensor(out=kst[0:64, h, :], in0=ps[p3][0:64, :], in1=rk[0:64], op=ALU.mult), rd=[psk[p3], "rk"], wr=["kst"])
                            P.op("act", lambda e, h=h: e.activation(out=kst[64:96, h, :], in_=kpe[64:96], func=AF.Copy), rd=["kpe"], wr=["kst"])
                        P.dma("sp", QM[j][:, tsl].rearrange("(h d) t -> d h t", h=4), qst[0:96], rd=["qst"], wr=[("QM", j)])
                        P.dma("sp", kdst[:, tsl].rearrange("(h d) t -> d h t", h=4), kst[0:96], rd=["kst"], wr=[("KM", j)])
                        for tb in range(4):
                            pv2 = nps()
                            def mv(e, pv2=pv2, tb=tb, tt=tt):
                                c0 = tt * 512 + tb * 128
                                e.matmul(ps[pv2][:, 0:256], lhsT=cq[:, 3, c0:c0 + 128], rhs=wkb[:, 0, 256:512], start=True, stop=False)
                                e.matmul(ps[pv2][:, 0:256], lhsT=cq[:, 4, c0:c0 + 128], rhs=wkb[:, 1, 256:512], start=False, stop=False)
                                return e.matmul(ps[pv2][:, 0:256], lhsT=cq[0:64, 5, c0:c0 + 128], rhs=wkb[0:64, 2, 256:512], start=False, stop=True)
                            P.op("pe", mv, rd=["wkb", "cq"], wr=[psk[pv2]])
                            P.op("dve", lambda e, pv2=pv2, tb=tb: e.tensor_scalar(out=vst[:, tb, :, 0:64], in0=ps[pv2][:, 0:256].rearrange("p (h d) -> p h d", h=4), scalar1=rkt[:, tb:tb + 1], scalar2=1.0, op0=ALU.mult, op1=ALU.mult),
                                 rd=[psk[pv2], "rkt"], wr=["vst"])
                        P.dma("sp", vdst[tt * 512:(tt + 1) * 512, :].rearrange("(tb p) c -> p tb c", p=128), vst[:].rearrange("p tb h d -> p tb (h d)"), rd=["vst"], wr=[("VM", j)])
                P.barrier()

        def do_ag():
            P.custom("pool", lambda e: e.collective_compute("AllGather", ALU.bypass, replica_groups=[list(range(NCORES))],
                                                            ins=[PK], outs=[AGB[R_TOT:9 * R_TOT, :]]),
                     rd=[("ZA", 0), ("KD", 0), ("VD", 0), ("KM", 0), ("VM", 0)], wr=["AGB"], inc=1)

        def mla_attend(l, j):
            nkt = 128 if j == 0 else 16
            with ExitStack() as s1:
                otok = sb([128, 16, 256], F32, s1)
                mqs = sb([128, 8], F32, s1)
                pT = [sb([128, 512], BF16, s1) for _ in range(3)]
                rc = sb([128, 4], F32, s1)
                kTs = [sb([128, nkt * 128], BF16, s1) for _ in range(2)]
                vvs = [sb([128, nkt, 65], BF16, s1) for _ in range(2)]
                qTs = [sb([128, T], BF16, s1) for _ in range(2)]
                sstmp = [(sb([128, 512], BF16, s1), sb([128, 32], F32, s1)) for _ in range(2)]

                def load(h):
                    hb = h % 2
                    kT, vv, qT = kTs[hb], vvs[hb], qTs[hb]
                    if j == 0:
                        for s_ in range(8):
                            base = (1 + s_) * R_TOT
                            P.dma("sp", kT[0:96, s_ * T:(s_ + 1) * T], AGB[base + R_KM + h * 96: base + R_KM + (h + 1) * 96, :], rd=["AGB"], wr=["kT%d" % hb])
                            vsrc = AGB[base + R_VM: base + R_VM + 260, :].rearrange("r c -> (r c)").rearrange("(tb p c) -> p tb c", p=128, c=260)
                            P.dma("sp", vv[:, s_ * 16:(s_ + 1) * 16, :], vsrc[:, :, h * 65:(h + 1) * 65], rd=["AGB"], wr=["vv%d" % hb])
                    else:
                        P.dma("sp", kT[0:96, :], KM[j][h * 96:(h + 1) * 96, :], rd=[("KM", j)], wr=["kT%d" % hb])
                        P.dma("sp", vv[:], VM[j].rearrange("(tb p) c -> p tb c", p=128)[:, :, h * 65:(h + 1) * 65], rd=[("VM", j)], wr=["vv%d" % hb])
                    P.dma("sp", qT[0:96, :], QM[j][h * 96:(h + 1) * 96, :], rd=[("QM", j)], wr=["qT%d" % hb])

                load(0)
                for h in range(4):
                    if h + 1 < 4:
                        load(h + 1)
                    hb = h % 2
                    kT, vv, qT = kTs[hb], vvs[hb], qTs[hb]
                    kTk, vvk, qTk = "kT%d" % hb, "vv%d" % hb, "qT%d" % hb
                    mq = mqs[:, hb * 4:hb * 4 + 4]
                    mk_ = "mq"
                    if True:
                        sumsq_max(s1, lambda t, qT=qT: qT[0:96, t * 512:(t + 1) * 512], 96, 4, mq[:, 0:1], [qTk], "mq", tmp=sstmp[0])
                        sumsq_max(s1, lambda t, kT=kT: kT[0:96, t * 512:(t + 1) * 512], 96, nkt // 4, mq[:, 1:2], [kTk], "mk", tmp=sstmp[1])
                        P.op("dve", lambda e, mq=mq: e.tensor_tensor(out=mq[:, 2:3], in0=mq[:, 0:1], in1=mq[:, 1:2], op=ALU.mult), rd=["mqdst", "mkdst"], wr=[mk_ + "2"])
                        P.op("act", lambda e, mq=mq: e.activation(out=mq[:, 2:3], in_=mq[:, 2:3], func=AF.Sqrt), rd=[mk_ + "2"], wr=[mk_ + "2"])
                        P.op("dve", lambda e, mq=mq: e.tensor_scalar(out=mq[:, 3:4], in0=mq[:, 2:3], scalar1=-1.0, scalar2=1.0, op0=ALU.mult, op1=ALU.mult), rd=[mk_ + "2"], wr=[mk_ + "3"])
                        for qt in range(4):
                            def emitS(kt, qt=qt):
                                pS = kt % 4
                                P.op("pe", lambda e, pS=pS, kt=kt, qt=qt: e.matmul(ps[pS][:, :], lhsT=kT[0:96, kt * 128:(kt + 1) * 128], rhs=qT[0:96, qt * 512:(qt + 1) * 512], start=True, stop=True),
                                     rd=[kTk, qTk], wr=[psk[pS]])
                            def emitE(kt):
                                pS = kt % 4; b = kt % 3
                                P.op("act", lambda e, pS=pS, b=b: e.activation(out=pT[b][:], in_=ps[pS][:, :], func=AF.Exp, bias=mq[:, 3:4]), rd=[psk[pS], mk_ + "3"], wr=["pT%d" % b])
                            def emitPV(kt):
                                b = kt % 3
                                def pv(e, b=b, kt=kt):
                                    ins = None
                                    for qs in range(4):
                                        ins = e.matmul(ps[4 + qs][:, 0:65], lhsT=pT[b][:, qs * 128:(qs + 1) * 128], rhs=vv[:, kt, :], start=(kt == 0), stop=(kt == nkt - 1))
                                    return ins
                                P.op("pe", pv, rd=["pT%d" % b, vvk], wr=[psk[4], psk[5], psk[6], psk[7]])
                            emitS(0); emitS(1)
                            for kt in range(nkt):
                                if kt + 2 < nkt:
                                    emitS(kt + 2)
                                emitE(kt)
                                emitPV(kt)
                            for qs in range(4):
                                P.op("dve", lambda e, qs=qs: e.reciprocal(out=rc[:, qs:qs + 1], in_=ps[4 + qs][:, 64:65]), rd=[psk[4 + qs]], wr=["rc%d" % qs])
                                P.op("dve", lambda e, qs=qs, qt=qt, h=h: e.tensor_scalar(out=otok[:, qt * 4 + qs, h * 64:(h + 1) * 64], in0=ps[4 + qs][:, 0:64], scalar1=rc[:, qs:qs + 1], scalar2=1.0, op0=ALU.mult, op1=ALU.mult),
                                     rd=[psk[4 + qs], "rc%d" % qs], wr=["otok"])
                ost = [sb([128, 2, 128], BF16, s1) for _ in range(2)]
                for tq in range(16):
                    b = tq % 2
                    pi = tq % 4
                    def tr(e, pi=pi, tq=tq):
                        e.transpose(out=ps[pi][:, 0:128], in_=otok[:, tq, 0:128], identity=ident[:])
                        return e.transpose(out=ps[pi][:, 128:256], in_=otok[:, tq, 128:256], identity=ident[:])
                    P.op("pe", tr, rd=["otok", "ident"], wr=[psk[pi]])
                    P.op("act", lambda e, pi=pi, b=b: e.activation(out=ost[b][:], in_=ps[pi][:, 0:256].rearrange("p (a t) -> p a t", a=2), func=AF.Copy), rd=[psk[pi]], wr=["ost%d" % b])
                    P.dma("sp", MT[j][1280:1536, tq * 128:(tq + 1) * 128].rearrange("(a p) t -> p a t", p=128), ost[b][:], rd=["ost%d" % b], wr=[("MT", j)], par=True)
            P.barrier()

        def setup_dil():
            with ExitStack() as s1:
                c1 = sb([128, T], BF16, s1); cz = sb([128, 2 * T], BF16, s1); cn = sb([128, T], BF16, s1)
                vrow = sb([128, 390], BF16, s1)
                P.op("pool", lambda e: e.memset(c1[:], 1.0), wr=["c1"])
                P.op("pool", lambda e: e.memset(cz[:], 0.0), wr=["cz"])
                P.op("pool", lambda e: e.memset(cn[:], NEG), wr=["cn"])
                P.op("pool", lambda e: e.memset(vrow[:], 0.0), wr=["vrow"])
                P.op("pool", lambda e: e.memset(vrow[:].rearrange("p (g d) -> p g d", d=65)[:, :, 64:65], 1.0), wr=["vrow"])
                for j in jobs:
                    P.dma("sp", QD[j].rearrange("(g d) t -> g d t", d=65)[:, 64, :], c1[0:6, :], rd=["c1"], wr=[("QD", j)])
                def fillk(dst390, ncol, neg_ranges, key):
                    for r0 in range(0, 390, 128):
                        n = min(128, 390 - r0)
                        P.dma("sp", dst390[r0:r0 + n, :], cz[0:n, 0:ncol], rd=["cz"], wr=[key])
                    for (a, b_) in neg_ranges:
                        P.dma("sp", dst390.rearrange("(g d) t -> g d t", d=65)[:, 64, a:b_], cn[0:6, 0:b_ - a], rd=["cn"], wr=[key])
                def fillv(dst_rows_fn, nblk, key):
                    for n in range(nblk):
                        P.dma("sp", dst_rows_fn(n), vrow[:], rd=["vrow"], wr=[key])
                for j in jobs:
                    if j == 0:
                        fillk(PK[R_KD:R_KD + 390, :], T, [], ("KD", 0))
                        pv = PK[R_VD:R_VD + 390, :].rearrange("r c -> (r c)").rearrange("(t c) -> t c", c=390)
                        fillv(lambda n, pv=pv: pv[n * 128:(n + 1) * 128, :], 16, ("VD", 0))
                        for s_ in (0, 9):
                            base = s_ * R_TOT
                            fillk(AGB[base + R_KD: base + R_KD + 390, :], T, [(0, T)], "AGB")
                            av = AGB[base + R_VD: base + R_VD + 390, :].rearrange("r c -> (r c)").rearrange("(t c) -> t c", c=390)
                            fillv(lambda n, av=av: av[n * 128:(n + 1) * 128, :], 16, "AGB")
                    else:
                        fillk(KDs[j], 2 * T, [(0, 1024), (3072, 4096)], ("KD", j))
                        fillv(lambda n, j=j: VDs[j][n * 128:(n + 1) * 128, :], 32, ("VD", j))
                ohs = sb([128, 32, 2, 128], BF16, s1)
                bnd = sb([128, 2, 128], F32, s1)
                rb = sb([128, 192], F32, s1)
                acc = sb([128, 2, 128], F32, s1)
                P.dma("sp", bnd[:], band_in, wr=["bnd"])
                P.dma("sp", rb[:], relb_in, wr=["rb"])
                for g in range(3):
                    P.dma("sp", ohs[:], oh_in[g], wr=["ohs"])
                    for hh in range(2):
                        gh = g * 2 + hh
                        P.op("dve", lambda e: e.tensor_copy(out=acc[:], in_=bnd[:]), rd=["bnd"], wr=["bacc"])
                        for b in range(32):
                            P.op("dve", lambda e, b=b, gh=gh: e.scalar_tensor_tensor(out=acc[:], in0=ohs[:, b, :, :], scalar=rb[:, b * 6 + gh: b * 6 + gh + 1], in1=acc[:], op0=ALU.mult, op1=ALU.add),
                                 rd=["ohs", "rb", "bacc"], wr=["bacc"])
                        P.op("act", lambda e, gh=gh: e.activation(out=biasT[:, gh, :, :], in_=acc[:], func=AF.Copy), rd=["bacc"], wr=["biasT"])
            P.barrier()

        def dyn_copy():
            pid = PID[0]
            for (c_a, c_b, so, s_a, s_b) in ((0, 1024, 0, 1024, 2048), (1024, 3072, 1, 0, 2048), (3072, 4096, 2, 0, 1024)):
                P.dma("pool", KDs[0][:, c_a:c_b], AGB[bass.ds(pid * R_TOT + (so * R_TOT + R_KD), 390), s_a:s_b], rd=["AGB"], wr=[("KD", 0)])
            vflat = VDs[0][0:2 * T, :].rearrange("t c -> (t c)").rearrange("(r w) -> r w", w=T)
            for (r_a, r_b, so, off) in ((0, 195, 0, 195), (195, 585, 1, 0), (585, 780, 2, 0)):
                P.dma("pool", vflat[r_a:r_b, :], AGB[bass.ds(pid * R_TOT + (so * R_TOT + R_VD + off), r_b - r_a), :], rd=["AGB"], wr=[("VD", 0)])

        def do_dil(l, j):
            with ExitStack() as s1:
                oacc = sb([128, 16, 65], F32, s1)
                mq = sb([128, 4], F32, s1)
                pT = [sb([128, 256], BF16, s1) for _ in range(3)]
                it = 0
                for g in range(3):
                    dil = (1, 4, 16)[g]
                    nb = T // dil // 128
                    Pd = 1024 // dil
                    for hh in range(2):
                        gh = g * 2 + hh
                        with ExitStack() as s2:
                            qT = sb([128, T], BF16, s2); kT = sb([128, 2 * T], BF16, s2)
                            vv = sb([128, dil * (nb + 1), 65], BF16, s2)
                            P.dma("sp", qT[0:65, :], QD[j][gh * 65:(gh + 1) * 65, :], rd=[("QD", j)], wr=["dqT"])
                            P.dma("sp", kT[0:65, :], KDs[j][gh * 65:(gh + 1) * 65, :], rd=[("KD", j)], wr=["dkT"])
                            for rho in range(dil):
                                base = rho + dil * (Pd - 64)
                                src = VDs[j][base: base + dil * 128 * (nb + 1), gh * 65:(gh + 1) * 65].rearrange("(m i d) c -> i m d c", i=128, d=dil)[:, :, 0, :]
                                P.dma("sp", vv[:, rho * (nb + 1):(rho + 1) * (nb + 1), :], src, rd=[("VD", j)], wr=["dvv"])
                            sumsq_max(s2, lambda t: qT[0:64, t * 512:(t + 1) * 512], 64, 4, mq[:, 0:1], ["dqT"], "dq")
                            sumsq_max(s2, lambda t: kT[0:64, t * 512:(t + 1) * 512], 64, 8, mq[:, 1:2], ["dkT"], "dk")
                            P.op("dve", lambda e: e.tensor_tensor(out=mq[:, 2:3], in0=mq[:, 0:1], in1=mq[:, 1:2], op=ALU.mult), rd=["dqdst", "dkdst"], wr=["dm2"])
                            P.op("act", lambda e: e.activation(out=mq[:, 2:3], in_=mq[:, 2:3], func=AF.Sqrt), rd=["dm2"], wr=["dm2"])
                            P.op("dve", lambda e, gh=gh: e.tensor_scalar(out=negM[:, gh:gh + 1], in0=mq[:, 2:3], scalar1=-1.0, scalar2=1.0, op0=ALU.mult, op1=ALU.mult), rd=["dm2"], wr=["negM"])
                            blocks = [(rho, b) for rho in range(dil) for b in range(nb)]
                            sbank = {}
                            def emit_ms(i, dil=dil, Pd=Pd, gh=gh):
                                rho, b = blocks[i]
                                pS = nps(); sbank[i] = pS
                                def ms(e, pS=pS, rho=rho, b=b):
                                    ins = None
                                    q0 = rho + dil * 128 * b
                                    for mi in range(2):
                                        k0 = rho + dil * (Pd - 64 + 128 * (b + mi))
                                        e.matmul(ps[pS][:, mi * 128:(mi + 1) * 128], lhsT=kT[0:65, k0:k0 + 127 * dil + 1:dil], rhs=qT[0:65, q0:q0 + 127 * dil + 1:dil], start=True, stop=False)
                                        ins = e.matmul(ps[pS][:, mi * 128:(mi + 1) * 128], lhsT=ident_bf[:], rhs=biasT[:, gh, mi, :], start=False, stop=True)
                                    return ins
                                P.op("pe", ms, rd=["dkT", "dqT", "identbf", "biasT"], wr=[psk[pS]])
                            def emit_rest(i, nb=nb, gh=gh):
                                rho, b = blocks[i]
                                slot = rho * nb + b
                                pS = sbank[i]; pb = i % 3
                                P.op("act", lambda e, pS=pS, pb=pb: e.activation(out=pT[pb][:], in_=ps[pS][:, 0:256], func=AF.Exp, bias=negM[:, gh:gh + 1]), rd=[psk[pS], "negM"], wr=["dpT%d" % pb])
                                pO = nps()
                                def mo(e, pO=pO, pb=pb, rho=rho, b=b):
                                    e.matmul(ps[pO][:, 0:65], lhsT=pT[pb][:, 0:128], rhs=vv[:, rho * (nb + 1) + b, :], start=True, stop=False)
                                    return e.matmul(ps[pO][:, 0:65], lhsT=pT[pb][:, 128:256], rhs=vv[:, rho * (nb + 1) + b + 1, :], start=False, stop=True)
                                P.op("pe", mo, rd=["dpT%d" % pb, "dvv"], wr=[psk[pO]])
                                P.op("dve", lambda e, pO=pO, slot=slot: e.tensor_copy(out=oacc[:, slot, :], in_=ps[pO][:, 0:65]), rd=[psk[pO]], wr=["oacc"])
                            emit_ms(0)
                            if len(blocks) > 1:
                                emit_ms(1)
                            for i in range(len(blocks)):
                                if i + 2 < len(blocks):
                                    emit_ms(i + 2)
                                emit_rest(i)
                            for rho in range(dil):
                                dst = DACC[j][g, rho: rho + T, hh * 65:(hh + 1) * 65].rearrange("(b i d) c -> i b d c", i=128, d=dil)[:, :, 0, :]
                                P.dma("sp", dst, oacc[:, rho * nb:(rho + 1) * nb, :], rd=["oacc"], wr=[("DACC", j)], par=True)
                w = sb([128, 8], F32, s1); mn = sb([128, 2], F32, s1)
                for hh in range(2):
                    P.op("dve", lambda e, hh=hh: e.tensor_tensor(out=mn[:, hh:hh + 1], in0=negM[:, hh:hh + 1], in1=negM[:, 2 + hh:3 + hh], op=ALU.min), rd=["negM"], wr=["mn"])
                    P.op("dve", lambda e, hh=hh: e.tensor_tensor(out=mn[:, hh:hh + 1], in0=mn[:, hh:hh + 1], in1=negM[:, 4 + hh:5 + hh], op=ALU.min), rd=["negM", "mn"], wr=["mn"])
                    for g in range(3):
                        P.op("act", lambda e, hh=hh, g=g: e.activation(out=w[:, g * 2 + hh:g * 2 + hh + 1], in_=negM[:, g * 2 + hh:g * 2 + hh + 1], func=AF.Exp, scale=-1.0, bias=mn[:, hh:hh + 1]),
                             rd=["negM", "mn"], wr=["dw"])
                da = [sb([128, 3, 130], F32, s1) for _ in range(2)]
                ss = [sb([128, 130], F32, s1) for _ in range(2)]
                oo = [sb([128, 128], F32, s1) for _ in range(2)]
                ob = [sb([128, 128], BF16, s1) for _ in range(2)]
                rc = [sb([128, 2], F32, s1) for _ in range(2)]
                for tb in range(T // 128):
                    b = tb % 2
                    P.dma("sp", da[b][:], DACC[j][:, tb * 128:(tb + 1) * 128, :].rearrange("g t c -> t g c"), rd=[("DACC", j)], wr=["da%d" % b])
                    for g in range(3):
                        for hh in range(2):
                            P.op("dve", lambda e, b=b, g=g, hh=hh: e.tensor_scalar(out=da[b][:, g, hh * 65:(hh + 1) * 65], in0=da[b][:, g, hh * 65:(hh + 1) * 65],
                                                                                 scalar1=w[:, g * 2 + hh:g * 2 + hh + 1], scalar2=1.0, op0=ALU.mult, op1=ALU.mult), rd=["da%d" % b, "dw"], wr=["da%d" % b])
                    P.op("pool", lambda e, b=b: e.tensor_tensor(out=ss[b][:], in0=da[b][:, 0, :], in1=da[b][:, 1, :], op=ALU.add), rd=["da%d" % b], wr=["ss%d" % b])
                    P.op("pool", lambda e, b=b: e.tensor_tensor(out=ss[b][:], in0=ss[b][:], in1=da[b][:, 2, :], op=ALU.add), rd=["da%d" % b, "ss%d" % b], wr=["ss%d" % b])
                    for hh in range(2):
                        P.op("dve", lambda e, b=b, hh=hh: e.reciprocal(out=rc[b][:, hh:hh + 1], in_=ss[b][:, hh * 65 + 64:hh * 65 + 65]), rd=["ss%d" % b], wr=["drc%d" % b])
                        P.op("dve", lambda e, b=b, hh=hh: e.tensor_scalar(out=oo[b][:, hh * 64:(hh + 1) * 64], in0=ss[b][:, hh * 65:hh * 65 + 64], scalar1=rc[b][:, hh:hh + 1], scalar2=1.0, op0=ALU.mult, op1=ALU.mult),
                             rd=["ss%d" % b, "drc%d" % b], wr=["oo%d" % b])
                    pi = nps()
                    P.op("pe", lambda e, pi=pi, b=b: e.transpose(out=ps[pi][:, 0:128], in_=oo[b][:], identity=ident[:]), rd=["oo%d" % b, "ident"], wr=[psk[pi]])
                    P.op("act", lambda e, pi=pi, b=b: e.activation(out=ob[b][:], in_=ps[pi][:, 0:128], func=AF.Copy), rd=[psk[pi]], wr=["dob%d" % b])
                    P.dma("sp", MT[j][768:896, tb * 128:(tb + 1) * 128], ob[b][:], rd=["dob%d" % b], wr=[("MT", j)], par=True)
            P.barrier()

        def do_fnet(l, j):
            S_tiles = 128 if j == 0 else 16
            tab = dftP if j == 0 else dftS

            def xsrc(nt):
                if j == 0:
                    s_, r_ = divmod(nt, 16)
                    base = (1 + s_) * R_TOT + R_ZA
                    v = AGB[base:base + 768, :].rearrange("r c -> (r c)").rearrange("(t c) -> t c", c=768)
                    return v[r_ * 128:(r_ + 1) * 128, :]
                v = ZA[j].rearrange("r c -> (r c)").rearrange("(t c) -> t c", c=768)
                return v[nt * 128:(nt + 1) * 128, :]
            xkey = "AGB" if j == 0 else ("ZA", j)
            variants = [(0, 128, 0), (0, 64, 128), (64, 64, 0), (0, 128, 64)]
            with ExitStack() as s1:
                chf = sb([128, 4, 2, 192], F32, s1); cht = sb([128, 4, 2, 192], BF16, s1)
                P.op("pool", lambda e: e.memset(chf[:], 0.0), wr=["chf"])
                for v, (p0, n, ch0) in enumerate(variants):
                    for cs_ in range(2):
                        P.dma("sp", chf[p0:p0 + n, v, cs_, :], chdft_in[cs_, ch0:ch0 + n, :], wr=["chf"])
                P.op("pool", lambda e: e.tensor_copy(out=cht[:], in_=chf[:]), rd=["chf"], wr=["cht"])
                uv = sb([128, 2, 6, 512], BF16, s1)
                xt = [sb([128, 384], BF16, s1) for _ in range(3)]
                ct = [sb([128, 512], BF16, s1) for _ in range(3)]
                st_ = [sb([128, 512], BF16, s1) for _ in range(3)]
                fo = [sb([128, 512], BF16, s1) for _ in range(2)]
                for kt in range(4):
                    for cg in range(2):
                        for nt in range(S_tiles):
                            b = nt % 3
                            P.dma("sp", xt[b][:], xsrc(nt)[:, cg * 384:(cg + 1) * 384], rd=[xkey], wr=["fx%d" % b])
                            P.dma("sp", ct[b][:], tab[0, nt * 128:(nt + 1) * 128, kt * 512:(kt + 1) * 512], wr=["fc%d" % b])
                            P.dma("sp", st_[b][:], tab[1, nt * 128:(nt + 1) * 128, kt * 512:(kt + 1) * 512], wr=["fs%d" % b])
                            def mm(e, b=b, nt=nt):
                                ins = None
                                for cc in range(3):
                                    e.matmul(ps[cc][:, :], lhsT=xt[b][:, cc * 128:(cc + 1) * 128], rhs=ct[b][:], start=(nt == 0), stop=(nt == S_tiles - 1))
                                    ins = e.matmul(ps[3 + cc][:, :], lhsT=xt[b][:, cc * 128:(cc + 1) * 128], rhs=st_[b][:], start=(nt == 0), stop=(nt == S_tiles - 1))
                                return ins
                            P.op("pe", mm, rd=["fx%d" % b, "fc%d" % b, "fs%d" % b], wr=[psk[i] for i in range(6)])
                        for cc in range(3):
                            P.op("act", lambda e, cc=cc, cg=cg: e.activation(out=uv[:, 0, cg * 3 + cc, :], in_=ps[cc][:, :], func=AF.Copy), rd=[psk[cc]], wr=["uv"])
                            P.op("dve", lambda e, cc=cc, cg=cg: e.tensor_copy(out=uv[:, 1, cg * 3 + cc, :], in_=ps[3 + cc][:, :]), rd=[psk[3 + cc]], wr=["uv"])
                    for oc in range(6):
                        pb = 6 + oc % 2
                        def chm(e, oc=oc, pb=pb):
                            ins = None
                            for gi in range(4):
                                c0 = 192 * gi
                                cA, offA = c0 // 128, c0 % 128
                                if offA == 0:
                                    pieces = [(cA, 0, 128, 0, 0), (cA + 1, 0, 64, 128, 1)]
                                else:
                                    pieces = [(cA, 64, 64, 0, 2), (cA + 1, 0, 128, 64, 3)]
                                for (mchunk, mp0, mn_, mch0, _) in pieces:
                                    if mchunk != oc:
                                        continue
                                    first = True
                                    for pi_, (kchunk, kp0, kn, kch0, kv) in enumerate(pieces):
                                        for cs_ in range(2):
                                            last = (pi_ == 1 and cs_ == 1)
                                            ins = e.matmul(ps[pb][mp0:mp0 + mn_, :], lhsT=cht[kp0:kp0 + kn, kv, cs_, mch0:mch0 + mn_], rhs=uv[kp0:kp0 + kn, cs_, kchunk, :], start=first, stop=last)
                                            first = False
                            return ins
                        P.op("pe", chm, rd=["cht", "uv"], wr=[psk[pb]])
                        fb = oc % 2
                        P.op("act", lambda e, pb=pb, fb=fb: e.activation(out=fo[fb][:], in_=ps[pb][:, :], func=AF.Copy), rd=[psk[pb]], wr=["fo%d" % fb])
                        P.dma("sp", MT[j][oc * 128:(oc + 1) * 128, kt * 512:(kt + 1) * 512], fo[fb][:], rd=["fo%d" % fb], wr=[("MT", j)], par=True)
            P.barrier()

        if flags['dil']:
            setup_dil()
        convert_weights(0)
        for l in range(depth):
            for j in jobs:
                with ExitStack() as s1:
                    hT = sb([128, KD, T], BF16, s1)
                    xt = [sb([128, KD, 512], F32, s1) for _ in range(2)]
                    sq = sb([128, KD, 512], BF16, s1)
                    rs = sb([128, 512], F32, s1)
                    stg = [sb([128, 512], BF16, s1) for _ in range(4)]
                    sgi = [0]
                    for tt in range(NTT):
                        b = tt % 2
                        P.dma("sp", xt[b][:], xT[j, :, tt * 512:(tt + 1) * 512].rearrange("(k p) t -> p k t", p=128), rd=[("xT", j)], wr=["xt%d" % b])
                        rms_tile(s1, xt[b], "xt%d" % b, KD, lambda k: a1[:, l, k, j:j + 1], lambda k: mods[:, l, k, j:j + 1],
                                 hT, "hT", tt * 512, 512, 1.0 / D, (sq, rs), "A")

                    def store(pi, nrow, ncol, dst, func=AF.Copy, scale=1.0, wrk=None):
                        si = sgi[0]; sgi[0] = (si + 1) % 4
                        P.op("act", lambda e: e.activation(out=stg[si][0:nrow, 0:ncol], in_=ps[pi][0:nrow, 0:ncol], func=func, scale=scale),
                             rd=[psk[pi]], wr=["stg%d" % si])
                        P.dma("act", dst, stg[si][0:nrow, 0:ncol], rd=["stg%d" % si], wr=[wrk], par=True)

                    pieces = [(0, 384), (384, 384), (768, 384), (1152, 384), (1536, 384), (1920, 384), (2304, 384), (2688, 384), (3072, 384)]
                    pieces += [(C_ZG + i * 512, 512) for i in range(8)]
                    kch = [(k * 128, 128) for k in range(KD)]
                    nxt = wload("w_in", l, kch, *pieces[0])
                    for pi_, (c0, ncols) in enumerate(pieces):
                        wt, wk = nxt
                        if pi_ + 1 < len(pieces):
                            nxt = wload("w_in", l, kch, *pieces[pi_ + 1])
                        if c0 < C_BQ:
                            for tb in range(T // 128):
                                pi = nps()
                                def mm(e, pi=pi, tb=tb, wt=wt, ncols=ncols):
                                    ins = None
                                    for k in range(KD):
                                        ins = e.matmul(ps[pi][:, 0:ncols], lhsT=hT[:, k, tb * 128:(tb + 1) * 128], rhs=wt[:, k, 0:ncols], start=(k == 0), stop=(k == KD - 1))
                                    return ins
                                P.op("pe", mm, rd=[wk, "hT"], wr=[psk[pi]])
                                if c0 < C_BV:
                                    zdst = (PK[R_ZA:R_ZA + 768, :] if j == 0 else ZA[j]).rearrange("r c -> (r c)").rearrange("(t c) -> t c", c=768)
                                    store(pi, 128, 384, zdst[tb * 128:(tb + 1) * 128, c0:c0 + 384], wrk=("ZA", j))
                                elif c0 < C_CV:
                                    if j == 0:
                                        vdst = PK[R_VD:R_VD + 390, :].rearrange("r c -> (r c)").rearrange("(t g h d) -> t g h d", g=3, h=2, d=65)
                                        vdst = vdst[tb * 128:(tb + 1) * 128, :, :, 0:64]
                                    else:
                                        vdst = VDs[j].rearrange("t (g h d) -> t g h d", g=3, h=2)[T // 2 + tb * 128:T // 2 + (tb + 1) * 128, :, :, 0:64]
                                    si = sgi[0]; sgi[0] = (si + 1) % 4
                                    P.op("act", lambda e, si=si, pi=pi: e.activation(out=stg[si][:, 0:384], in_=ps[pi][:, 0:384], func=AF.Copy), rd=[psk[pi]], wr=["stg%d" % si])
                                    P.dma("act", vdst, stg[si][:, 0:384].rearrange("p (g h d) -> p g h d", g=3, h=2), rd=["stg%d" % si], wr=[("VD", j)], par=True)
                                else:
                                    store(pi, 128, 384, CV[j][tb * 128:(tb + 1) * 128, :], wrk=("CV", j))
                        else:
                            if c0 == C_CU:
                                chunks = [(i * 96, 96) for i in range(4)]
                            elif c0 == C_CKV:
                                chunks = [(0, 128), (128, 128), (256, 64), (320, 64)]
                            else:
                                chunks = [(i * 128, 128) for i in range(ncols // 128)]
                            for ci, (co, M) in enumerate(chunks):
                                for tt in range(NTT):
                                    pi = nps()
                                    def mm(e, pi=pi, tt=tt, wt=wt, co=co, M=M):
                                        ins = None
                                        for k in range(KD):
                                            ins = e.matmul(ps[pi][0:M, :], lhsT=wt[:, k, co:co + M], rhs=hT[:, k, tt * 512:(tt + 1) * 512], start=(k == 0), stop=(k == KD - 1))
                                        return ins
                                    P.op("pe", mm, rd=[wk, "hT"], wr=[psk[pi]])
                                    tsl = slice(tt * 512, (tt + 1) * 512)
                                    if c0 == C_BQ:
                                        dst = QD[j].rearrange("(g h d) t -> g h d t", g=3, h=2)[ci, :, 0:64, tsl]
                                        si = sgi[0]; sgi[0] = (si + 1) % 4
                                        P.op("act", lambda e, si=si, pi=pi: e.activation(out=stg[si][:, :], in_=ps[pi][:, :], func=AF.Copy, scale=0.125), rd=[psk[pi]], wr=["stg%d" % si])
                                        for hh in range(2):
                                            P.dma("act", dst[hh], stg[si][hh * 64:(hh + 1) * 64, :], rd=["stg%d" % si], wr=[("QD", j)], par=True)
                                    elif c0 == C_BK:
                                        if j == 0:
                                            dst = PK[R_KD:R_KD + 390, :].rearrange("(g h d) t -> g h d t", g=3, h=2)[ci, :, 0:64, tsl]
                                        else:
                                            dst = KDs[j].rearrange("(g h d) t -> g h d t", g=3, h=2)[ci, :, 0:64, T // 2 + tt * 512:T // 2 + (tt + 1) * 512]
                                        si = sgi[0]; sgi[0] = (si + 1) % 4
                                        P.op("act", lambda e, si=si, pi=pi: e.activation(out=stg[si][:, :], in_=ps[pi][:, :], func=AF.Copy), rd=[psk[pi]], wr=["stg%d" % si])
                                        for hh in range(2):
                                            P.dma("act", dst[hh], stg[si][hh * 64:(hh + 1) * 64, :], rd=["stg%d" % si], wr=[("KD", j)], par=True)
                                    elif c0 == C_CU:
                                        store(pi, 96, 512, CU[j][ci * 96:(ci + 1) * 96, tsl], wrk=("CU", j))
                                    elif c0 == C_CQ:
                                        store(pi, 128, 512, CQ[j][ci * 128:(ci + 1) * 128, tsl], wrk=("CQ", j))
                                    elif c0 == C_CKV:
                                        store(pi, M, 512, CQ[j][384 + co:384 + co + M, tsl], wrk=("CQ", j))
                                    else:
                                        zc = (c0 - C_ZG) // 128 + ci
                                        store(pi, 128, 512, GT[j][zc * 128:(zc + 1) * 128, tsl], func=AF.Sigmoid, wrk=("GT", j))
                P.barrier()

            if l + 1 < depth:
                convert_weights(l + 1)
            if flags["mla"]:
                for j in jobs:
                    do_mla(l, j)
            if 0 in jobs:
                do_ag()
            if 0 in jobs and flags["dil"]:
                dyn_copy()
            for j in jobs:
                if flags["sgu"]:
                    do_sgu(l, j)
            for j in jobs:
                if flags["dil"]:
                    do_dil(l, j)
            for j in jobs:
                if flags["fnet"]:
                    do_fnet(l, j)
            for j in jobs:
                if flags["mla"]:
                    mla_attend(l, j)

            for j in jobs:
                for hf in range(2):
                    t0 = hf * 1024
                    with ExitStack() as s1:
                        xr = sb([128, KD, 1024], F32, s1)
                        mg = sb([128, KD, 1024], BF16, s1)
                        h2 = mg
                        sq = sb([128, KD, 512], BF16, s1)
                        rs = sb([128, 512], F32, s1)
                        xc = sb([128, KD, 512], F32, s1)
                        any_mix = flags["fnet"] or flags["dil"] or flags["sgu"] or flags["mla"]
                        if not any_mix:
                            P.dma("sp", xr[:], xT[j, :, t0:t0 + 1024].rearrange("(k p) t -> p k t", p=128), rd=[("xT", j)], wr=["xr"])
                        if any_mix:
                            with ExitStack() as s2:
                                br = sb([128, 13, 1024], BF16, s2)
                                gt = [sb([128, 1024], BF16, s2) for _ in range(4)]
                                tmpf = [sb([128, 1024], F32, s2) for _ in range(2)]
                                acc = sb([128, 1024], F32, s2)
                                brk = [(i * 128, 128) for i in range(6)] + [(768, 128)] + [(896 + i * 96, 96) for i in range(4)] + [(1280, 128), (1408, 128)]
                                for i, (r0, n) in enumerate(brk):
                                    P.dma("sp", br[0:n, i, :], MT[j][r0:r0 + n, t0:t0 + 1024], rd=[("MT", j)], wr=["br"], par=True)
                                P.dma("sp", xr[:], xT[j, :, t0:t0 + 1024].rearrange("(k p) t -> p k t", p=128), rd=[("xT", j)], wr=["xr"])
                                bsets = [(list(range(0, 6)), flags["fnet"]), ([6], flags["dil"]), (list(range(7, 11)), flags["sgu"]), ([11, 12], flags["mla"])]
                                gi = 0
                                for dc in range(KD):
                                    first = True
                                    for bi_, (ks, en) in enumerate(bsets):
                                        if not en:
                                            continue
                                        wt, wk = wload("p_all", l, [brk[k] for k in ks], dc * 128, 128)
                                        g = gi % 4; gi += 1
                                        P.dma("sp", gt[g][:], GT[j][bi_ * D + dc * 128: bi_ * D + (dc + 1) * 128, t0:t0 + 1024], rd=[("GT", j)], wr=["gt%d" % g])
                                        for th in range(2):
                                            pi = nps()
                                            def mm(e, pi=pi, th=th, wt=wt, ks=ks):
                                                ins = None
                                                for ii, k in enumerate(ks):
                                                    n = brk[k][1]
                                                    ins = e.matmul(ps[pi][:, :], lhsT=wt[0:n, ii, 0:128], rhs=br[0:n, k, th * 512:(th + 1) * 512], start=(ii == 0), stop=(ii == len(ks) - 1))
                                                return ins
                                            P.op("pe", mm, rd=[wk, "br"], wr=[psk[pi]])
                                            tsl = slice(th * 512, (th + 1) * 512)
                                            if first:
                                                P.op("dve", lambda e, pi=pi, g=g, tsl=tsl: e.tensor_tensor(out=acc[:, tsl], in0=ps[pi][:, :], in1=gt[g][:, tsl], op=ALU.mult),
                                                     rd=[psk[pi], "gt%d" % g], wr=["acc%d" % th])
                                            else:
                                                P.op("dve", lambda e, pi=pi, g=g, tsl=tsl, th=th: e.tensor_tensor(out=tmpf[th][:, 0:512], in0=ps[pi][:, :], in1=gt[g][:, tsl], op=ALU.mult),
                                                     rd=[psk[pi], "gt%d" % g], wr=["tmpf%d" % th])
                                                P.op("dve", lambda e, tsl=tsl, th=th: e.tensor_tensor(out=acc[:, tsl], in0=acc[:, tsl], in1=tmpf[th][:, 0:512], op=ALU.add),
                                                     rd=["tmpf%d" % th, "acc%d" % th], wr=["acc%d" % th])
                                        first = False
                                    P.op("act", lambda e, dc=dc: e.activation(out=mg[:, dc, :], in_=acc[:], func=AF.Copy), rd=["acc0", "acc1"], wr=["mg"])
                                P.barrier()
                            for dc2 in range(2):
                                wt, wk = wload("w_o", l, [(k * 128, 128) for k in range(KD)], dc2 * 512, 512)
                                for cc in range(4):
                                    dc = dc2 * 4 + cc
                                    for th in range(2):
                                        pi = nps()
                                        def mm(e, pi=pi, th=th, wt=wt, cc=cc):
                                            ins = None
                                            for k in range(KD):
                                                ins = e.matmul(ps[pi][:, :], lhsT=wt[:, k, cc * 128:(cc + 1) * 128], rhs=mg[:, k, th * 512:(th + 1) * 512], start=(k == 0), stop=(k == KD - 1))
                                            return ins
                                        P.op("pe", mm, rd=[wk, "mg"], wr=[psk[pi]])
                                        P.op("dve", lambda e, pi=pi, dc=dc, th=th: e.scalar_tensor_tensor(
                                            out=xr[:, dc, th * 512:(th + 1) * 512], in0=ps[pi][:, :], scalar=mods[:, l, 16 + dc, j:j + 1], in1=xr[:, dc, th * 512:(th + 1) * 512],
                                            op0=ALU.mult, op1=ALU.add), rd=[psk[pi], "xr"], wr=["xr"])
                        for th in range(2):
                            P.op("act", lambda e, th=th: e.activation(out=xc[:], in_=xr[:, :, th * 512:(th + 1) * 512], func=AF.Copy), rd=["xr"], wr=["xc"])
                            rms_tile(s1, xc, "xc", KD, lambda k: a2[:, l, k, j:j + 1], lambda k: mods[:, l, 24 + k, j:j + 1], h2, "mg", th * 512, 512, 1.0 / D, (sq, rs), "C")
                        aT = sb([128, 32, 1024], BF16, s1)
                        kch = [(k * 128, 128) for k in range(KD)]
                        nxt = wload("w1", l, kch, 0, 512)
                        for pc in range(8):
                            wt, wk = nxt
                            if pc + 1 < 8:
                                nxt = wload("w1", l, kch, (pc + 1) * 512, 512)
                            for cc in range(4):
                                hc = pc * 4 + cc
                                for th in range(2):
                                    pi = nps()
                                    def mm(e, pi=pi, th=th, wt=wt, cc=cc):
                                        ins = None
                                        for k in range(KD):
                                            ins = e.matmul(ps[pi][:, :], lhsT=wt[:, k, cc * 128:(cc + 1) * 128], rhs=h2[:, k, th * 512:(th + 1) * 512], start=(k == 0), stop=(k == KD - 1))
                                        return ins
                                    P.op("pe", mm, rd=[wk, "mg"], wr=[psk[pi]])
                                    P.op("act", lambda e, pi=pi, hc=hc, th=th: e.activation(out=aT[:, hc, th * 512:(th + 1) * 512], in_=ps[pi][:, :], func=AF.Relu),
                                         rd=[psk[pi]], wr=[("aT", hc, th)])
                                    P.op("dve", lambda e, hc=hc, th=th: e.tensor_tensor(out=aT[:, hc, th * 512:(th + 1) * 512], in0=aT[:, hc, th * 512:(th + 1) * 512],
                                                                                       in1=aT[:, hc, th * 512:(th + 1) * 512], op=ALU.mult),
                                         rd=[("aT", hc, th)], wr=[("aT", hc, th)])
                        for dc in range(KD):
                            wts = [wload("w2", l, [((kg * 8 + k) * 128, 128) for k in range(8)], dc * 128, 128) for kg in range(3)]
                            pis = [nps(), nps()]
                            for kg in range(4):
                                if kg == 3:
                                    wt, wk = wload("w2", l, [((kg * 8 + k) * 128, 128) for k in range(8)], dc * 128, 128)
                                else:
                                    wt, wk = wts[kg]
                                for th in range(2):
                                    pi = pis[th]
                                    def mm(e, pi=pi, th=th, wt=wt, kg=kg):
                                        ins = None
                                        for k in range(8):
                                            ins = e.matmul(ps[pi][:, :], lhsT=wt[:, k, 0:128], rhs=aT[:, kg * 8 + k, th * 512:(th + 1) * 512],
                                                           start=(kg == 0 and k == 0), stop=(kg == 3 and k == 7))
                                        return ins
                                    P.op("pe", mm, rd=[wk] + [("aT", kg * 8 + k, th) for k in range(8)], wr=[psk[pi]])
                            for th in range(2):
                                pi = pis[th]
                                P.op("dve", lambda e, pi=pi, dc=dc, th=th: e.scalar_tensor_tensor(
                                    out=xr[:, dc, th * 512:(th + 1) * 512], in0=ps[pi][:, :], scalar=mods[:, l, 40 + dc, j:j + 1], in1=xr[:, dc, th * 512:(th + 1) * 512],
                                    op0=ALU.mult, op1=ALU.add), rd=[psk[pi], "xr"], wr=["xr"])
                        P.dma("sp", xT[j, :, t0:t0 + 1024].rearrange("(k p) t -> p k t", p=128), xr[:], rd=["xr"], wr=[("xT", j)])
                        P.barrier()

        with ExitStack() as s1:
            xt = [sb([128, KD, 512], F32, s1) for _ in range(2)]
            sqf = sb([128, KD, 512], BF16, s1)
            rsf = sb([128, 512], F32, s1)
            yo = [sb([128, D], F32, s1) for _ in range(2)]
            hf32 = sb([128, KD, 512], F32, s1)
            it = 0
            for j in jobs:
                for tt in range(NTT):
                    b = tt % 2
                    P.dma("sp", xt[b][:], xT[j, :, tt * 512:(tt + 1) * 512].rearrange("(k p) t -> p k t", p=128), rd=[("xT", j)], wr=["xt%d" % b])
                    rms_tile(s1, xt[b], "xt%d" % b, KD, lambda k: gfs[:, k:k + 1], lambda k: zero_t[:, 0:1], hf32, "hf32", 0, 512, 1.0 / D, (sqf, rsf), "F")
                    for tb in range(4):
                        ob = it % 2; it += 1
                        for hf in range(2):
                            pi = nps()
                            def tr(e, pi=pi, hf=hf, tb=tb):
                                ins = None
                                for q in range(4):
                                    k = hf * 4 + q
                                    ins = e.transpose(out=ps[pi][:, q * 128:(q + 1) * 128], in_=hf32[:, k, tb * 128:(tb + 1) * 128], identity=ident[:])
                                return ins
                            P.op("pe", tr, rd=["hf32", "ident"], wr=[psk[pi]])
                            if hf == 0:
                                P.op("act", lambda e, pi=pi, ob=ob: e.activation(out=yo[ob][:, 0:512], in_=ps[pi][:, :], func=AF.Copy), rd=[psk[pi]], wr=["yo%d0" % ob])
                            else:
                                P.op("dve", lambda e, pi=pi, ob=ob: e.tensor_copy(out=yo[ob][:, 512:1024], in_=ps[pi][:, :]), rd=[psk[pi]], wr=["yo%d1" % ob])
                        P.dma("sp", y_out[j, tt * 512 + tb * 128: tt * 512 + (tb + 1) * 128, :], yo[ob][:], rd=["yo%d0" % ob, "yo%d1" % ob], wr=["y"])
        P.barrier(["sp"])
    return nc


def prep_shared(inp):
    f = lambda a: np.ascontiguousarray(np.asarray(a, dtype=np.float32))
    sh = {}
    sh["ada_w"] = f(inp["ada_w"])
    sh["ada_bT"] = f(np.asarray(inp["ada_b"]).reshape(DEPTH, 48, 128).transpose(2, 0, 1))
    sh["g1T"] = f(np.asarray(inp["norm1_g"]).reshape(DEPTH, KD, 128).transpose(2, 0, 1))
    sh["g2T"] = f(np.asarray(inp["norm2_g"]).reshape(DEPTH, KD, 128).transpose(2, 0, 1))
    sh["gfT"] = f(np.asarray(inp["final_g"]).reshape(KD, 128).transpose(1, 0))
    w = np.asarray(inp["w_in"])
    za, bq, bk, bv, cu, cv, dcq, dckv, dkr, zg = np.split(w, np.cumsum([768, 384, 384, 384, 384, 384, 384, 320, 32])[:].tolist(), axis=-1)
    dkr_sw = np.concatenate([dkr[..., 16:32], dkr[..., 0:16]], -1)
    sh["w_in"] = f(np.concatenate([za, bv, cv, bq, bk, cu, dcq, dckv, dkr, dkr_sw, zg], -1))
    assert sh["w_in"].shape[-1] == C_END
    qg = np.asarray(inp["mla_q_norm_g"]).reshape(DEPTH, 3, 128).transpose(2, 0, 1)
    sh["qng"] = f(qg)
    kg = np.zeros((DEPTH, 384), np.float32)
    kg[:, :320] = np.asarray(inp["mla_kv_norm_g"])
    sh["kvg"] = f(kg.reshape(DEPTH, 3, 128).transpose(2, 0, 1))
    wq = np.asarray(inp["mla_w_uq"]).reshape(DEPTH, 384, 4, 96)
    sw = np.concatenate([np.zeros_like(wq[..., 0:64]), wq[..., 80:96], wq[..., 64:80]], -1)
    sh["w_uq"] = f(np.concatenate([wq.reshape(DEPTH, 384, 384), sw.reshape(DEPTH, 384, 384)], -1))
    wkv = np.asarray(inp["mla_w_ukv"]).reshape(DEPTH, 320, 4, 128)
    sh["w_ukv"] = f(np.concatenate([wkv[..., 0:64].reshape(DEPTH, 320, 256), wkv[..., 64:128].reshape(DEPTH, 320, 256)], -1))
    sh["lng"] = f(np.broadcast_to(np.asarray(inp["sgu_ln_g"])[None], (128, DEPTH, 384)))
    sh["lnb"] = f(np.broadcast_to(np.asarray(inp["sgu_ln_b"])[None], (128, DEPTH, 384)))
    sh["wsT"] = f(np.asarray(inp["sgu_w"]).transpose(0, 1, 3, 2))
    sh["sgb"] = f(np.broadcast_to(np.asarray(inp["sgu_b"])[None], (128, DEPTH, 4, 128)))
    sh["p_all"] = f(np.concatenate([np.asarray(inp["p_a"]), np.asarray(inp["p_b"]), np.asarray(inp["p_c"]), np.asarray(inp["p_d"])], 1))
    sh["w_o"] = f(inp["w_o"])
    sh["w1"] = f(inp["mlp_w1"])
    sh["w2"] = f(inp["mlp_w2"])
    sh["relb"] = f(np.broadcast_to(np.asarray(inp["rel_bias"]).reshape(1, 192), (128, 192)))
    sh["dftS"] = dft_table(T, 0, T)
    return sh


def kernel(**inp):
    flags = FLAGS
    sh = prep_shared(inp)
    xp = np.asarray(inp["x_prompt"], np.float32)
    xs = np.asarray(inp["x_sample"], np.float32)
    cp = np.asarray(inp["c_prompt"], np.float32)
    cs = np.asarray(inp["c_sample"], np.float32)
    in_maps = []
    for r in range(NCORES):
        m = dict(sh)
        m.update(host_tables(r))
        m["x"] = np.ascontiguousarray(np.stack([xp[0, r * T:(r + 1) * T], xs[2 * r], xs[2 * r + 1]], 0))
        c3 = np.stack([cp[0], cs[2 * r], cs[2 * r + 1]], 0)
        m["cT"] = np.ascontiguousarray(c3.reshape(3, KD, 128).transpose(2, 1, 0))
        m["dftP"] = dft_table(8 * T, r * T, T)
        in_maps.append(m)
    nc = build(flags)
    res = run_bass_kernel_spmd(nc, in_maps, core_ids=list(range(NCORES)))
    yp = np.zeros((1, 8 * T, D), np.float32)
    ys = np.zeros((16, T, D), np.float32)
    for r in range(NCORES):
        y = np.asarray(res.results[r]["y"], np.float32)
        yp[0, r * T:(r + 1) * T] = y[0]
        ys[2 * r] = y[1]
        ys[2 * r + 1] = y[2]
    return (yp, ys)
```
